# Optimizing a Trainium2 kernel written in Bass

```python
import math
import jax, jax.numpy as jnp
from jax import lax
import numpy as np

D_MODEL = 1024
BATCH = 8
SEQ = 2048
DEPTH = 2

GRID_W = 64
CTX_LEN = 256
HEAD_DIM = 64
ROPE_THETA = 10000.0
Q_BLOCK = 128
EPS = 1e-6

B_WIDTH = D_MODEL // 4
B_GROUPS = 4
B_GROUP_DIM = B_WIDTH // B_GROUPS
POOL_WINDOWS = (2, 4, 8, 16)
A_WIDTH = D_MODEL - B_WIDTH
A_Q_HEADS = A_WIDTH // HEAD_DIM
A_KV_HEADS = A_Q_HEADS // 3
A_GROUP = A_Q_HEADS // A_KV_HEADS
A_KV_WIDTH = A_KV_HEADS * HEAD_DIM
A_IN_WIDTH = A_WIDTH + 2 * A_KV_WIDTH + B_WIDTH
D_CH = D_MODEL // 4
C_QK_DIM = HEAD_DIM
C_V_DIM = 2 * HEAD_DIM
C_WIDTH = D_MODEL - D_CH
C_HEADS = C_WIDTH // C_V_DIM
CONV_WIDTH = 31
C_IN_WIDTH = 3 * C_WIDTH + 2 * D_CH
FFN_HIDDEN = -(-8 * D_MODEL // (3 * 256)) * 256
ALPHA = (2.0 * DEPTH) ** 0.25
BETA = (8.0 * DEPTH) ** -0.25

F32 = jnp.float32

kernel_name = 'hybrid_diffusion_gqa_pool_diffattn_conformer'


def layer_norm(x, g, b):
    xf = x.astype(F32)
    mu = jnp.mean(xf, axis=-1, keepdims=True)
    var = jnp.mean(jnp.square(xf - mu), axis=-1, keepdims=True)
    return ((xf - mu) * lax.rsqrt(var + EPS) * g.astype(F32) + b.astype(F32)).astype(x.dtype)


def rms_norm(x, g):
    xf = x.astype(F32)
    return (xf * lax.rsqrt(jnp.mean(jnp.square(xf), axis=-1, keepdims=True) + EPS) * g.astype(F32)).astype(x.dtype)


def axial_rope_tables(n_tokens):
    rows = n_tokens // GRID_W
    row = jnp.repeat(jnp.arange(rows, dtype=F32), GRID_W)
    col = jnp.tile(jnp.arange(GRID_W, dtype=F32), rows)
    axis_dim = HEAD_DIM // 2
    freqs = ROPE_THETA ** (-jnp.arange(0, axis_dim, 2, dtype=F32) / axis_dim)
    ang = jnp.concatenate([row[:, None] * freqs, col[:, None] * freqs], axis=-1)
    return jnp.cos(ang), jnp.sin(ang)


def apply_axial_rope(x, cos, sin):
    q = HEAD_DIM // 4
    xf = x.astype(F32)

    def rot(xa, c, s):
        x1, x2 = xa[..., :q], xa[..., q:]
        return jnp.concatenate([x1 * c - x2 * s, x2 * c + x1 * s], axis=-1)

    out = jnp.concatenate([rot(xf[..., :2 * q], cos[:, :q], sin[:, :q]),
                           rot(xf[..., 2 * q:], cos[:, q:], sin[:, q:])], axis=-1)
    return out.astype(x.dtype)


def map_query_blocks(fn, q):
    t = q.shape[-2]
    nb = t // Q_BLOCK
    qb = jnp.moveaxis(q.reshape(q.shape[:-2] + (nb, Q_BLOCK, q.shape[-1])), -3, 0)
    out = jnp.moveaxis(lax.map(fn, qb), 0, -3)
    return out.reshape(out.shape[:-3] + (t, out.shape[-1]))


def gqa_attend(q, k, v):
    s = jnp.einsum('bkgqd,bksd->bkgqs', q, k, preferred_element_type=F32) * (HEAD_DIM ** -0.5)
    p = jax.nn.softmax(s, axis=-1)
    return jnp.einsum('bkgqs,bksd->bkgqd', p.astype(v.dtype), v)


def diff_attend(q, k, v, lam):
    s = jnp.einsum('bhiqd,bhisd->bhiqs', q, k, preferred_element_type=F32) * (C_QK_DIM ** -0.5)
    p = jax.nn.softmax(s, axis=-1)
    a = p[:, :, 0] - lam * p[:, :, 1]
    return jnp.einsum('bhqs,bhsd->bhqd', a.astype(v.dtype), v)


def centred_window_mean(x, w):
    t_len = x.shape[1]
    cs = jnp.pad(jnp.cumsum(x.astype(F32), axis=1), ((0, 0), (1, 0), (0, 0)))
    t = jnp.arange(t_len)
    lo = jnp.clip(t - w // 2, 0, t_len)
    hi = jnp.clip(t + (w - 1 - w // 2) + 1, 0, t_len)
    s = jnp.take(cs, hi, axis=1) - jnp.take(cs, lo, axis=1)
    return (s / (hi - lo).astype(F32)[None, :, None]).astype(x.dtype)


def pool_mixer(u, w_pool, pool_scale):
    bsz, n, _ = u.shape
    ug = u.reshape(bsz, n, B_GROUPS, B_GROUP_DIM)
    pooled = jnp.stack([centred_window_mean(ug[:, :, g], w) - ug[:, :, g]
                        for g, w in enumerate(POOL_WINDOWS)], axis=2)
    mixed = jnp.einsum('bngc,gcd->bngd', pooled, w_pool)
    return mixed.reshape(bsz, n, B_WIDTH) * pool_scale


def conformer_conv(u, conv_w, conv_b, ln_g, ln_b):
    a, g = jnp.split(u, 2, axis=-1)
    z = a * jax.nn.sigmoid(g)
    z = lax.conv_general_dilated(z, conv_w[:, None, :].astype(z.dtype), window_strides=(1,),
                                 padding=[(CONV_WIDTH // 2, CONV_WIDTH // 2)],
                                 dimension_numbers=('NWC', 'WIO', 'NWC'),
                                 feature_group_count=D_CH) + conv_b
    return jax.nn.silu(layer_norm(z, ln_g, ln_b))


def swiglu(h, w_in, w_out):
    a, g = jnp.split(h @ w_in, 2, axis=-1)
    return (jax.nn.silu(g) * a) @ w_out


def mixer_ab(h, hc, w_in, q_gain, k_gain, w_pool, pool_scale, w_out, cos, sin, with_ctx_out):
    bsz, n_lat, _ = h.shape
    n_ctx = hc.shape[1]
    n = n_ctx + n_lat
    p = jnp.concatenate([hc, h], axis=1) @ w_in
    q, k, v, u = jnp.split(p, [A_WIDTH, A_WIDTH + A_KV_WIDTH, A_WIDTH + 2 * A_KV_WIDTH], axis=-1)
    q = rms_norm(q.reshape(bsz, n, A_KV_HEADS, A_GROUP, HEAD_DIM), q_gain).transpose(0, 2, 3, 1, 4)
    k = rms_norm(k.reshape(bsz, n, A_KV_HEADS, HEAD_DIM), k_gain).transpose(0, 2, 1, 3)
    v = v.reshape(bsz, n, A_KV_HEADS, HEAD_DIM).transpose(0, 2, 1, 3)
    k_all = jnp.concatenate([k[..., :n_ctx, :], apply_axial_rope(k[..., n_ctx:, :], cos, sin)], axis=-2)
    q_lat = apply_axial_rope(q[..., n_ctx:, :], cos, sin)
    o_lat = map_query_blocks(lambda qb: gqa_attend(qb, k_all, v), q_lat)

    def finish(o, uu):
        o = o.transpose(0, 3, 1, 2, 4).reshape(o.shape[0], o.shape[3], A_WIDTH)
        return jnp.concatenate([o, pool_mixer(uu, w_pool, pool_scale)], axis=-1) @ w_out

    y = finish(o_lat, u[:, n_ctx:])
    yc = None
    if with_ctx_out:
        o_ctx = gqa_attend(q[..., :n_ctx, :], k[..., :n_ctx, :], v[:, :, :n_ctx])
        yc = finish(o_ctx, u[:, :n_ctx])
    return y, yc


def mixer_cd(h, hc, w_in, lq1, lk1, lq2, lk2, subln_gain, conv_w, conv_b, conv_ln_g, conv_ln_b,
             w_out, cos, sin, lam_init, with_ctx_out):
    bsz, n_lat, _ = h.shape
    n_ctx = hc.shape[1]
    n = n_ctx + n_lat
    p = jnp.concatenate([hc, h], axis=1) @ w_in
    q, k, v, u = jnp.split(p, [C_WIDTH, 2 * C_WIDTH, 3 * C_WIDTH], axis=-1)
    q = q.reshape(bsz, n, C_HEADS, 2, C_QK_DIM).transpose(0, 2, 3, 1, 4)
    k = k.reshape(bsz, n, C_HEADS, 2, C_QK_DIM).transpose(0, 2, 3, 1, 4)
    v = v.reshape(bsz, n, C_HEADS, C_V_DIM).transpose(0, 2, 1, 3)
    lam = (jnp.exp(jnp.sum(lq1.astype(F32) * lk1.astype(F32)))
           - jnp.exp(jnp.sum(lq2.astype(F32) * lk2.astype(F32))) + lam_init)
    k_all = jnp.concatenate([k[..., :n_ctx, :], apply_axial_rope(k[..., n_ctx:, :], cos, sin)], axis=-2)
    q_lat = apply_axial_rope(q[..., n_ctx:, :], cos, sin)
    o_lat = map_query_blocks(lambda qb: diff_attend(qb, k_all, v, lam), q_lat)

    def finish(o, uu):
        o = rms_norm(o, subln_gain) * (1.0 - lam_init)
        o = o.transpose(0, 2, 1, 3).reshape(o.shape[0], o.shape[2], C_WIDTH)
        conv = conformer_conv(uu, conv_w, conv_b, conv_ln_g, conv_ln_b)
        return jnp.concatenate([o, conv], axis=-1) @ w_out

    y = finish(o_lat, u[:, n_ctx:])
    yc = None
    if with_ctx_out:
        o_ctx = diff_attend(q[..., :n_ctx, :], k[..., :n_ctx, :], v[:, :, :n_ctx], lam)
        yc = finish(o_ctx, u[:, :n_ctx])
    return y, yc


def setup_inputs(seed: int = 0) -> dict:
    key = jax.random.key(seed)
    ks = iter(jax.random.split(key, 32))
    n_even = (DEPTH + 1) // 2
    n_odd = DEPTH // 2

    def nrm(shape, scale):
        return jax.random.normal(next(ks), shape, F32) * scale

    def gain(shape):
        return 1.0 + nrm(shape, 0.05)

    return {
        'x': nrm((BATCH, SEQ, D_MODEL), 1.0),
        'c': nrm((BATCH, D_MODEL), 1.0),
        'ctx': nrm((BATCH, CTX_LEN, D_MODEL), 1.0),
        'c_ctx': nrm((D_MODEL,), 1.0),
        'ab_w_in': nrm((n_even, D_MODEL, A_IN_WIDTH), D_MODEL ** -0.5),
        'ab_q_gain': gain((n_even, HEAD_DIM)),
        'ab_k_gain': gain((n_even, HEAD_DIM)),
        'ab_w_pool': nrm((n_even, B_GROUPS, B_GROUP_DIM, B_GROUP_DIM), B_GROUP_DIM ** -0.5),
        'ab_pool_scale': gain((n_even, B_WIDTH)),
        'ab_w_out': nrm((n_even, D_MODEL, D_MODEL), BETA * D_MODEL ** -0.5),
        'cd_w_in': nrm((n_odd, D_MODEL, C_IN_WIDTH), D_MODEL ** -0.5),
        'cd_lambda_q1': nrm((n_odd, C_QK_DIM), 0.1),
        'cd_lambda_k1': nrm((n_odd, C_QK_DIM), 0.1),
        'cd_lambda_q2': nrm((n_odd, C_QK_DIM), 0.1),
        'cd_lambda_k2': nrm((n_odd, C_QK_DIM), 0.1),
        'cd_subln_gain': gain((n_odd, C_V_DIM)),
        'cd_conv_w': nrm((n_odd, CONV_WIDTH, D_CH), CONV_WIDTH ** -0.5),
        'cd_conv_b': nrm((n_odd, D_CH), 0.02),
        'cd_conv_ln_g': gain((n_odd, D_CH)),
        'cd_conv_ln_b': nrm((n_odd, D_CH), 0.02),
        'cd_w_out': nrm((n_odd, D_MODEL, D_MODEL), BETA * D_MODEL ** -0.5),
        'ada_w': nrm((DEPTH, D_MODEL, 6 * D_MODEL), 0.5 * D_MODEL ** -0.5),
        'ada_b': nrm((DEPTH, 6 * D_MODEL), 0.02),
        'ln1_g': gain((DEPTH, D_MODEL)),
        'ln1_b': nrm((DEPTH, D_MODEL), 0.02),
        'ln2_g': gain((DEPTH, D_MODEL)),
        'ln2_b': nrm((DEPTH, D_MODEL), 0.02),
        'ffn_w_in': nrm((DEPTH, D_MODEL, 2 * FFN_HIDDEN), D_MODEL ** -0.5),
        'ffn_w_out': nrm((DEPTH, FFN_HIDDEN, D_MODEL), BETA * FFN_HIDDEN ** -0.5),
    }


def reference(x, c, ctx, c_ctx, ab_w_in, ab_q_gain, ab_k_gain, ab_w_pool, ab_pool_scale, ab_w_out,
              cd_w_in, cd_lambda_q1, cd_lambda_k1, cd_lambda_q2, cd_lambda_k2, cd_subln_gain,
              cd_conv_w, cd_conv_b, cd_conv_ln_g, cd_conv_ln_b, cd_w_out,
              ada_w, ada_b, ln1_g, ln1_b, ln2_g, ln2_b, ffn_w_in, ffn_w_out):
    cos, sin = axial_rope_tables(x.shape[1])
    xc = ctx
    for l in range(DEPTH):
        with_ctx_out = l < DEPTH - 1
        mod = (jax.nn.silu(c) @ ada_w[l] + ada_b[l])[:, None, :]
        mod_c = (jax.nn.silu(c_ctx) @ ada_w[l] + ada_b[l])[None, None, :]
        sh1, sc1, g1, sh2, sc2, g2 = jnp.split(mod, 6, axis=-1)
        csh1, csc1, cg1, csh2, csc2, cg2 = jnp.split(mod_c, 6, axis=-1)
        h = x * (1.0 + sc1) + sh1
        hc = xc * (1.0 + csc1) + csh1
        i = l // 2
        if l % 2 == 0:
            y, yc = mixer_ab(h, hc, ab_w_in[i], ab_q_gain[i], ab_k_gain[i], ab_w_pool[i], ab_pool_scale[i],
                             ab_w_out[i], cos, sin, with_ctx_out)
        else:
            lam_init = 0.8 - 0.6 * math.exp(-0.3 * l)
            y, yc = mixer_cd(h, hc, cd_w_in[i], cd_lambda_q1[i], cd_lambda_k1[i], cd_lambda_q2[i], cd_lambda_k2[i],
                             cd_subln_gain[i], cd_conv_w[i], cd_conv_b[i], cd_conv_ln_g[i], cd_conv_ln_b[i],
                             cd_w_out[i], cos, sin, lam_init, with_ctx_out)
        x = layer_norm(ALPHA * x + g1 * y, ln1_g[l], ln1_b[l])
        x = layer_norm(ALPHA * x + g2 * swiglu(x * (1.0 + sc2) + sh2, ffn_w_in[l], ffn_w_out[l]), ln2_g[l], ln2_b[l])
        if with_ctx_out:
            xc = layer_norm(ALPHA * xc + cg1 * yc, ln1_g[l], ln1_b[l])
            xc = layer_norm(ALPHA * xc + cg2 * swiglu(xc * (1.0 + csc2) + csh2, ffn_w_in[l], ffn_w_out[l]),
                            ln2_g[l], ln2_b[l])
    return x
```

```python
import math
from contextlib import ExitStack

import numpy as np
import concourse.bass as bass
import concourse.mybir as mybir
from concourse.bass_utils import run_bass_kernel_spmd

F32 = mybir.dt.float32
BF16 = mybir.dt.bfloat16
AF = mybir.ActivationFunctionType
ALU = mybir.AluOpType
AX = mybir.AxisListType

NCORES = 8
D = 1024
NT = 18
TOK = 2304
ALPHA = 4.0 ** 0.25
EPS = 1e-6
FFN_H = 2816
LAM_INIT1 = 0.8 - 0.6 * math.exp(-0.3)
POOL_W = (2, 4, 8, 16)
NV = 192
C_ID = 0
C_RC = 128
C_RS = C_RC + 16 * 64
C_PE = C_RS + 16 * 64
NCST = C_PE + 32
UW = 2336


class Sched:
    def __init__(self, nc, es):
        self.nc = nc
        self.es = es
        self.engs = ['pe', 'act', 'dve', 'pool', 'sp']
        self.prog = {e: [] for e in self.engs}
        self.cnt = {e: 0 for e in self.engs}
        self.seen = {e: {} for e in self.engs}
        self.lastw = {}
        self.readers = {}
        self.sems = {}
        self.dcum = {}
        self.sbuf_dma = set()
        for e in self.engs:
            self.sems[e] = es.enter_context(nc.semaphore("sem_" + e))

    def _deps(self, reads, writes):
        deps = {}
        raw = {}

        def add(dct, s, v):
            if dct.get(s, 0) < v:
                dct[s] = v
        for k in reads:
            t = self.lastw.get(k)
            if t is not None:
                add(deps, *t)
                add(raw, *t)
        for k in writes:
            t = self.lastw.get(k)
            if t is not None:
                add(deps, *t)
            for s, v in self.readers.get(k, {}).items():
                add(deps, s, v)
        self._raw = raw
        return deps

    def _commit(self, tok, reads, writes):
        s, v = tok
        for k in reads:
            r = self.readers.setdefault(k, {})
            if r.get(s, 0) < v:
                r[s] = v
        for k in writes:
            self.lastw[k] = tok
            self.readers[k] = {}

    def _waits(self, eng, deps):
        w = []
        for s, v in deps.items():
            if s == eng and eng in ('pe', 'sp'):
                continue
            if self.seen[eng].get(s, 0) >= v:
                continue
            self.seen[eng][s] = v
            w.append((s, v))
        return w

    def op(self, eng, fn, reads=(), writes=()):
        deps = self._deps(reads, writes)
        waits = self._waits(eng, deps)
        self.cnt[eng] += 1
        self.prog[eng].append((waits, fn, (eng, 1)))
        self._commit((eng, self.cnt[eng]), reads, writes)

    def dma(self, eng, fn, reads, writes, semkey, sbuf=True):
        if semkey not in self.sems:
            self.sems[semkey] = self.es.enter_context(self.nc.semaphore("d_" + semkey))
            self.dcum[semkey] = 0
        if sbuf:
            self.sbuf_dma.add(semkey)
        deps = self._deps(reads, writes)
        waits = self._waits(eng, deps)
        self.dcum[semkey] += 16
        self.prog[eng].append((waits, fn, (semkey, 16)))
        self._commit((semkey, self.dcum[semkey]), reads, writes)

    def barrier(self, engs=('pe', 'act', 'dve', 'sp', 'pool')):
        toks = {e: self.cnt[e] for e in ('pe', 'act', 'dve', 'pool') if self.cnt[e] > 0}
        for k in self.sbuf_dma:
            toks[k] = self.dcum[k]
        self._raw = dict(toks)
        for e in engs:
            w = self._waits(e, toks)
            if w:
                self.prog[e].append((w, None, None))

    def wait_all(self, eng, semkeys):
        toks = {k: self.dcum[k] for k in semkeys if k in self.dcum}
        self._raw = dict(toks)
        w = self._waits(eng, toks)
        if w:
            self.prog[eng].append((w, None, None))

    def emit(self, block):
        emap = {'pe': block.tensor, 'act': block.scalar, 'dve': block.vector,
                'pool': block.gpsimd, 'sp': block.sync}
        for e in self.engs:
            prog = self.prog[e]

            def body(eng, prog=prog, ename=e):
                for waits, fn, inc in prog:
                    attach = None
                    if fn is not None and ename in ('act', 'dve') and waits:
                        attach = waits[-1]
                        waits = waits[:-1]
                    for s, v in waits:
                        eng.wait_ge(self.sems[s], v)
                    if fn is not None:
                        ins = fn(eng)
                        if attach is not None:
                            ins._wait_ge(self.sems[attach[0]], attach[1])
                        ins.then_inc(self.sems[inc[0]], inc[1])
            emap[e](body)


def build_program(debug=False, stop=None):
    nc = bass.Bass("TRN2", target_bir_lowering=False, dynamic_dma_scratch_size=4096)
    es = ExitStack()

    def din(name, shape, dt=F32):
        return nc.dram_tensor(name, list(shape), dt, kind="ExternalInput").ap()

    x_d = din("x", [2048, D])
    ctx_d = din("ctx", [256, D])
    vecs_d = din("vecs", [128, NV])
    cst_d = din("consts", [128, NCST])
    lnbc_d = din("lnbc", [8, D])
    gains_d = din("gains", [2, 64])
    wq_d = [din("ab_w_in", [D, 1536]), din("cd_w_in", [D, 2816])]
    wo_d = [din("ab_w_out", [D, D]), din("cd_w_out", [D, D])]
    wpool_d = din("ab_w_pool", [256, 64])
    ada_d = din("ada_w", [2 * D, 6 * D])
    fi_d = din("ffn_w_in", [2 * D, 2 * FFN_H])
    fo_d = din("ffn_w_out", [2 * FFN_H, D])
    out_d = nc.dram_tensor("out", [2048, D], F32, kind="ExternalOutput").ap()
    dbg_d = None
    if debug:
        dbg_d = nc.dram_tensor("dbg", [TOK, D], F32, kind="ExternalOutput").ap()

    def dscr(name, shape):
        if debug and name in ("qkT0_s", "v0_s", "qkT1_s", "v1_s"):
            return nc.dram_tensor(name, list(shape), BF16, kind="ExternalOutput").ap()
        return nc.dram_tensor(name, list(shape), BF16).ap()

    if debug:
        dbg_hb = nc.dram_tensor("dbg_hb", [128, 18432], BF16, kind="ExternalOutput").ap()
        dbg_mod = nc.dram_tensor("dbg_mod", [128, 192], F32, kind="ExternalOutput").ap()
        dbg_r3 = nc.dram_tensor("dbg_r3", [128, 4672], F32, kind="ExternalOutput").ap()
        dbg_att = nc.dram_tensor("dbg_att", [128, 1024], F32, kind="ExternalOutput").ap()
        dbg_stg = nc.dram_tensor("dbg_stg", [128, 2048], F32, kind="ExternalOutput").ap()
        dbg_pt = nc.dram_tensor("dbg_pt", [128, 2048], BF16, kind="ExternalOutput").ap()

    wq_s = [dscr("wq0_s", [D, 1536]), dscr("wq1_s", [D, 2816])]
    wo_s = [dscr("wo0_s", [D, D]), dscr("wo1_s", [D, D])]
    ada_s = [dscr("ada0_s", [D, 6 * D]), dscr("ada1_s", [D, 6 * D])]
    fi_s = [dscr("fi0_s", [D, 2 * FFN_H]), dscr("fi1_s", [D, 2 * FFN_H])]
    fo_s = [dscr("fo0_s", [FFN_H, D]), dscr("fo1_s", [FFN_H, D])]
    NQK = [1024, 1536]
    VW = [256, 768]
    qkT_s = [dscr("qkT0_s", [NQK[0], TOK]), dscr("qkT1_s", [NQK[1], TOK])]
    v_s = [dscr("v0_s", [TOK, VW[0]]), dscr("v1_s", [TOK, VW[1]])]

    def sb(name, shape, dt):
        return es.enter_context(nc.sbuf_tensor(name, list(shape), dt))

    X = sb("X", [128, NT, D], F32)
    HB = sb("HB", [128, 30720], BF16)
    RA = sb("RA", [128, 16384], BF16)
    R3 = sb("R3", [128, 4672], F32)
    CST = sb("CST", [128, NCST], F32)
    VEC = sb("VEC", [128, NV], F32)
    MOD = sb("MOD", [128, 2, 48, 2], F32)
    SILC = sb("SILC", [128, 8, 2], BF16)
    IDB = sb("IDB", [128, 128], BF16)
    ONB = sb("ONB", [128, 128], BF16)
    ONF = sb("ONF", [128, 128], F32)
    STG = sb("STG", [128, 2048], F32)
    TT = sb("TT", [128, 2, 512], BF16)
    VST = sb("VST", [128, 2, 512], BF16)
    PT = sb("PT", [128, 4, 512], BF16)
    SG = sb("SG", [128, 2, 512], F32)
    GBC = sb("GBC", [128, 2, 64], F32)
    BD = sb("BD", [128, 2, 128], BF16)
    DG = sb("DG", [128, 2, 128], F32)
    SM = sb("SM", [128, 256], F32)
    LAM = sb("LAM", [128, 8], F32)

    PSB = [es.enter_context(nc.psum_tensor(f"psb{i}", [128, 1024], F32)) for i in range(4)]

    def PS(i):
        return PSB[i // 2][:, (i % 2) * 512:(i % 2) * 512 + 512]

    ident = CST[:, C_ID:C_ID + 128]
    ropeC = CST[:, C_RC:C_RC + 1024].rearrange("p (t d) -> p t d", d=64)
    ropeS = CST[:, C_RS:C_RS + 1024].rearrange("p (t d) -> p t d", d=64)
    pedge = CST[:, C_PE:C_PE + 32].rearrange("p (c s i) -> p c s i", c=2, s=2)

    hT_all = HB[:, 0:8 * TOK].rearrange("p (k t) -> p k t", k=8)
    catT = hT_all
    WSLAB = [HB[:, 18432 + s * 4096: 18432 + (s + 1) * 4096].rearrange("p (k c) -> p k c", k=8) for s in range(2)]
    WOUT = HB[:, 18432:18432 + 8192].rearrange("p (k c) -> p k c", k=8)
    WO = HB[:, 0:22528].rearrange("p (j c) -> p j c", j=22)
    FRING = [(HB[:, 22528 + s * 4096: 22528 + s * 4096 + 2048].rearrange("p (k c) -> p k c", k=8),
              HB[:, 22528 + s * 4096 + 2048: 22528 + (s + 1) * 4096].rearrange("p (k c) -> p k c", k=8))
             for s in range(2)]
    hidT = RA[:, 0:11264].rearrange("p (j t) -> p j t", j=22)
    h2T = RA[:, 11264:15360].rearrange("p (k t) -> p k t", k=8)
    BCT = [R3[:, i * 1024:(i + 1) * 1024] for i in range(4)]
    uT = R3[:, 0:2 * UW].rearrange("p (c w) -> p c w", c=2)
    aT = R3[:, 0:4096].rearrange("p (c t) -> p c t", c=2)
    RAF = RA[:, :].bitcast(F32)

    S = Sched(nc, es)
    rot = {}

    def nxt(name, n):
        v = rot.get(name, 0)
        rot[name] = (v + 1) % n
        return v

    SCALE = 0.125

    def cast2d(dst, src, key):
        r, c = src.shape
        if c > 2048:
            if c % 2048 == 0:
                d2 = dst.rearrange("a (b c) -> a b c", c=2048)
                s2 = src.rearrange("a (b c) -> a b c", c=2048)
            else:
                d2 = dst.rearrange("a b -> (a b)").rearrange("(n c) -> n c", c=2048)
                s2 = src.rearrange("a b -> (a b)").rearrange("(n c) -> n c", c=2048)
        else:
            d2, s2 = dst, src
        S.dma('pool', lambda e: e.dma_start(out=d2, in_=s2), [], [key], semkey="c_" + key, sbuf=False)

    cast2d(ada_s[0][:, 0:2048], ada_d[0:D, 0:2048], "ada0a")
    cast2d(wq_s[0], wq_d[0], "wq0")
    cast2d(ada_s[0][:, 2048:6144], ada_d[0:D, 2048:6144], "ada0b")

    S.dma('sp', lambda e: e.dma_start(out=CST[:, :], in_=cst_d), [], ['cst'], 'l_cst')
    S.dma('sp', lambda e: e.dma_start(out=VEC[:, :], in_=vecs_d), [], ['vec'], 'l_vec')
    S.dma('sp', lambda e: e.dma_start(out=GBC[:, :, :], in_=gains_d.partition_broadcast(128)), [], ['gbc'], 'l_gbc')
    S.dma('sp', lambda e: e.dma_start(out=X[:, 0:2, :], in_=ctx_d.rearrange("(t p) d -> p t d", p=128)),
          [], [('x', 0), ('x', 1)], 'l_ctx')
    for i in range(4):
        S.dma('sp', lambda e, i=i: e.dma_start(out=X[:, 2 + 4 * i:6 + 4 * i, :],
                                               in_=x_d[i * 512:(i + 1) * 512, :].rearrange("(t p) d -> p t d", p=128)),
              [], [('x', 2 + 4 * i + j) for j in range(4)], f'l_x{i}')

    S.op('dve', lambda e: e.memset(BD[:, :, :], 0.0), [], ['bd'])
    for c in range(2):
        for hh in range(2):
            g = 2 * c + hh
            S.dma('pool', lambda e, c=c, hh=hh, g=g: e.dma_start(out=BD[hh * 64:(hh + 1) * 64, c, hh * 64:(hh + 1) * 64],
                                                                 in_=wpool_d[g * 64:(g + 1) * 64, :]),
                  [], ['bd'], 'l_bd')
    cast2d(wo_s[0], wo_d[0], "wo0")
    cast2d(fi_s[0], fi_d[0:D, :], "fi0")
    cast2d(fo_s[0], fo_d[0:FFN_H, :], "fo0")
    cast2d(ada_s[1], ada_d[D:2 * D, :], "ada1")
    cast2d(wq_s[1], wq_d[1], "wq1")
    cast2d(wo_s[1], wo_d[1], "wo1")
    cast2d(fi_s[1], fi_d[D:2 * D, :], "fi1")
    cast2d(fo_s[1], fo_d[FFN_H:2 * FFN_H, :], "fo1")

    S.op('dve', lambda e: e.memset(ONB[:, :], 1.0), [], ['onb'])
    S.op('dve', lambda e: e.memset(ONF[:, :], 1.0), [], ['onf'])
    S.op('dve', lambda e: e.tensor_copy(out=IDB[:, :], in_=ident), ['cst'], ['idb'])
    S.op('act', lambda e: e.activation(out=SILC[:, :, :].rearrange("p k w -> p w k"),
                                       in_=VEC[:, 0:16].rearrange("p (w k) -> p w k", w=2), func=AF.Silu),
         ['vec'], ['silc'])

    def compute_mod(l):
        keys_a = ['ada0a', 'ada0b'] if l == 0 else ['ada1']
        for s_ in range(12):
            slot = nxt('wslab', 2)
            S.dma('sp', lambda e, s_=s_, slot=slot: e.dma_start(
                out=WSLAB[slot][:, :, :], in_=ada_s[l][:, s_ * 512:(s_ + 1) * 512].rearrange("(k p) c -> p k c", p=128)),
                keys_a, [('wslab', slot)], f'wslab{slot}')
            for o4 in range(4):
                oc = s_ * 4 + o4
                for kc in range(8):
                    S.op('pe', lambda e, slot=slot, o4=o4, kc=kc, oc=oc: e.matmul(
                        PS(0)[:, oc * 2:oc * 2 + 2], lhsT=WSLAB[slot][:, kc, o4 * 128:(o4 + 1) * 128],
                        rhs=SILC[:, kc, :], start=(kc == 0), stop=(kc == 7)),
                        [('wslab', slot), 'silc'], [('ps', 0)])
        mps = PS(0)[:, 0:96].rearrange("p (o w) -> p o w", w=2)
        for w in range(2):
            S.op('dve', lambda e, w=w: e.tensor_tensor(out=MOD[:, l, :, w], in0=mps[:, :, w],
                                                       in1=VEC[:, 16 + 48 * l:64 + 48 * l], op=ALU.add),
                 [('ps', 0), 'vec'], [('mod', l)])
        for s0 in (8, 32):
            S.op('dve', lambda e, s0=s0: e.tensor_scalar(out=MOD[:, l, s0:s0 + 8, :], in0=MOD[:, l, s0:s0 + 8, :],
                                                         scalar1=1.0, scalar2=None, op0=ALU.add),
                 [('mod', l)], [('mod', l)])

    def modv(l, s_, kc, w):
        return MOD[:, l, s_ * 8 + kc, w:w + 1]

    def build_gate_bc(dst, l, s_, w, key):
        for half in range(2):
            bank = nxt('bcbank', 2)
            for k4 in range(4):
                kc = half * 4 + k4
                dslot = nxt('dg', 2)
                S.op('dve', lambda e, kc=kc, dslot=dslot: e.tensor_scalar(
                    out=DG[:, dslot, :], in0=ident, scalar1=modv(l, s_, kc, w), scalar2=None, op0=ALU.mult),
                    ['cst', ('mod', l)], [('dg', dslot)])
                S.op('pe', lambda e, k4=k4, dslot=dslot, bank=bank: e.matmul(
                    PS(bank)[:, k4 * 128:(k4 + 1) * 128], lhsT=ONF[:, :], rhs=DG[:, dslot, :], start=True, stop=True),
                    ['onf', ('dg', dslot)], [('ps', bank)])
            S.op('act', lambda e, half=half, bank=bank: e.activation(
                out=dst[:, half * 512:(half + 1) * 512], in_=PS(bank), func=AF.Copy),
                [('ps', bank)], [key])

    def load_ln_bc(dst, row, key):
        S.dma('sp', lambda e: e.dma_start(out=dst.unsqueeze(1), in_=lnbc_d[row:row + 1, :].partition_broadcast(128)),
              [], [key], 'l_' + str(key[1]))

    BLOCKS = [([0, 1], 1)] + [([2 + 4 * i + j for j in range(4)], 0) for i in range(4)]

    def phase_p0(l):
        for tiles, w in BLOCKS:
            n = 128 * len(tiles)
            for kc in range(8):
                bank = nxt('p0bank', 4)
                for i, t in enumerate(tiles):
                    S.op('pe', lambda e, i=i, t=t, kc=kc, bank=bank: e.transpose(
                        out=PS(bank)[:, i * 128:(i + 1) * 128], in_=X[:, t, kc * 128:(kc + 1) * 128], identity=ident),
                        [('x', t), 'cst'], [('ps', bank)])
                S.op('act', lambda e, kc=kc, bank=bank, n=n, t0=tiles[0], w=w: e.activation(
                    out=hT_all[:, kc, t0 * 128:t0 * 128 + n], in_=PS(bank)[:, 0:n], func=AF.Identity,
                    bias=modv(l, 0, kc, w), scale=modv(l, 1, kc, w)),
                    [('ps', bank), ('mod', l)], [('hT', tiles[0], kc)])

    A_, B_, C_, D_ = (STG[:, 0:512], STG[:, 512:1024], STG[:, 1024:1536], STG[:, 1536:2048])

    def hv(ap, n):
        return ap.rearrange("p (h d) -> p h d", d=64)

    def post_qk(l, t, bank, cs, n, row0, gain):
        src = PS(bank)[:, cs:cs + n]
        NH = n // 64
        pk = ('ps', bank)
        if l == 0:
            S.op('act', lambda e: e.activation(out=C_[:, 0:n], in_=src, func=AF.Square), [pk], ['stgC'])
            S.op('dve', lambda e: e.tensor_reduce(out=SM[:, 0:NH], in_=hv(C_[:, 0:n], n), axis=AX.X, op=ALU.add),
                 ['stgC'], ['sm_ss'])
            S.op('dve', lambda e: e.tensor_scalar(out=SM[:, 0:NH], in0=SM[:, 0:NH], scalar1=1.0 / 64, scalar2=EPS,
                                                  op0=ALU.mult, op1=ALU.add), ['sm_ss'], ['sm_ss'])
            S.op('act', lambda e: e.activation(out=SM[:, 16:16 + NH], in_=SM[:, 0:NH], func=AF.Ln), ['sm_ss'], ['sm_ln'])
            S.op('act', lambda e: e.activation(out=SM[:, 32:32 + NH], in_=SM[:, 16:16 + NH], func=AF.Exp, scale=-0.5),
                 ['sm_ln'], ['sm_rs'])
            S.op('dve', lambda e: e.tensor_tensor(out=hv(A_[:, 0:n], n), in0=hv(src, n),
                                                  in1=SM[:, 32:32 + NH].unsqueeze(2).to_broadcast([128, NH, 64]), op=ALU.mult),
                 [pk, 'sm_rs'], ['stgA'])
            S.op('dve', lambda e: e.tensor_tensor(out=hv(A_[:, 0:n], n), in0=hv(A_[:, 0:n], n),
                                                  in1=GBC[:, gain, :].unsqueeze(1).to_broadcast([128, NH, 64]), op=ALU.mult),
                 ['stgA', 'gbc'], ['stgA'])
            xin, xk = A_[:, 0:n], 'stgA'
        else:
            xin, xk = src, pk
        if t < 2:
            if l == 0:
                res, rk = A_[:, 0:n], 'stgA'
            else:
                S.op('act', lambda e: e.activation(out=B_[:, 0:n], in_=src, func=AF.Copy), [pk], ['stgB'])
                res, rk = B_[:, 0:n], 'stgB'
        else:
            tt = t - 2
            S.op('dve', lambda e: e.tensor_tensor(out=hv(B_[:, 0:n], n), in0=hv(xin, n),
                                                  in1=ropeC[:, tt, :].unsqueeze(1).to_broadcast([128, NH, 64]), op=ALU.mult),
                 [xk, 'cst'], ['stgB'])

            def v4(ap):
                return ap.rearrange("p (h a two s) -> p h a two s", a=2, two=2, s=16)
            sv = ropeS[:, tt, :].rearrange("p (a two s) -> p a two s", a=2, two=2)
            for two in range(2):
                S.op('dve', lambda e, two=two: e.tensor_tensor(
                    out=v4(C_[:, 0:n])[:, :, :, two, :], in0=v4(xin)[:, :, :, 1 - two, :],
                    in1=sv[:, :, two, :].unsqueeze(1).to_broadcast([128, NH, 2, 16]), op=ALU.mult),
                    [xk, 'cst'], ['stgC'])
            S.op('dve', lambda e: e.tensor_tensor(out=B_[:, 0:n], in0=B_[:, 0:n], in1=C_[:, 0:n], op=ALU.add),
                 ['stgB', 'stgC'], ['stgB'])
            res, rk = B_[:, 0:n], 'stgB'
        tb = 3 + nxt('p1tb', 2)
        for i in range(n // 128):
            S.op('pe', lambda e, i=i, tb=tb: e.transpose(out=PS(tb)[:, i * 128:(i + 1) * 128],
                                                         in_=res[:, i * 128:(i + 1) * 128], identity=ident),
                 [rk, 'cst'], [('ps', tb)])
        ts = nxt('tt', 2)
        S.op('act', lambda e, tb=tb, ts=ts: e.activation(out=TT[:, ts, 0:n], in_=PS(tb)[:, 0:n], func=AF.Copy),
             [('ps', tb)], [('tt', ts)])
        S.dma('sp', lambda e, ts=ts: e.dma_start(
            out=qkT_s[l][row0:row0 + n, t * 128:(t + 1) * 128].rearrange("(i p) c -> p i c", p=128),
            in_=TT[:, ts, 0:n].rearrange("p (i c) -> p i c", c=128)),
            [('tt', ts)], [('qkT', l, t, row0)], f'ttw{ts}', sbuf=False)

    def post_v(l, t, bank, cs, n, vc0):
        vs = nxt('vst', 2)
        S.op('act', lambda e: e.activation(out=VST[:, vs, 0:n], in_=PS(bank)[:, cs:cs + n], func=AF.Copy),
             [('ps', bank)], [('vst', vs)])
        S.dma('sp', lambda e: e.dma_start(out=v_s[l][t * 128:(t + 1) * 128, vc0:vc0 + n], in_=VST[:, vs, 0:n]),
              [('vst', vs)], [('vs', l, t, vc0)], f'vstw{vs}', sbuf=False)

    def post_u_T(t, bank, cs):
        S.op('act', lambda e: e.activation(out=D_[:, 0:256], in_=PS(bank)[:, cs:cs + 256], func=AF.Copy),
             [('ps', bank)], ['stgD'])
        tb = 3 + nxt('p1tb', 2)
        for c in range(2):
            S.op('pe', lambda e, c=c, tb=tb: e.transpose(out=PS(tb)[:, c * 128:(c + 1) * 128],
                                                         in_=D_[:, c * 128:(c + 1) * 128], identity=ident),
                 ['stgD', 'cst'], [('ps', tb)])
        return tb

    def uoff(t):
        return 8 + t * 128 if t < 2 else 280 + (t - 2) * 128

    def post_u0(t, bank, cs):
        tb = post_u_T(t, bank, cs)
        S.op('act', lambda e: e.activation(out=uT[:, :, uoff(t):uoff(t) + 128],
                                           in_=PS(tb)[:, 0:256].rearrange("p (c t) -> p c t", c=2), func=AF.Copy),
             [('ps', tb)], ['uT'])

    def post_ua(t, bank, cs):
        tb = post_u_T(t, bank, cs)
        S.op('act', lambda e: e.activation(out=aT[:, :, (t - 2) * 128:(t - 1) * 128],
                                           in_=PS(tb)[:, 0:256].rearrange("p (c t) -> p c t", c=2), func=AF.Copy),
             [('ps', tb)], [('aT', t)])

    zbf = RA[:, 0:4160].rearrange("p (c w) -> p c w", c=2)

    def post_ug(t, bank, cs):
        tb = post_u_T(t, bank, cs)
        S.op('act', lambda e: e.activation(out=SG[:, 0, 0:256], in_=PS(tb)[:, 0:256], func=AF.Sigmoid),
             [('ps', tb)], ['sg0'])
        o = 15 + (t - 2) * 128
        S.op('dve', lambda e: e.tensor_tensor(out=zbf[:, :, o:o + 128], in0=aT[:, :, (t - 2) * 128:(t - 1) * 128],
                                              in1=SG[:, 0, 0:256].rearrange("p (c t) -> p c t", c=2), op=ALU.mult),
             [('aT', t), 'sg0'], ['zbf'])

    def segs_for(l, si):
        if l == 0:
            return [[('qk', 0, 512, (0, 0))],
                    [('qk', 0, 256, (512, 0)), ('qk', 256, 256, (768, 1))],
                    [('v', 0, 256, 0), ('u0', 256, 256, None)]][si]
        return [[('qk', 0, 512, (0, None))],
                [('qk', 0, 256, (512, None)), ('qk', 256, 256, (768, None))],
                [('qk', 0, 512, (1024, None))],
                [('v', 0, 512, 0)],
                [('v', 0, 256, 512), ('ua', 256, 256, None)],
                [('ug', 0, 256, None)]][si]

    def phase_p1(l):
        ncols = 1536 if l == 0 else 2816
        slabs = [(c0, min(512, ncols - c0)) for c0 in range(0, ncols, 512)]
        if l == 1:
            S.op('dve', lambda e: e.memset(zbf[:, :, :], 0.0), [], ['zbf'])
        for si, (c0, ncs) in enumerate(slabs):
            slot = nxt('wslab', 2)
            S.dma('sp', lambda e, c0=c0, ncs=ncs, slot=slot: e.dma_start(
                out=WSLAB[slot][:, :, 0:ncs], in_=wq_s[l][:, c0:c0 + ncs].rearrange("(k p) c -> p k c", p=128)),
                [f'wq{l}'], [('wslab', slot)], f'wslab{slot}')
            segs = segs_for(l, si)
            for t in range(NT):
                if t < 2 and all(k in ('ua', 'ug') for k, _, _, _ in segs):
                    continue
                bank = nxt('p1bank', 3)
                t0 = 0 if t < 2 else 2 + 4 * ((t - 2) // 4)
                for kc in range(8):
                    S.op('pe', lambda e, kc=kc, t=t, bank=bank, slot=slot, ncs=ncs: e.matmul(
                        PS(bank)[:, 0:ncs], lhsT=hT_all[:, kc, t * 128:(t + 1) * 128], rhs=WSLAB[slot][:, kc, 0:ncs],
                        start=(kc == 0), stop=(kc == 7)),
                        [('hT', t0, kc), ('wslab', slot)], [('ps', bank)])
                for kind, cs, n, ex in segs:
                    if kind == 'qk':
                        post_qk(l, t, bank, cs, n, ex[0], ex[1])
                    elif kind == 'v':
                        post_v(l, t, bank, cs, n, ex)
                    elif kind == 'u0':
                        post_u0(t, bank, cs)
                    elif kind == 'ua' and t >= 2:
                        post_ua(t, bank, cs)
                    elif kind == 'ug' and t >= 2:
                        post_ug(t, bank, cs)

    def phase_pool():
        T1 = [RAF[:, 0:UW], RAF[:, UW:2 * UW]]
        pooled = RA[:, 9472:9472 + 2 * TOK].rearrange("p (c t) -> p c t", c=2)
        segsP = [(8, 0, 256), (280, 256, 2048)]

        def emit_group(c, hh, w, Aw):
            P0_, P1_ = hh * 64, hh * 64 + 64
            hw = w // 2
            for (ps_, ts_, L) in segsP:
                S.op('dve', lambda e, ps_=ps_, ts_=ts_, L=L: e.scalar_tensor_tensor(
                    out=pooled[P0_:P1_, c, ts_:ts_ + L], in0=Aw[P0_:P1_, ps_ - hw:ps_ - hw + L], scalar=1.0 / w,
                    in1=uT[P0_:P1_, c, ps_:ps_ + L], op0=ALU.mult, op1=ALU.subtract),
                    ['uT', 'poolA'], ['pooled'])
                for side in range(2):
                    po = ps_ if side == 0 else ps_ + L - 8
                    to = ts_ if side == 0 else ts_ + L - 8
                    S.op('dve', lambda e, po=po, side=side: e.tensor_tensor(
                        out=SM[P0_:P1_, 64:72], in0=Aw[P0_:P1_, po - hw:po - hw + 8], in1=pedge[P0_:P1_, c, side, :], op=ALU.mult),
                        ['poolA', 'cst'], ['sm_pe'])
                    S.op('dve', lambda e, po=po, to=to: e.tensor_tensor(
                        out=pooled[P0_:P1_, c, to:to + 8], in0=SM[P0_:P1_, 64:72], in1=uT[P0_:P1_, c, po:po + 8], op=ALU.subtract),
                        ['sm_pe', 'uT'], ['pooled'])

        for c in range(2):
            src = uT[:, c, :]
            cur = 0
            S.op('dve', lambda e, src=src: e.tensor_tensor(out=T1[0][:, 0:UW - 1], in0=src[:, 0:UW - 1], in1=src[:, 1:UW], op=ALU.add),
                 ['uT', 'pooled'], ['poolA'])
            if c == 0:
                emit_group(0, 0, 2, T1[0])
            S.op('dve', lambda e: e.tensor_tensor(out=T1[1][:, 0:UW - 3], in0=T1[0][:, 0:UW - 3], in1=T1[0][:, 2:UW - 1], op=ALU.add),
                 ['poolA'], ['poolA'])
            if c == 0:
                emit_group(0, 1, 4, T1[1])
            else:
                S.op('dve', lambda e: e.tensor_tensor(out=T1[0][:, 0:UW - 7], in0=T1[1][:, 0:UW - 7], in1=T1[1][:, 4:UW - 3], op=ALU.add),
                     ['poolA'], ['poolA'])
                emit_group(1, 0, 8, T1[0])
                S.op('dve', lambda e: e.tensor_tensor(out=T1[1][:, 0:UW - 15], in0=T1[0][:, 0:UW - 15], in1=T1[0][:, 8:UW - 7], op=ALU.add),
                     ['poolA'], ['poolA'])
                emit_group(1, 1, 16, T1[1])
        for c in range(2):
            for t0 in range(0, TOK, 512):
                n = min(512, TOK - t0)
                bank = nxt('p2bank', 2)
                S.op('pe', lambda e, c=c, t0=t0, n=n, bank=bank: e.matmul(
                    PS(bank)[:, 0:n], lhsT=BD[:, c, :], rhs=pooled[:, c, t0:t0 + n], start=True, stop=True),
                    ['bd', 'pooled'], [('ps', bank)])
                S.op('act', lambda e, c=c, t0=t0, n=n, bank=bank: e.activation(
                    out=catT[:, 6 + c, t0:t0 + n], in_=PS(bank)[:, 0:n], func=AF.Identity, scale=VEC[:, 112 + c:113 + c]),
                    [('ps', bank), 'vec'], [('cat', 6 + c)])

    def phase_conv():
        DIAG = RA[:, 4160:4160 + 62 * 128].rearrange("p (j m) -> p j m", m=128)
        sts = RAF[:, 6048:6048 + 2048]
        MEAN, VAR, TMP = sts[:, 0:512], sts[:, 512:1024], sts[:, 1024:1536]
        Y = STG[:, 0:1024].rearrange("p (c t) -> p c t", c=2)
        YSQ = STG[:, 1024:2048].rearrange("p (c t) -> p c t", c=2)
        for j in range(31):
            for c in range(2):
                S.op('dve', lambda e, j=j, c=c: e.tensor_scalar(
                    out=DIAG[:, 2 * j + c, :], in0=IDB[:, :], scalar1=VEC[:, 125 + 2 * j + c:126 + 2 * j + c],
                    scalar2=None, op0=ALU.mult), ['idb', 'vec'], ['diag'])
        for b in range(4):
            for c in range(2):
                bank = c
                for j in range(31):
                    S.op('pe', lambda e, j=j, c=c, b=b, bank=bank: e.matmul(
                        PS(bank), lhsT=DIAG[:, 2 * j + c, :], rhs=zbf[:, c, b * 512 + j:b * 512 + j + 512],
                        start=(j == 0), stop=(j == 30)), ['diag', 'zbf'], [('ps', bank)])
                S.op('act', lambda e, c=c, bank=bank: e.activation(
                    out=Y[:, c, :], in_=PS(bank), func=AF.Identity, bias=VEC[:, 114 + c:115 + c]),
                    [('ps', bank), 'vec'], [('cy', c)])
                S.op('dve', lambda e, c=c: e.tensor_tensor(out=YSQ[:, c, :], in0=Y[:, c, :], in1=Y[:, c, :], op=ALU.mult),
                     [('cy', c)], [('cysq', c)])
            for c in range(2):
                S.op('pe', lambda e, c=c: e.matmul(PS(2), lhsT=ONF[:, :], rhs=Y[:, c, :], start=(c == 0), stop=(c == 1)),
                     ['onf', ('cy', c)], [('ps', 2)])
            for c in range(2):
                S.op('pe', lambda e, c=c: e.matmul(PS(3), lhsT=ONF[:, :], rhs=YSQ[:, c, :], start=(c == 0), stop=(c == 1)),
                     ['onf', ('cysq', c)], [('ps', 3)])
            S.op('dve', lambda e: e.tensor_scalar(out=MEAN, in0=PS(2), scalar1=1.0 / 256, scalar2=None, op0=ALU.mult),
                 [('ps', 2)], ['cmean'])
            S.op('dve', lambda e: e.tensor_tensor(out=TMP, in0=MEAN, in1=MEAN, op=ALU.mult), ['cmean'], ['ctmp'])
            S.op('dve', lambda e: e.scalar_tensor_tensor(out=VAR, in0=PS(3), scalar=1.0 / 256, in1=TMP,
                                                         op0=ALU.mult, op1=ALU.subtract), [('ps', 3), 'ctmp'], ['cvar'])
            S.op('dve', lambda e: e.tensor_scalar(out=VAR, in0=VAR, scalar1=EPS, scalar2=None, op0=ALU.add), ['cvar'], ['cvar'])
            S.op('act', lambda e: e.activation(out=TMP, in_=VAR, func=AF.Ln), ['cvar'], ['ctmp'])
            S.op('act', lambda e: e.activation(out=VAR, in_=TMP, func=AF.Exp, scale=-0.5), ['ctmp'], ['cvar'])
            for c in range(2):
                S.op('dve', lambda e, c=c: e.tensor_tensor(out=Y[:, c, :], in0=Y[:, c, :], in1=MEAN, op=ALU.subtract),
                     [('cy', c), 'cmean'], [('cy', c)])
                S.op('dve', lambda e, c=c: e.tensor_tensor(out=Y[:, c, :], in0=Y[:, c, :], in1=VAR, op=ALU.mult),
                     [('cy', c), 'cvar'], [('cy', c)])
                S.op('act', lambda e, c=c, b=b: e.activation(
                    out=catT[:, 6 + c, 256 + b * 512:256 + (b + 1) * 512], in_=Y[:, c, :], func=AF.Silu,
                    bias=VEC[:, 118 + c:119 + c], scale=VEC[:, 116 + c:117 + c]),
                    [('cy', c), 'vec'], [('cat', 6 + c)])

    def att_views(l, slot):
        base = slot * 8192
        Q = RA[:, base:base + TOK]
        K = RA[:, base + TOK:base + 2 * TOK]
        V = RA[:, base + 4608:base + 4608 + 18 * 128].rearrange("p (c d) -> p c d", d=128)
        return Q, K, V, V

    def phase_att(l):
        if l == 0:
            pass
        else:
            S.op('dve', lambda e: e.tensor_tensor(out=LAM[:, 0:2].unsqueeze(2), in0=VEC[:, 121:125].rearrange("p (a b) -> p a b", b=2)[:, :, 0:1],
                                                  in1=VEC[:, 121:125].rearrange("p (a b) -> p a b", b=2)[:, :, 1:2], op=ALU.mult),
                 ['vec'], ['lam'])
            S.op('pe', lambda e: e.matmul(PS(0)[:, 0:2], lhsT=ONF[:, :], rhs=LAM[:, 0:2], start=True, stop=True),
                 ['onf', 'lam'], [('ps', 0)])
            S.op('act', lambda e: e.activation(out=LAM[:, 2:4], in_=PS(0)[:, 0:2], func=AF.Exp), [('ps', 0)], ['lam'])
            S.op('dve', lambda e: e.scalar_tensor_tensor(out=LAM[:, 4:5], in0=LAM[:, 3:4], scalar=-LAM_INIT1, in1=LAM[:, 2:3],
                                                         op0=ALU.add, op1=ALU.subtract), ['lam'], ['lam'])
            S.op('dve', lambda e: e.tensor_scalar(out=LAM[:, 5:6], in0=VEC[:, 120:121], scalar1=1.0 - LAM_INIT1, scalar2=None,
                                                  op0=ALU.mult), ['vec'], ['lam'])
        nunits = 6

        def load_unit(u):
            slot = u % 2
            Q, K, vA, vB = att_views(l, slot)
            rk = [('qkT', l, t, r) for t in range(NT) for r in ()]
            wk = [('att', slot)]
            deps = [k for k in S.lastw.keys() if isinstance(k, tuple) and k[0] in ('qkT', 'vs') and k[1] == l]
            sk = f'att{slot}'
            if l == 0:
                hA, hB = 2 * u, 2 * u + 1
                gA, gB = hA // 3, hB // 3
                for half, h, g in ((0, hA, gA), (1, hB, gB)):
                    S.dma('sp', lambda e, half=half, h=h: e.dma_start(out=Q[half * 64:(half + 1) * 64, :],
                                                                     in_=qkT_s[0][h * 64:(h + 1) * 64, :]), deps, wk, sk)
                    S.dma('sp', lambda e, half=half, g=g: e.dma_start(out=K[half * 64:(half + 1) * 64, :],
                                                                     in_=qkT_s[0][768 + g * 64:768 + (g + 1) * 64, :]), deps, wk, sk)
                S.dma('sp', lambda e: e.dma_start(out=vA[:, :, 0:64],
                                                  in_=v_s[0][:, gA * 64:(gA + 1) * 64].rearrange("(c p) d -> p c d", p=128)), deps, wk, sk)
                S.dma('sp', lambda e: e.dma_start(out=vA[:, :, 64:128],
                                                  in_=v_s[0][:, gB * 64:(gB + 1) * 64].rearrange("(c p) d -> p c d", p=128)), deps, wk, sk)
            else:
                h = u
                S.dma('sp', lambda e: e.dma_start(out=Q[:, :], in_=qkT_s[1][h * 128:(h + 1) * 128, :]), deps, wk, sk)
                S.dma('sp', lambda e: e.dma_start(out=K[:, :], in_=qkT_s[1][768 + h * 128:768 + (h + 1) * 128, :]), deps, wk, sk)
                S.dma('sp', lambda e: e.dma_start(out=vA[:, :, :],
                                                  in_=v_s[1][:, h * 128:(h + 1) * 128].rearrange("(c p) d -> p c d", p=128)), deps, wk, sk)

        qblocks = ([(0, 256, [0, 1])] if l == 0 else []) + [(256 + qb * 512, 512, list(range(18))) for qb in range(4)]
        load_unit(0)
        for u in range(nunits):
            if u + 1 < nunits:
                load_unit(u + 1)
            slot = u % 2
            Q, K, vA, vB = att_views(l, slot)
            ak = ('att', slot)
            for (q0_, n_, chunks_) in qblocks:
                att_block(l, u, Q, K, vA, ak, q0_, n_, chunks_)

    def att_block(l, u, Q, K, vA, ak, q0, n, chunks):
            if True:
                def qk(c, ci):
                    for i in range(2):
                        bank = 2 * i + (ci % 2)
                        S.op('pe', lambda e, i=i, c=c, bank=bank: e.matmul(
                            PS(bank)[:, 0:n], lhsT=K[i * 64:(i + 1) * 64, c * 128:(c + 1) * 128],
                            rhs=Q[i * 64:(i + 1) * 64, q0:q0 + n], start=True, stop=True), [ak], [('ps', bank)])

                def ex(c, ci):
                    for i in range(2):
                        bank = 2 * i + (ci % 2)
                        ps_ = (2 * ci + i) % 4
                        S.op('act', lambda e, bank=bank, ps_=ps_: e.activation(
                            out=PT[:, ps_, 0:n], in_=PS(bank)[:, 0:n], func=AF.Exp, scale=SCALE),
                            [('ps', bank)], [('pt', ps_)])

                def pv(c, ci):
                    first, last = (ci == 0), (ci == len(chunks) - 1)
                    for i in range(2):
                        ps_ = (2 * ci + i) % 4
                        if False:
                            pass
                        else:
                            S.op('pe', lambda e, c=c, ps_=ps_, i=i: e.matmul(PS(4 + i)[:, 0:n], lhsT=vA[:, c, :], rhs=PT[:, ps_, 0:n],
                                                                             start=first, stop=last), [ak, ('pt', ps_)], [('ps', 4 + i)])
                            S.op('pe', lambda e, ps_=ps_, i=i: e.matmul(PS(6 + i)[:, 0:n], lhsT=ONB[:, :], rhs=PT[:, ps_, 0:n],
                                                                        start=first, stop=last), ['onb', ('pt', ps_)], [('ps', 6 + i)])

                nch = len(chunks)
                qk(chunks[0], 0)
                ex(chunks[0], 0)
                for ci in range(nch):
                    if ci + 1 < nch:
                        qk(chunks[ci + 1], ci + 1)
                        ex(chunks[ci + 1], ci + 1)
                    pv(chunks[ci], ci)
                if l == 0:
                    for i in range(2):
                        r0 = i * 64
                        RD = STG[:, i * 512:i * 512 + 512]
                        S.op('dve', lambda e, r0=r0, RD=RD, i=i: e.reciprocal(out=RD[r0:r0 + 64, 0:n], in_=PS(6 + i)[r0:r0 + 64, 0:n]),
                             [('ps', 6 + i)], [('rd', i)])
                        S.op('dve', lambda e, i=i, r0=r0, RD=RD: e.tensor_tensor(
                            out=catT[r0:r0 + 64, u, q0:q0 + n], in0=PS(4 + i)[r0:r0 + 64, 0:n], in1=RD[r0:r0 + 64, 0:n], op=ALU.mult),
                            [('ps', 4 + i), ('rd', i)], [('cat', u)])
                else:
                    T0, T1_ = STG[:, 0:512], STG[:, 512:1024]
                    SQB = TT[:, 0, :]
                    for i, Ti in ((0, T0), (1, T1_)):
                        S.op('dve', lambda e, i=i, Ti=Ti: e.reciprocal(out=Ti[:, 0:n], in_=PS(6 + i)[:, 0:n]), [('ps', 6 + i)], [('T', i)])
                        S.op('dve', lambda e, i=i, Ti=Ti: e.tensor_tensor(out=Ti[:, 0:n], in0=PS(4 + i)[:, 0:n], in1=Ti[:, 0:n], op=ALU.mult),
                             [('ps', 4 + i), ('T', i)], [('T', i)])
                    S.op('dve', lambda e: e.scalar_tensor_tensor(out=T0[:, 0:n], in0=T1_[:, 0:n], scalar=LAM[:, 4:5], in1=T0[:, 0:n],
                                                                 op0=ALU.mult, op1=ALU.add), [('T', 0), ('T', 1), 'lam'], [('T', 0)])
                    S.op('dve', lambda e: e.tensor_tensor(out=SQB[:, 0:n], in0=T0[:, 0:n], in1=T0[:, 0:n], op=ALU.mult),
                         [('T', 0)], [('tt', 0)])
                    S.op('pe', lambda e: e.matmul(PS(6)[:, 0:n], lhsT=ONB[:, :], rhs=SQB[:, 0:n], start=True, stop=True),
                         ['onb', ('tt', 0)], [('ps', 6)])
                    S.op('act', lambda e: e.activation(out=T1_[:, 0:n], in_=PS(6)[:, 0:n], func=AF.Ln, scale=1.0 / 128, bias=SM[:, 80:81]),
                         [('ps', 6), 'sm_eps'], [('T', 1)])
                    S.op('act', lambda e: e.activation(out=T1_[:, 0:n], in_=T1_[:, 0:n], func=AF.Exp, scale=-0.5), [('T', 1)], [('T', 1)])
                    S.op('dve', lambda e: e.scalar_tensor_tensor(out=catT[:, u, q0:q0 + n], in0=T0[:, 0:n], scalar=LAM[:, 5:6], in1=T1_[:, 0:n],
                                                                 op0=ALU.mult, op1=ALU.mult), [('T', 0), ('T', 1), 'lam'], [('cat', u)])

    def ln_update(t, Y, gbc, gkey, lng, lnb, lnkeys, ykeys):
        k = nxt('lnstg', 2)
        stg = STG[:, k * 1024:(k + 1) * 1024]
        sk = ('lnstg', k)
        so = 96 + k * 32
        stats = SM[:, so:so + 12]
        mv = SM[:, so + 12:so + 14]
        lnv = SM[:, so + 14:so + 15]
        rstd = SM[:, so + 15:so + 16]
        nb = SM[:, so + 16:so + 17]
        smk = ('lnsm', k)
        xk = ('x', t)
        S.op('dve', lambda e: e.tensor_tensor(out=stg, in0=Y, in1=gbc, op=ALU.mult), list(ykeys) + [gkey], [sk])
        S.op('dve', lambda e: e.scalar_tensor_tensor(out=X[:, t, :], in0=X[:, t, :], scalar=ALPHA, in1=stg,
                                                     op0=ALU.mult, op1=ALU.add), [xk, sk], [xk])
        for hh in range(2):
            S.op('dve', lambda e, hh=hh: e.bn_stats(out=stats[:, hh * 6:(hh + 1) * 6], in_=X[:, t, hh * 512:(hh + 1) * 512]),
                 [xk], [smk])
        S.op('dve', lambda e: e.bn_aggr(out=mv, in_=stats), [smk], [smk])
        S.op('act', lambda e: e.activation(out=lnv, in_=mv[:, 1:2], func=AF.Ln, bias=SM[:, 80:81]), [smk, 'sm_eps'], [smk])
        S.op('act', lambda e: e.activation(out=rstd, in_=lnv, func=AF.Exp, scale=-0.5), [smk], [smk])
        S.op('dve', lambda e: e.scalar_tensor_tensor(out=nb, in0=mv[:, 0:1], scalar=-1.0, in1=rstd, op0=ALU.mult, op1=ALU.mult),
             [smk], [smk])
        S.op('act', lambda e: e.activation(out=stg, in_=X[:, t, :], func=AF.Identity, bias=nb, scale=rstd), [xk, smk], [sk])
        S.op('dve', lambda e: e.tensor_tensor(out=stg, in0=stg, in1=lng, op=ALU.mult), [sk, lnkeys[0]], [sk])
        S.op('dve', lambda e: e.tensor_tensor(out=X[:, t, :], in0=stg, in1=lnb, op=ALU.add), [sk, lnkeys[1]], [xk])

    def phase_p4(l):
        S.dma('sp', lambda e: e.dma_start(out=WOUT[:, :, :], in_=wo_s[l].rearrange("(k p) c -> p k c", p=128)),
              [f'wo{l}'], ['wout'], 'wout')
        build_gate_bc(BCT[0], l, 2, 0, ('bct', 0))
        if l == 0:
            build_gate_bc(BCT[1], l, 2, 1, ('bct', 1))
        load_ln_bc(BCT[2], l * 4 + 0, ('bct', 2))
        load_ln_bc(BCT[3], l * 4 + 1, ('bct', 3))
        tiles = range(NT) if l == 0 else range(2, NT)
        for t in tiles:
            p4_tile(l, t)

    def p4_tile(l, t):
        if True:
            pb = nxt('p4pair', 2)
            w = 1 if t < 2 else 0
            for nh in range(2):
                bank = 2 * pb + nh
                for kc in range(8):
                    S.op('pe', lambda e, kc=kc, nh=nh, bank=bank: e.matmul(
                        PS(bank), lhsT=catT[:, kc, t * 128:(t + 1) * 128], rhs=WOUT[:, kc, nh * 512:(nh + 1) * 512],
                        start=(kc == 0), stop=(kc == 7)), [('cat', kc), 'wout'], [('ps', bank)])
            ln_update(t, PSB[pb][:, :], BCT[w], ('bct', w), BCT[2], BCT[3], [('bct', 2), ('bct', 3)],
                      [('ps', 2 * pb), ('ps', 2 * pb + 1)])

    def phase_p5(l, last):
        S.dma('sp', lambda e: e.dma_start(out=WO[:, 0:11, :], in_=fo_s[l][0:1408, :].rearrange("(j p) c -> p j c", p=128)),
              [f'fo{l}'], ['wo'], 'wo')
        S.dma('sp', lambda e: e.dma_start(out=WO[:, 11:22, :], in_=fo_s[l][1408:2816, :].rearrange("(j p) c -> p j c", p=128)),
              [f'fo{l}'], ['wo'], 'wo')
        build_gate_bc(BCT[0], l, 5, 0, ('bct', 0))
        if l == 0:
            build_gate_bc(BCT[1], l, 5, 1, ('bct', 1))
        load_ln_bc(BCT[2], l * 4 + 2, ('bct', 2))
        load_ln_bc(BCT[3], l * 4 + 3, ('bct', 3))
        sbs = BLOCKS if l == 0 else BLOCKS[1:]
        for tiles, w in sbs:
            p5_sb(l, last, tiles, w)

    def p5_sb(l, last, tiles, w):
        if True:
            n = 128 * len(tiles)
            for kc in range(8):
                bank = 4 + nxt('p5tb', 4)
                for i, t in enumerate(tiles):
                    S.op('pe', lambda e, i=i, t=t, kc=kc, bank=bank: e.transpose(
                        out=PS(bank)[:, i * 128:(i + 1) * 128], in_=X[:, t, kc * 128:(kc + 1) * 128], identity=ident),
                        [('x', t), 'cst'], [('ps', bank)])
                S.op('act', lambda e, kc=kc, bank=bank: e.activation(
                    out=h2T[:, kc, 0:n], in_=PS(bank)[:, 0:n], func=AF.Identity, bias=modv(l, 3, kc, w), scale=modv(l, 4, kc, w)),
                    [('ps', bank), ('mod', l)], [('h2T', kc)])
            for jp in range(11):
                slot = nxt('fring', 2)
                sa, sg_ = FRING[slot]
                S.dma('sp', lambda e, jp=jp, sa=sa: e.dma_start(
                    out=sa[:, :, :], in_=fi_s[l][:, jp * 256:(jp + 1) * 256].rearrange("(k p) c -> p k c", p=128)),
                    [f'fi{l}'], [('fring', slot)], f'fring{slot}')
                S.dma('sp', lambda e, jp=jp, sg_=sg_: e.dma_start(
                    out=sg_[:, :, :], in_=fi_s[l][:, FFN_H + jp * 256:FFN_H + (jp + 1) * 256].rearrange("(k p) c -> p k c", p=128)),
                    [f'fi{l}'], [('fring', slot)], f'fring{slot}')
                for jj in range(2):
                    j = 2 * jp + jj
                    pr = nxt('p5ab', 2)
                    ba, bg = 2 * pr, 2 * pr + 1
                    for (bank, slab) in ((ba, sa), (bg, sg_)):
                        for kc in range(8):
                            S.op('pe', lambda e, kc=kc, bank=bank, slab=slab, jj=jj: e.matmul(
                                PS(bank)[:, 0:n], lhsT=slab[:, kc, jj * 128:(jj + 1) * 128], rhs=h2T[:, kc, 0:n],
                                start=(kc == 0), stop=(kc == 7)), [('fring', slot), ('h2T', kc)], [('ps', bank)])
                    sgs = nxt('sgs', 2)
                    S.op('act', lambda e, bg=bg, sgs=sgs: e.activation(out=SG[:, sgs, 0:n], in_=PS(bg)[:, 0:n], func=AF.Silu),
                         [('ps', bg)], [('sg', sgs)])
                    S.op('dve', lambda e, ba=ba, sgs=sgs, j=j: e.tensor_tensor(out=hidT[:, j, 0:n], in0=PS(ba)[:, 0:n],
                                                                              in1=SG[:, sgs, 0:n], op=ALU.mult),
                         [('ps', ba), ('sg', sgs)], [('hid', j)])
            for ti, t in enumerate(tiles):
                pb = 2 + nxt('p5pair', 2)
                for nh in range(2):
                    bank = 2 * pb + nh
                    for j in range(22):
                        S.op('pe', lambda e, j=j, nh=nh, bank=bank, ti=ti: e.matmul(
                            PS(bank), lhsT=hidT[:, j, ti * 128:(ti + 1) * 128], rhs=WO[:, j, nh * 512:(nh + 1) * 512],
                            start=(j == 0), stop=(j == 21)), [('hid', j), 'wo'], [('ps', bank)])
                ln_update(t, PSB[pb][:, :], BCT[w], ('bct', w), BCT[2], BCT[3], [('bct', 2), ('bct', 3)],
                          [('ps', 2 * pb), ('ps', 2 * pb + 1)])
                if last:
                    S.dma('sp', lambda e, t=t: e.dma_start(out=out_d[(t - 2) * 128:(t - 1) * 128, :], in_=X[:, t, :]),
                          [('x', t)], [('out', t)], 'outw', sbuf=False)

    S.op('dve', lambda e: e.memset(SM[:, 80:81], EPS), [], ['sm_eps'])

    def dump_all():
        S.barrier()
        for t in range(NT):
            S.dma('sp', lambda e, t=t: e.dma_start(out=dbg_d[t * 128:(t + 1) * 128, :], in_=X[:, t, :]),
                  [('x', t)], [('dbg', t)], 'dbgw', sbuf=False)
        S.dma('sp', lambda e: e.dma_start(out=dbg_hb, in_=HB[:, 0:18432]), [], ['dbg_hb'], 'dbgw', sbuf=False)
        S.dma('sp', lambda e: e.dma_start(out=dbg_mod, in_=MOD[:, :, :, :].rearrange("p l o w -> p (l o w)")), [], ['dbg_mod'], 'dbgw', sbuf=False)
        S.dma('sp', lambda e: e.dma_start(out=dbg_r3, in_=R3[:, :]), [], ['dbg_r3'], 'dbgw', sbuf=False)

    def run_all():
        for l in range(2):
            compute_mod(l)
            S.barrier()
            if stop == f'mod{l}':
                return
            phase_p0(l)
            if stop == f'p0{l}':
                return
            if l == 0:
                S.op('dve', lambda e: e.memset(R3[:, :], 0.0), [], ['uT'])
            phase_p1(l)
            S.barrier()
            if stop == f'p1{l}':
                return
            if l == 0:
                phase_pool()
            else:
                phase_conv()
            S.barrier()
            if stop == f'p2{l}':
                return
            phase_att(l)
            S.barrier()
            if stop == f'p3{l}':
                return
            phase_p4(l)
            S.barrier()
            if stop == f'p4{l}':
                return
            phase_p5(l, last=(l == 1))
            S.barrier()
            if stop == f'p5{l}':
                return
    run_all()
    if debug:
        dump_all()
    S.wait_all('sp', ['outw', 'dbgw'])

    with nc.Block() as block:
        S.emit(block)
    es.close()
    return nc


def _rope_tables():
    n = np.arange(2048)
    row = (n // 64).astype(np.float32)
    col = (n % 64).astype(np.float32)
    freqs = (np.float32(10000.0) ** (-np.arange(0, 32, 2, dtype=np.float32) / np.float32(32))).astype(np.float32)
    ang = np.concatenate([row[:, None] * freqs, col[:, None] * freqs], axis=-1).astype(np.float32)
    cos, sin = np.cos(ang).astype(np.float32), np.sin(ang).astype(np.float32)
    cr, cc, sr, sc = cos[:, :16], cos[:, 16:], sin[:, :16], sin[:, 16:]
    C = np.concatenate([cr, cr, cc, cc], axis=-1)
    Sg = np.concatenate([-sr, sr, -sc, sc], axis=-1)
    C = C.reshape(16, 128, 64).transpose(1, 0, 2)
    Sg = Sg.reshape(16, 128, 64).transpose(1, 0, 2)
    return np.ascontiguousarray(C), np.ascontiguousarray(Sg)


def _consts():
    cst = np.zeros((128, NCST), np.float32)
    cst[:, C_ID:C_ID + 128] = np.eye(128, dtype=np.float32)
    C, Sg = _rope_tables()
    cst[:, C_RC:C_RC + 1024] = C.reshape(128, 1024)
    cst[:, C_RS:C_RS + 1024] = Sg.reshape(128, 1024)
    pe = np.zeros((128, 2, 2, 8), np.float32)
    for p in range(128):
        for c in range(2):
            w = POOL_W[2 * c + p // 64]
            for i in range(8):
                pe[p, c, 0, i] = 1.0 / ((i + w // 2) - max(i - w // 2, 0))
                pe[p, c, 1, i] = 1.0 / min(w, 8 - i + w // 2)
    cst[:, C_PE:C_PE + 32] = pe.reshape(128, 32)
    return cst


def _colvec(v):
    v = np.asarray(v, np.float32).reshape(-1, 128)
    return v.T


_NC_CACHE = {}


def kernel(x, c, ctx, c_ctx, ab_w_in, ab_q_gain, ab_k_gain, ab_w_pool, ab_pool_scale, ab_w_out,
           cd_w_in, cd_lambda_q1, cd_lambda_k1, cd_lambda_q2, cd_lambda_k2, cd_subln_gain,
           cd_conv_w, cd_conv_b, cd_conv_ln_g, cd_conv_ln_b, cd_w_out,
           ada_w, ada_b, ln1_g, ln1_b, ln2_g, ln2_b, ffn_w_in, ffn_w_out, _debug=False, _stop=None):
    f = lambda a: np.ascontiguousarray(np.asarray(a, dtype=np.float32))
    x, c, ctx, c_ctx = f(x), f(c), f(ctx), f(c_ctx)
    if (_debug, _stop) not in _NC_CACHE:
        _NC_CACHE[(_debug, _stop)] = build_program(debug=_debug, stop=_stop)
    nc = _NC_CACHE[(_debug, _stop)]
    cst = _consts()
    lnbc = np.stack([f(ln1_g)[0], f(ln1_b)[0], f(ln2_g)[0], f(ln2_b)[0],
                     f(ln1_g)[1], f(ln1_b)[1], f(ln2_g)[1], f(ln2_b)[1]], axis=0)
    gains = np.stack([f(ab_q_gain)[0], f(ab_k_gain)[0]], axis=0)
    shared = {
        "consts": cst, "lnbc": np.ascontiguousarray(lnbc), "gains": np.ascontiguousarray(gains),
        "ab_w_in": f(ab_w_in)[0], "cd_w_in": f(cd_w_in)[0], "ab_w_out": f(ab_w_out)[0], "cd_w_out": f(cd_w_out)[0],
        "ab_w_pool": f(ab_w_pool)[0].reshape(256, 64), "ada_w": f(ada_w).reshape(2 * D, 6 * D),
        "ffn_w_in": f(ffn_w_in).reshape(2 * D, 2 * FFN_H), "ffn_w_out": f(ffn_w_out).reshape(2 * FFN_H, D),
    }
    in_maps = []
    for b in range(NCORES):
        vecs = np.zeros((128, NV), np.float32)
        vecs[:, 0:8] = _colvec(c[b])
        vecs[:, 8:16] = _colvec(c_ctx)
        vecs[:, 16:64] = _colvec(f(ada_b)[0])
        vecs[:, 64:112] = _colvec(f(ada_b)[1])
        vecs[:, 112:114] = _colvec(f(ab_pool_scale)[0])
        vecs[:, 114:116] = _colvec(f(cd_conv_b)[0])
        vecs[:, 116:118] = _colvec(f(cd_conv_ln_g)[0])
        vecs[:, 118:120] = _colvec(f(cd_conv_ln_b)[0])
        vecs[:, 120:121] = _colvec(f(cd_subln_gain)[0])
        vecs[0:64, 121] = f(cd_lambda_q1)[0]
        vecs[0:64, 122] = f(cd_lambda_k1)[0]
        vecs[0:64, 123] = f(cd_lambda_q2)[0]
        vecs[0:64, 124] = f(cd_lambda_k2)[0]
        cw = f(cd_conv_w)[0]
        for j in range(31):
            vecs[:, 125 + 2 * j:127 + 2 * j] = _colvec(cw[j])
        m = dict(shared)
        m["x"] = x[b]
        m["ctx"] = ctx[b]
        m["vecs"] = vecs
        in_maps.append(m)
    res = run_bass_kernel_spmd(nc, in_maps, core_ids=list(range(NCORES)))
    out = np.stack([np.asarray(r["out"], dtype=np.float32) for r in res.results], axis=0)
    if _debug:
        return out, res.results
    return out
```

```python
import math
from contextlib import ExitStack

import numpy as np
import concourse.bass as bass
import concourse.mybir as mybir
from concourse.bass_utils import run_bass_kernel_spmd

F32 = mybir.dt.float32
BF16 = mybir.dt.bfloat16
AF = mybir.ActivationFunctionType
ALU = mybir.AluOpType
AX = mybir.AxisListType

NCORES = 8
D = 1024
NT = 18
TOK = 2304
ALPHA = 4.0 ** 0.25
EPS = 1e-6
FFN_H = 2816
LAM_INIT1 = 0.8 - 0.6 * math.exp(-0.3)
POOL_W = (2, 4, 8, 16)
NV = 192
C_ID = 0
C_RC = 128
C_RS = C_RC + 16 * 64
C_PE = C_RS + 16 * 64
NCST = C_PE + 32
UW = 2336


class Sched:
    def __init__(self, nc, es):
        self.nc = nc
        self.es = es
        self.engs = ['pe', 'act', 'dve', 'pool', 'sp']
        self.prog = {e: [] for e in self.engs}
        self.cnt = {e: 0 for e in self.engs}
        self.seen = {e: {} for e in self.engs}
        self.lastw = {}
        self.readers = {}
        self.sems = {}
        self.dcum = {}
        self.sbuf_dma = set()
        for e in self.engs:
            self.sems[e] = es.enter_context(nc.semaphore("sem_" + e))

    def _deps(self, reads, writes):
        deps = {}
        raw = {}

        def add(dct, s, v):
            if dct.get(s, 0) < v:
                dct[s] = v
        for k in reads:
            t = self.lastw.get(k)
            if t is not None:
                add(deps, *t)
                add(raw, *t)
        for k in writes:
            t = self.lastw.get(k)
            if t is not None:
                add(deps, *t)
            for s, v in self.readers.get(k, {}).items():
                add(deps, s, v)
        self._raw = raw
        return deps

    def _commit(self, tok, reads, writes):
        s, v = tok
        for k in reads:
            r = self.readers.setdefault(k, {})
            if r.get(s, 0) < v:
                r[s] = v
        for k in writes:
            self.lastw[k] = tok
            self.readers[k] = {}

    def _waits(self, eng, deps):
        w = []
        for s, v in deps.items():
            if s == eng and eng in ('pe', 'sp'):
                continue
            if self.seen[eng].get(s, 0) >= v:
                continue
            self.seen[eng][s] = v
            w.append((s, v))
        return w

    def op(self, eng, fn, reads=(), writes=()):
        deps = self._deps(reads, writes)
        waits = self._waits(eng, deps)
        self.cnt[eng] += 1
        self.prog[eng].append((waits, fn, (eng, 1)))
        self._commit((eng, self.cnt[eng]), reads, writes)

    def dma(self, eng, fn, reads, writes, semkey, sbuf=True):
        if semkey not in self.sems:
            self.sems[semkey] = self.es.enter_context(self.nc.semaphore("d_" + semkey))
            self.dcum[semkey] = 0
        if sbuf:
            self.sbuf_dma.add(semkey)
        deps = self._deps(reads, writes)
        waits = self._waits(eng, deps)
        self.dcum[semkey] += 16
        self.prog[eng].append((waits, fn, (semkey, 16)))
        self._commit((semkey, self.dcum[semkey]), reads, writes)

    def barrier(self, engs=('pe', 'act', 'dve', 'sp', 'pool')):
        toks = {e: self.cnt[e] for e in ('pe', 'act', 'dve', 'pool') if self.cnt[e] > 0}
        for k in self.sbuf_dma:
            toks[k] = self.dcum[k]
        self._raw = dict(toks)
        for e in engs:
            w = self._waits(e, toks)
            if w:
                self.prog[e].append((w, None, None))

    def gate(self, eng, on):
        toks = {on: self.cnt[on]}
        self._raw = dict(toks)
        w = self._waits(eng, toks)
        if w:
            self.prog[eng].append((w, None, None))

    def wait_all(self, eng, semkeys):
        toks = {k: self.dcum[k] for k in semkeys if k in self.dcum}
        self._raw = dict(toks)
        w = self._waits(eng, toks)
        if w:
            self.prog[eng].append((w, None, None))

    def emit(self, block):
        emap = {'pe': block.tensor, 'act': block.scalar, 'dve': block.vector,
                'pool': block.gpsimd, 'sp': block.sync}
        for e in self.engs:
            prog = self.prog[e]

            def body(eng, prog=prog, ename=e):
                for waits, fn, inc in prog:
                    attach = None
                    if fn is not None and ename in ('act', 'dve') and waits:
                        attach = waits[-1]
                        waits = waits[:-1]
                    for s, v in waits:
                        eng.wait_ge(self.sems[s], v)
                    if fn is not None:
                        ins = fn(eng)
                        if attach is not None:
                            ins._wait_ge(self.sems[attach[0]], attach[1])
                        ins.then_inc(self.sems[inc[0]], inc[1])
            emap[e](body)


def build_program(debug=False, stop=None):
    nc = bass.Bass("TRN2", target_bir_lowering=False, dynamic_dma_scratch_size=4096)
    es = ExitStack()

    def din(name, shape, dt=F32):
        return nc.dram_tensor(name, list(shape), dt, kind="ExternalInput").ap()

    x_d = din("x", [2048, D])
    ctx_d = din("ctx", [256, D])
    vecs_d = din("vecs", [128, NV])
    cst_d = din("consts", [128, NCST])
    lnbc_d = din("lnbc", [8, D])
    gains_d = din("gains", [2, 64])
    wq_d = [din("ab_w_in", [D, 1536]), din("cd_w_in", [D, 2816])]
    wo_d = [din("ab_w_out", [D, D]), din("cd_w_out", [D, D])]
    wpool_d = din("ab_w_pool", [256, 64])
    ada_d = din("ada_w", [2 * D, 6 * D])
    fi_d = din("ffn_w_in", [2 * D, 2 * FFN_H])
    fo_d = din("ffn_w_out", [2 * FFN_H, D])
    out_d = nc.dram_tensor("out", [2048, D], F32, kind="ExternalOutput").ap()
    dbg_d = None
    if debug:
        dbg_d = nc.dram_tensor("dbg", [TOK, D], F32, kind="ExternalOutput").ap()

    def dscr(name, shape):
        if debug and name in ("qkT0_s", "v0_s", "qkT1_s", "v1_s"):
            return nc.dram_tensor(name, list(shape), BF16, kind="ExternalOutput").ap()
        return nc.dram_tensor(name, list(shape), BF16).ap()

    if debug:
        dbg_hb = nc.dram_tensor("dbg_hb", [128, 18432], BF16, kind="ExternalOutput").ap()
        dbg_mod = nc.dram_tensor("dbg_mod", [128, 192], F32, kind="ExternalOutput").ap()
        dbg_r3 = nc.dram_tensor("dbg_r3", [128, 4672], F32, kind="ExternalOutput").ap()
        dbg_att = nc.dram_tensor("dbg_att", [128, 1024], F32, kind="ExternalOutput").ap()
        dbg_stg = nc.dram_tensor("dbg_stg", [128, 2048], F32, kind="ExternalOutput").ap()
        dbg_pt = nc.dram_tensor("dbg_pt", [128, 2048], BF16, kind="ExternalOutput").ap()

    wq_s = [dscr("wq0_s", [D, 1536]), dscr("wq1_s", [D, 2816])]
    wo_s = [dscr("wo0_s", [D, D]), dscr("wo1_s", [D, D])]
    ada_s = [dscr("ada0_s", [D, 6 * D]), dscr("ada1_s", [D, 6 * D])]
    fi_s = [dscr("fi0_s", [D, 2 * FFN_H]), dscr("fi1_s", [D, 2 * FFN_H])]
    fo_s = [dscr("fo0_s", [FFN_H, D]), dscr("fo1_s", [FFN_H, D])]
    NQK = [1024, 1536]
    VW = [256, 768]
    qkT_s = [dscr("qkT0_s", [NQK[0], TOK]), dscr("qkT1_s", [NQK[1], TOK])]
    v_s = [dscr("v0_s", [TOK, VW[0]]), dscr("v1_s", [TOK, VW[1]])]

    def sb(name, shape, dt):
        return es.enter_context(nc.sbuf_tensor(name, list(shape), dt))

    X = sb("X", [128, NT, D], F32)
    HB = sb("HB", [128, 30720], BF16)
    RA = sb("RA", [128, 16384], BF16)
    R3 = sb("R3", [128, 4672], F32)
    CST = sb("CST", [128, NCST], F32)
    VEC = sb("VEC", [128, NV], F32)
    MOD = sb("MOD", [128, 2, 48, 2], F32)
    SILC = sb("SILC", [128, 8, 2], BF16)
    IDB = sb("IDB", [128, 128], BF16)
    ONB = sb("ONB", [128, 128], BF16)
    ONF = sb("ONF", [128, 128], F32)
    STG = sb("STG", [128, 2048], F32)
    TT = sb("TT", [128, 2, 512], BF16)
    VST = sb("VST", [128, 2, 512], BF16)
    PT = sb("PT", [128, 4, 512], BF16)
    SG = sb("SG", [128, 2, 512], F32)
    GBC = sb("GBC", [128, 2, 64], F32)
    BD = sb("BD", [128, 2, 128], BF16)
    DG = sb("DG", [128, 2, 128], F32)
    SM = sb("SM", [128, 256], F32)
    LAM = sb("LAM", [128, 8], F32)

    PSB = [es.enter_context(nc.psum_tensor(f"psb{i}", [128, 1024], F32)) for i in range(4)]

    def PS(i):
        return PSB[i // 2][:, (i % 2) * 512:(i % 2) * 512 + 512]

    ident = CST[:, C_ID:C_ID + 128]
    ropeC = CST[:, C_RC:C_RC + 1024].rearrange("p (t d) -> p t d", d=64)
    ropeS = CST[:, C_RS:C_RS + 1024].rearrange("p (t d) -> p t d", d=64)
    pedge = CST[:, C_PE:C_PE + 32].rearrange("p (c s i) -> p c s i", c=2, s=2)

    hT_all = HB[:, 0:8 * TOK].rearrange("p (k t) -> p k t", k=8)
    catT = hT_all
    WSLAB = [HB[:, 18432 + s * 4096: 18432 + (s + 1) * 4096].rearrange("p (k c) -> p k c", k=8) for s in range(2)]
    WOUT = HB[:, 18432:18432 + 8192].rearrange("p (k c) -> p k c", k=8)
    WO = HB[:, 0:22528].rearrange("p (j c) -> p j c", j=22)
    FRING = [(HB[:, 22528 + s * 4096: 22528 + s * 4096 + 2048].rearrange("p (k c) -> p k c", k=8),
              HB[:, 22528 + s * 4096 + 2048: 22528 + (s + 1) * 4096].rearrange("p (k c) -> p k c", k=8))
             for s in range(2)]
    hidT = RA[:, 0:11264].rearrange("p (j t) -> p j t", j=22)
    h2T = RA[:, 11264:15360].rearrange("p (k t) -> p k t", k=8)
    BCT = [R3[:, i * 1024:(i + 1) * 1024] for i in range(4)]
    uT = R3[:, 0:2 * UW].rearrange("p (c w) -> p c w", c=2)
    aT = R3[:, 0:4096].rearrange("p (c t) -> p c t", c=2)
    RAF = RA[:, :].bitcast(F32)

    S = Sched(nc, es)
    rot = {}

    def nxt(name, n):
        v = rot.get(name, 0)
        rot[name] = (v + 1) % n
        return v

    SCALE = 0.125

    def cast2d(dst, src, key):
        r, c = src.shape
        if c > 2048:
            if c % 2048 == 0:
                d2 = dst.rearrange("a (b c) -> a b c", c=2048)
                s2 = src.rearrange("a (b c) -> a b c", c=2048)
            else:
                d2 = dst.rearrange("a b -> (a b)").rearrange("(n c) -> n c", c=2048)
                s2 = src.rearrange("a b -> (a b)").rearrange("(n c) -> n c", c=2048)
        else:
            d2, s2 = dst, src
        S.dma('pool', lambda e: e.dma_start(out=d2, in_=s2), [], [key], semkey="c_" + key, sbuf=False)

    CASTQ = []

    def qcast(dst, src, key, rows_per):
        r = src.shape[0]
        for r0 in range(0, r, rows_per):
            r1 = min(r, r0 + rows_per)
            CASTQ.append((key, lambda dst=dst, src=src, r0=r0, r1=r1, key=key: cast2d(dst[r0:r1, :], src[r0:r1, :], key)))

    def pump(n=1):
        for _ in range(n):
            if not CASTQ:
                return
            S.gate('pool', 'pe')
            CASTQ.pop(0)[1]()

    def need(key):
        while any(k == key for k, _ in CASTQ):
            CASTQ.pop(0)[1]()

    cast2d(ada_s[0][:, 0:2048], ada_d[0:D, 0:2048], "ada0a")
    cast2d(wq_s[0], wq_d[0], "wq0")
    cast2d(ada_s[0][:, 2048:6144], ada_d[0:D, 2048:6144], "ada0b")
    qcast(wo_s[0], wo_d[0], "wo0", 1024)
    qcast(fi_s[0], fi_d[0:D, :], "fi0", 128)
    qcast(fo_s[0], fo_d[0:FFN_H, :], "fo0", 704)
    qcast(ada_s[1], ada_d[D:2 * D, :], "ada1", 128)
    qcast(wq_s[1], wq_d[1], "wq1", 256)
    qcast(wo_s[1], wo_d[1], "wo1", 1024)
    qcast(fi_s[1], fi_d[D:2 * D, :], "fi1", 128)
    qcast(fo_s[1], fo_d[FFN_H:2 * FFN_H, :], "fo1", 704)

    S.dma('sp', lambda e: e.dma_start(out=CST[:, :], in_=cst_d), [], ['cst'], 'l_cst')
    S.dma('sp', lambda e: e.dma_start(out=VEC[:, :], in_=vecs_d), [], ['vec'], 'l_vec')
    S.dma('sp', lambda e: e.dma_start(out=GBC[:, :, :], in_=gains_d.partition_broadcast(128)), [], ['gbc'], 'l_gbc')
    S.dma('sp', lambda e: e.dma_start(out=X[:, 0:2, :], in_=ctx_d.rearrange("(t p) d -> p t d", p=128)),
          [], [('x', 0), ('x', 1)], 'l_ctx')
    for i in range(4):
        S.dma('sp', lambda e, i=i: e.dma_start(out=X[:, 2 + 4 * i:6 + 4 * i, :],
                                               in_=x_d[i * 512:(i + 1) * 512, :].rearrange("(t p) d -> p t d", p=128)),
              [], [('x', 2 + 4 * i + j) for j in range(4)], f'l_x{i}')

    S.op('dve', lambda e: e.memset(BD[:, :, :], 0.0), [], ['bd'])
    for c in range(2):
        for hh in range(2):
            g = 2 * c + hh
            S.dma('pool', lambda e, c=c, hh=hh, g=g: e.dma_start(out=BD[hh * 64:(hh + 1) * 64, c, hh * 64:(hh + 1) * 64],
                                                                 in_=wpool_d[g * 64:(g + 1) * 64, :]),
                  [], ['bd'], 'l_bd')
    S.op('dve', lambda e: e.memset(ONB[:, :], 1.0), [], ['onb'])
    S.op('dve', lambda e: e.memset(ONF[:, :], 1.0), [], ['onf'])
    S.op('dve', lambda e: e.tensor_copy(out=IDB[:, :], in_=ident), ['cst'], ['idb'])
    S.op('act', lambda e: e.activation(out=SILC[:, :, :].rearrange("p k w -> p w k"),
                                       in_=VEC[:, 0:16].rearrange("p (w k) -> p w k", w=2), func=AF.Silu),
         ['vec'], ['silc'])

    def compute_mod(l, part):
        if l == 0:
            keys_a = ['ada0a'] if part == 0 else ['ada0b']
        else:
            need('ada1')
            keys_a = ['ada1']
        srange = range(0, 4) if part == 0 else range(4, 12)
        oc0, oc1 = (0, 16) if part == 0 else (16, 48)
        for s_ in srange:
            slot = nxt('wslab', 2)
            S.dma('sp', lambda e, s_=s_, slot=slot: e.dma_start(
                out=WSLAB[slot][:, :, :], in_=ada_s[l][:, s_ * 512:(s_ + 1) * 512].rearrange("(k p) c -> p k c", p=128)),
                keys_a, [('wslab', slot)], f'wslab{slot}')
            for o4 in range(4):
                oc = s_ * 4 + o4
                for kc in range(8):
                    S.op('pe', lambda e, slot=slot, o4=o4, kc=kc, oc=oc: e.matmul(
                        PS(0)[:, oc * 2:oc * 2 + 2], lhsT=WSLAB[slot][:, kc, o4 * 128:(o4 + 1) * 128],
                        rhs=SILC[:, kc, :], start=(kc == 0), stop=(kc == 7)),
                        [('wslab', slot), 'silc'], [('ps', 0)])
        mps = PS(0)[:, 0:96].rearrange("p (o w) -> p o w", w=2)
        mk = ('mod', l, part)
        for w in range(2):
            S.op('dve', lambda e, w=w: e.tensor_tensor(out=MOD[:, l, oc0:oc1, w], in0=mps[:, oc0:oc1, w],
                                                       in1=VEC[:, 16 + 48 * l + oc0:16 + 48 * l + oc1], op=ALU.add),
                 [('ps', 0), 'vec'], [mk])
        s0 = 8 if part == 0 else 32
        S.op('dve', lambda e: e.tensor_scalar(out=MOD[:, l, s0:s0 + 8, :], in0=MOD[:, l, s0:s0 + 8, :],
                                              scalar1=1.0, scalar2=None, op0=ALU.add), [mk], [mk])

    def modv(l, s_, kc, w):
        return MOD[:, l, s_ * 8 + kc, w:w + 1]

    def build_gate_bc(dst, l, s_, w, key):
        for half in range(2):
            bank = nxt('bcbank', 2)
            for k4 in range(4):
                kc = half * 4 + k4
                dslot = nxt('dg', 2)
                S.op('dve', lambda e, kc=kc, dslot=dslot: e.tensor_scalar(
                    out=DG[:, dslot, :], in0=ident, scalar1=modv(l, s_, kc, w), scalar2=None, op0=ALU.mult),
                    ['cst', ('mod', l, 1)], [('dg', dslot)])
                S.op('pe', lambda e, k4=k4, dslot=dslot, bank=bank: e.matmul(
                    PS(bank)[:, k4 * 128:(k4 + 1) * 128], lhsT=ONF[:, :], rhs=DG[:, dslot, :], start=True, stop=True),
                    ['onf', ('dg', dslot)], [('ps', bank)])
            S.op('act', lambda e, half=half, bank=bank: e.activation(
                out=dst[:, half * 512:(half + 1) * 512], in_=PS(bank), func=AF.Copy),
                [('ps', bank)], [key])

    def load_ln_bc(dst, row, key):
        S.dma('sp', lambda e: e.dma_start(out=dst.unsqueeze(1), in_=lnbc_d[row:row + 1, :].partition_broadcast(128)),
              [], [key], 'l_' + str(key[1]))

    BLOCKS = [([0, 1], 1)] + [([2 + 4 * i + j for j in range(4)], 0) for i in range(4)]

    def phase_p0(l):
        for tiles, w in BLOCKS:
            n = 128 * len(tiles)
            for kc in range(8):
                bank = nxt('p0bank', 4)
                for i, t in enumerate(tiles):
                    S.op('pe', lambda e, i=i, t=t, kc=kc, bank=bank: e.transpose(
                        out=PS(bank)[:, i * 128:(i + 1) * 128], in_=X[:, t, kc * 128:(kc + 1) * 128], identity=ident),
                        [('x', t), 'cst'], [('ps', bank)])
                S.op('act', lambda e, kc=kc, bank=bank, n=n, t0=tiles[0], w=w: e.activation(
                    out=hT_all[:, kc, t0 * 128:t0 * 128 + n], in_=PS(bank)[:, 0:n], func=AF.Identity,
                    bias=modv(l, 0, kc, w), scale=modv(l, 1, kc, w)),
                    [('ps', bank), ('mod', l, 0)], [('hT', tiles[0], kc)])

    A_, B_, C_, D_ = (STG[:, 0:512], STG[:, 512:1024], STG[:, 1024:1536], STG[:, 1536:2048])

    def hv(ap, n):
        return ap.rearrange("p (h d) -> p h d", d=64)

    def post_qk(l, t, bank, cs, n, row0, gain):
        src = PS(bank)[:, cs:cs + n]
        NH = n // 64
        pk = ('ps', bank)
        if l == 0:
            S.op('act', lambda e: e.activation(out=C_[:, 0:n], in_=src, func=AF.Square), [pk], ['stgC'])
            S.op('dve', lambda e: e.tensor_reduce(out=SM[:, 0:NH], in_=hv(C_[:, 0:n], n), axis=AX.X, op=ALU.add),
                 ['stgC'], ['sm_ss'])
            S.op('dve', lambda e: e.tensor_scalar(out=SM[:, 0:NH], in0=SM[:, 0:NH], scalar1=1.0 / 64, scalar2=EPS,
                                                  op0=ALU.mult, op1=ALU.add), ['sm_ss'], ['sm_ss'])
            S.op('act', lambda e: e.activation(out=SM[:, 16:16 + NH], in_=SM[:, 0:NH], func=AF.Ln), ['sm_ss'], ['sm_ln'])
            S.op('act', lambda e: e.activation(out=SM[:, 32:32 + NH], in_=SM[:, 16:16 + NH], func=AF.Exp, scale=-0.5),
                 ['sm_ln'], ['sm_rs'])
            S.op('dve', lambda e: e.tensor_tensor(out=hv(A_[:, 0:n], n), in0=hv(src, n),
                                                  in1=SM[:, 32:32 + NH].unsqueeze(2).to_broadcast([128, NH, 64]), op=ALU.mult),
                 [pk, 'sm_rs'], ['stgA'])
            S.op('dve', lambda e: e.tensor_tensor(out=hv(A_[:, 0:n], n), in0=hv(A_[:, 0:n], n),
                                                  in1=GBC[:, gain, :].unsqueeze(1).to_broadcast([128, NH, 64]), op=ALU.mult),
                 ['stgA', 'gbc'], ['stgA'])
            xin, xk = A_[:, 0:n], 'stgA'
        else:
            xin, xk = src, pk
        if t < 2:
            if l == 0:
                res, rk = A_[:, 0:n], 'stgA'
            else:
                S.op('act', lambda e: e.activation(out=B_[:, 0:n], in_=src, func=AF.Copy), [pk], ['stgB'])
                res, rk = B_[:, 0:n], 'stgB'
        else:
            tt = t - 2
            S.op('dve', lambda e: e.tensor_tensor(out=hv(B_[:, 0:n], n), in0=hv(xin, n),
                                                  in1=ropeC[:, tt, :].unsqueeze(1).to_broadcast([128, NH, 64]), op=ALU.mult),
                 [xk, 'cst'], ['stgB'])

            def v4(ap):
                return ap.rearrange("p (h a two s) -> p h a two s", a=2, two=2, s=16)
            sv = ropeS[:, tt, :].rearrange("p (a two s) -> p a two s", a=2, two=2)
            for two in range(2):
                S.op('dve', lambda e, two=two: e.tensor_tensor(
                    out=v4(C_[:, 0:n])[:, :, :, two, :], in0=v4(xin)[:, :, :, 1 - two, :],
                    in1=sv[:, :, two, :].unsqueeze(1).to_broadcast([128, NH, 2, 16]), op=ALU.mult),
                    [xk, 'cst'], ['stgC'])
            S.op('dve', lambda e: e.tensor_tensor(out=B_[:, 0:n], in0=B_[:, 0:n], in1=C_[:, 0:n], op=ALU.add),
                 ['stgB', 'stgC'], ['stgB'])
            res, rk = B_[:, 0:n], 'stgB'
        tb = 3 + nxt('p1tb', 2)
        for i in range(n // 128):
            S.op('pe', lambda e, i=i, tb=tb: e.transpose(out=PS(tb)[:, i * 128:(i + 1) * 128],
                                                         in_=res[:, i * 128:(i + 1) * 128], identity=ident),
                 [rk, 'cst'], [('ps', tb)])
        ts = nxt('tt', 2)
        S.op('act', lambda e, tb=tb, ts=ts: e.activation(out=TT[:, ts, 0:n], in_=PS(tb)[:, 0:n], func=AF.Copy),
             [('ps', tb)], [('tt', ts)])
        S.dma('sp', lambda e, ts=ts: e.dma_start(
            out=qkT_s[l][row0:row0 + n, t * 128:(t + 1) * 128].rearrange("(i p) c -> p i c", p=128),
            in_=TT[:, ts, 0:n].rearrange("p (i c) -> p i c", c=128)),
            [('tt', ts)], [('qkT', l, t, row0)], f'ttw{ts}', sbuf=False)

    def post_v(l, t, bank, cs, n, vc0):
        vs = nxt('vst', 2)
        S.op('act', lambda e: e.activation(out=VST[:, vs, 0:n], in_=PS(bank)[:, cs:cs + n], func=AF.Copy),
             [('ps', bank)], [('vst', vs)])
        S.dma('sp', lambda e: e.dma_start(out=v_s[l][t * 128:(t + 1) * 128, vc0:vc0 + n], in_=VST[:, vs, 0:n]),
              [('vst', vs)], [('vs', l, t, vc0)], f'vstw{vs}', sbuf=False)

    def post_u_T(t, bank, cs):
        S.op('act', lambda e: e.activation(out=D_[:, 0:256], in_=PS(bank)[:, cs:cs + 256], func=AF.Copy),
             [('ps', bank)], ['stgD'])
        tb = 3 + nxt('p1tb', 2)
        for c in range(2):
            S.op('pe', lambda e, c=c, tb=tb: e.transpose(out=PS(tb)[:, c * 128:(c + 1) * 128],
                                                         in_=D_[:, c * 128:(c + 1) * 128], identity=ident),
                 ['stgD', 'cst'], [('ps', tb)])
        return tb

    def uoff(t):
        return 8 + t * 128 if t < 2 else 280 + (t - 2) * 128

    def post_u0(t, bank, cs):
        tb = post_u_T(t, bank, cs)
        S.op('act', lambda e: e.activation(out=uT[:, :, uoff(t):uoff(t) + 128],
                                           in_=PS(tb)[:, 0:256].rearrange("p (c t) -> p c t", c=2), func=AF.Copy),
             [('ps', tb)], ['uT'])

    def post_ua(t, bank, cs):
        tb = post_u_T(t, bank, cs)
        S.op('act', lambda e: e.activation(out=aT[:, :, (t - 2) * 128:(t - 1) * 128],
                                           in_=PS(tb)[:, 0:256].rearrange("p (c t) -> p c t", c=2), func=AF.Copy),
             [('ps', tb)], [('aT', t)])

    zbf = RA[:, 0:4160].rearrange("p (c w) -> p c w", c=2)

    def post_ug(t, bank, cs):
        tb = post_u_T(t, bank, cs)
        S.op('act', lambda e: e.activation(out=SG[:, 0, 0:256], in_=PS(tb)[:, 0:256], func=AF.Sigmoid),
             [('ps', tb)], ['sg0'])
        o = 15 + (t - 2) * 128
        S.op('dve', lambda e: e.tensor_tensor(out=zbf[:, :, o:o + 128], in0=aT[:, :, (t - 2) * 128:(t - 1) * 128],
                                              in1=SG[:, 0, 0:256].rearrange("p (c t) -> p c t", c=2), op=ALU.mult),
             [('aT', t), 'sg0'], ['zbf'])

    def segs_for(l, si):
        if l == 0:
            return [[('qk', 0, 512, (0, 0))],
                    [('qk', 0, 256, (512, 0)), ('qk', 256, 256, (768, 1))],
                    [('v', 0, 256, 0), ('u0', 256, 256, None)]][si]
        return [[('qk', 0, 512, (0, None))],
                [('qk', 0, 256, (512, None)), ('qk', 256, 256, (768, None))],
                [('qk', 0, 512, (1024, None))],
                [('v', 0, 512, 0)],
                [('v', 0, 256, 512), ('ua', 256, 256, None)],
                [('ug', 0, 256, None)]][si]

    def phase_p1(l):
        ncols = 1536 if l == 0 else 2816
        slabs = [(c0, min(512, ncols - c0)) for c0 in range(0, ncols, 512)]
        if l == 1:
            S.op('dve', lambda e: e.memset(zbf[:, :, :], 0.0), [], ['zbf'])
        need(f'wq{l}')
        for si, (c0, ncs) in enumerate(slabs):
            slot = nxt('wslab', 2)
            S.dma('sp', lambda e, c0=c0, ncs=ncs, slot=slot: e.dma_start(
                out=WSLAB[slot][:, :, 0:ncs], in_=wq_s[l][:, c0:c0 + ncs].rearrange("(k p) c -> p k c", p=128)),
                [f'wq{l}'], [('wslab', slot)], f'wslab{slot}')
            segs = segs_for(l, si)
            for t in range(NT):
                if t < 2 and all(k in ('ua', 'ug') for k, _, _, _ in segs):
                    continue
                bank = nxt('p1bank', 3)
                if t % 6 == 5:
                    pump()
                t0 = 0 if t < 2 else 2 + 4 * ((t - 2) // 4)
                for kc in range(8):
                    S.op('pe', lambda e, kc=kc, t=t, bank=bank, slot=slot, ncs=ncs: e.matmul(
                        PS(bank)[:, 0:ncs], lhsT=hT_all[:, kc, t * 128:(t + 1) * 128], rhs=WSLAB[slot][:, kc, 0:ncs],
                        start=(kc == 0), stop=(kc == 7)),
                        [('hT', t0, kc), ('wslab', slot)], [('ps', bank)])
                for kind, cs, n, ex in segs:
                    if kind == 'qk':
                        post_qk(l, t, bank, cs, n, ex[0], ex[1])
                    elif kind == 'v':
                        post_v(l, t, bank, cs, n, ex)
                    elif kind == 'u0':
                        post_u0(t, bank, cs)
                    elif kind == 'ua' and t >= 2:
                        post_ua(t, bank, cs)
                    elif kind == 'ug' and t >= 2:
                        post_ug(t, bank, cs)

    def phase_pool():
        T1 = [RAF[:, 0:UW], RAF[:, UW:2 * UW]]
        pooled = RA[:, 9472:9472 + 2 * TOK].rearrange("p (c t) -> p c t", c=2)
        segsP = [(8, 0, 256), (280, 256, 2048)]

        def emit_group(c, hh, w, Aw):
            P0_, P1_ = hh * 64, hh * 64 + 64
            hw = w // 2
            for (ps_, ts_, L) in segsP:
                S.op('dve', lambda e, ps_=ps_, ts_=ts_, L=L: e.scalar_tensor_tensor(
                    out=pooled[P0_:P1_, c, ts_:ts_ + L], in0=Aw[P0_:P1_, ps_ - hw:ps_ - hw + L], scalar=1.0 / w,
                    in1=uT[P0_:P1_, c, ps_:ps_ + L], op0=ALU.mult, op1=ALU.subtract),
                    ['uT', 'poolA'], ['pooled'])
                for side in range(2):
                    po = ps_ if side == 0 else ps_ + L - 8
                    to = ts_ if side == 0 else ts_ + L - 8
                    S.op('dve', lambda e, po=po, side=side: e.tensor_tensor(
                        out=SM[P0_:P1_, 64:72], in0=Aw[P0_:P1_, po - hw:po - hw + 8], in1=pedge[P0_:P1_, c, side, :], op=ALU.mult),
                        ['poolA', 'cst'], ['sm_pe'])
                    S.op('dve', lambda e, po=po, to=to: e.tensor_tensor(
                        out=pooled[P0_:P1_, c, to:to + 8], in0=SM[P0_:P1_, 64:72], in1=uT[P0_:P1_, c, po:po + 8], op=ALU.subtract),
                        ['sm_pe', 'uT'], ['pooled'])

        for c in range(2):
            src = uT[:, c, :]
            cur = 0
            S.op('dve', lambda e, src=src: e.tensor_tensor(out=T1[0][:, 0:UW - 1], in0=src[:, 0:UW - 1], in1=src[:, 1:UW], op=ALU.add),
                 ['uT', 'pooled'], ['poolA'])
            if c == 0:
                emit_group(0, 0, 2, T1[0])
            S.op('dve', lambda e: e.tensor_tensor(out=T1[1][:, 0:UW - 3], in0=T1[0][:, 0:UW - 3], in1=T1[0][:, 2:UW - 1], op=ALU.add),
                 ['poolA'], ['poolA'])
            if c == 0:
                emit_group(0, 1, 4, T1[1])
            else:
                S.op('dve', lambda e: e.tensor_tensor(out=T1[0][:, 0:UW - 7], in0=T1[1][:, 0:UW - 7], in1=T1[1][:, 4:UW - 3], op=ALU.add),
                     ['poolA'], ['poolA'])
                emit_group(1, 0, 8, T1[0])
                S.op('dve', lambda e: e.tensor_tensor(out=T1[1][:, 0:UW - 15], in0=T1[0][:, 0:UW - 15], in1=T1[0][:, 8:UW - 7], op=ALU.add),
                     ['poolA'], ['poolA'])
                emit_group(1, 1, 16, T1[1])
        for c in range(2):
            for t0 in range(0, TOK, 512):
                n = min(512, TOK - t0)
                bank = nxt('p2bank', 2)
                S.op('pe', lambda e, c=c, t0=t0, n=n, bank=bank: e.matmul(
                    PS(bank)[:, 0:n], lhsT=BD[:, c, :], rhs=pooled[:, c, t0:t0 + n], start=True, stop=True),
                    ['bd', 'pooled'], [('ps', bank)])
                S.op('act', lambda e, c=c, t0=t0, n=n, bank=bank: e.activation(
                    out=catT[:, 6 + c, t0:t0 + n], in_=PS(bank)[:, 0:n], func=AF.Identity, scale=VEC[:, 112 + c:113 + c]),
                    [('ps', bank), 'vec'], [('cat', 6 + c)])

    def phase_conv():
        DIAG = RA[:, 4160:4160 + 62 * 128].rearrange("p (j m) -> p j m", m=128)
        sts = RAF[:, 6048:6048 + 2048]
        MEAN, VAR, TMP = sts[:, 0:512], sts[:, 512:1024], sts[:, 1024:1536]
        Y = STG[:, 0:1024].rearrange("p (c t) -> p c t", c=2)
        YSQ = STG[:, 1024:2048].rearrange("p (c t) -> p c t", c=2)
        for j in range(31):
            for c in range(2):
                S.op('dve', lambda e, j=j, c=c: e.tensor_scalar(
                    out=DIAG[:, 2 * j + c, :], in0=IDB[:, :], scalar1=VEC[:, 125 + 2 * j + c:126 + 2 * j + c],
                    scalar2=None, op0=ALU.mult), ['idb', 'vec'], ['diag'])
        for b in range(4):
            for c in range(2):
                bank = c
                for j in range(31):
                    S.op('pe', lambda e, j=j, c=c, b=b, bank=bank: e.matmul(
                        PS(bank), lhsT=DIAG[:, 2 * j + c, :], rhs=zbf[:, c, b * 512 + j:b * 512 + j + 512],
                        start=(j == 0), stop=(j == 30)), ['diag', 'zbf'], [('ps', bank)])
                S.op('act', lambda e, c=c, bank=bank: e.activation(
                    out=Y[:, c, :], in_=PS(bank), func=AF.Identity, bias=VEC[:, 114 + c:115 + c]),
                    [('ps', bank), 'vec'], [('cy', c)])
                S.op('dve', lambda e, c=c: e.tensor_tensor(out=YSQ[:, c, :], in0=Y[:, c, :], in1=Y[:, c, :], op=ALU.mult),
                     [('cy', c)], [('cysq', c)])
            for c in range(2):
                S.op('pe', lambda e, c=c: e.matmul(PS(2), lhsT=ONF[:, :], rhs=Y[:, c, :], start=(c == 0), stop=(c == 1)),
                     ['onf', ('cy', c)], [('ps', 2)])
            for c in range(2):
                S.op('pe', lambda e, c=c: e.matmul(PS(3), lhsT=ONF[:, :], rhs=YSQ[:, c, :], start=(c == 0), stop=(c == 1)),
                     ['onf', ('cysq', c)], [('ps', 3)])
            S.op('dve', lambda e: e.tensor_scalar(out=MEAN, in0=PS(2), scalar1=1.0 / 256, scalar2=None, op0=ALU.mult),
                 [('ps', 2)], ['cmean'])
            S.op('dve', lambda e: e.tensor_tensor(out=TMP, in0=MEAN, in1=MEAN, op=ALU.mult), ['cmean'], ['ctmp'])
            S.op('dve', lambda e: e.scalar_tensor_tensor(out=VAR, in0=PS(3), scalar=1.0 / 256, in1=TMP,
                                                         op0=ALU.mult, op1=ALU.subtract), [('ps', 3), 'ctmp'], ['cvar'])
            S.op('dve', lambda e: e.tensor_scalar(out=VAR, in0=VAR, scalar1=EPS, scalar2=None, op0=ALU.add), ['cvar'], ['cvar'])
            S.op('act', lambda e: e.activation(out=TMP, in_=VAR, func=AF.Ln), ['cvar'], ['ctmp'])
            S.op('act', lambda e: e.activation(out=VAR, in_=TMP, func=AF.Exp, scale=-0.5), ['ctmp'], ['cvar'])
            for c in range(2):
                S.op('dve', lambda e, c=c: e.tensor_tensor(out=Y[:, c, :], in0=Y[:, c, :], in1=MEAN, op=ALU.subtract),
                     [('cy', c), 'cmean'], [('cy', c)])
                S.op('dve', lambda e, c=c: e.tensor_tensor(out=Y[:, c, :], in0=Y[:, c, :], in1=VAR, op=ALU.mult),
                     [('cy', c), 'cvar'], [('cy', c)])
                S.op('act', lambda e, c=c, b=b: e.activation(
                    out=catT[:, 6 + c, 256 + b * 512:256 + (b + 1) * 512], in_=Y[:, c, :], func=AF.Silu,
                    bias=VEC[:, 118 + c:119 + c], scale=VEC[:, 116 + c:117 + c]),
                    [('cy', c), 'vec'], [('cat', 6 + c)])

    def att_views(l, slot):
        base = slot * 8192
        Q = RA[:, base:base + TOK]
        K = RA[:, base + TOK:base + 2 * TOK]
        V = RA[:, base + 4608:base + 4608 + 18 * 128].rearrange("p (c d) -> p c d", d=128)
        return Q, K, V, V

    def phase_att(l):
        if l == 0:
            pass
        else:
            S.op('dve', lambda e: e.tensor_tensor(out=LAM[:, 0:2].unsqueeze(2), in0=VEC[:, 121:125].rearrange("p (a b) -> p a b", b=2)[:, :, 0:1],
                                                  in1=VEC[:, 121:125].rearrange("p (a b) -> p a b", b=2)[:, :, 1:2], op=ALU.mult),
                 ['vec'], ['lam'])
            S.op('pe', lambda e: e.matmul(PS(0)[:, 0:2], lhsT=ONF[:, :], rhs=LAM[:, 0:2], start=True, stop=True),
                 ['onf', 'lam'], [('ps', 0)])
            S.op('act', lambda e: e.activation(out=LAM[:, 2:4], in_=PS(0)[:, 0:2], func=AF.Exp), [('ps', 0)], ['lam'])
            S.op('dve', lambda e: e.scalar_tensor_tensor(out=LAM[:, 4:5], in0=LAM[:, 3:4], scalar=-LAM_INIT1, in1=LAM[:, 2:3],
                                                         op0=ALU.add, op1=ALU.subtract), ['lam'], ['lam'])
            S.op('dve', lambda e: e.tensor_scalar(out=LAM[:, 5:6], in0=VEC[:, 120:121], scalar1=1.0 - LAM_INIT1, scalar2=None,
                                                  op0=ALU.mult), ['vec'], ['lam'])
        nunits = 6

        def load_unit(u):
            slot = u % 2
            Q, K, vA, vB = att_views(l, slot)
            rk = [('qkT', l, t, r) for t in range(NT) for r in ()]
            wk = [('att', slot)]
            deps = [k for k in S.lastw.keys() if isinstance(k, tuple) and k[0] in ('qkT', 'vs') and k[1] == l]
            sk = f'att{slot}'
            if l == 0:
                hA, hB = 2 * u, 2 * u + 1
                gA, gB = hA // 3, hB // 3
                for half, h, g in ((0, hA, gA), (1, hB, gB)):
                    S.dma('sp', lambda e, half=half, h=h: e.dma_start(out=Q[half * 64:(half + 1) * 64, :],
                                                                     in_=qkT_s[0][h * 64:(h + 1) * 64, :]), deps, wk, sk)
                    S.dma('sp', lambda e, half=half, g=g: e.dma_start(out=K[half * 64:(half + 1) * 64, :],
                                                                     in_=qkT_s[0][768 + g * 64:768 + (g + 1) * 64, :]), deps, wk, sk)
                S.dma('sp', lambda e: e.dma_start(out=vA[:, :, 0:64],
                                                  in_=v_s[0][:, gA * 64:(gA + 1) * 64].rearrange("(c p) d -> p c d", p=128)), deps, wk, sk)
                S.dma('sp', lambda e: e.dma_start(out=vA[:, :, 64:128],
                                                  in_=v_s[0][:, gB * 64:(gB + 1) * 64].rearrange("(c p) d -> p c d", p=128)), deps, wk, sk)
            else:
                h = u
                S.dma('sp', lambda e: e.dma_start(out=Q[:, :], in_=qkT_s[1][h * 128:(h + 1) * 128, :]), deps, wk, sk)
                S.dma('sp', lambda e: e.dma_start(out=K[:, :], in_=qkT_s[1][768 + h * 128:768 + (h + 1) * 128, :]), deps, wk, sk)
                S.dma('sp', lambda e: e.dma_start(out=vA[:, :, :],
                                                  in_=v_s[1][:, h * 128:(h + 1) * 128].rearrange("(c p) d -> p c d", p=128)), deps, wk, sk)

        qblocks = ([(0, 256, [0, 1])] if l == 0 else []) + [(256 + qb * 512, 512, list(range(18))) for qb in range(4)]
        load_unit(0)
        for u in range(nunits):
            if u + 1 < nunits:
                load_unit(u + 1)
            slot = u % 2
            Q, K, vA, vB = att_views(l, slot)
            ak = ('att', slot)
            for (q0_, n_, chunks_) in qblocks:
                pump()
                att_block(l, u, Q, K, vA, ak, q0_, n_, chunks_)

    def att_block(l, u, Q, K, vA, ak, q0, n, chunks):
            if True:
                def qk(c, ci):
                    for i in range(2):
                        bank = 2 * i + (ci % 2)
                        S.op('pe', lambda e, i=i, c=c, bank=bank: e.matmul(
                            PS(bank)[:, 0:n], lhsT=K[i * 64:(i + 1) * 64, c * 128:(c + 1) * 128],
                            rhs=Q[i * 64:(i + 1) * 64, q0:q0 + n], start=True, stop=True), [ak], [('ps', bank)])

                def ex(c, ci):
                    for i in range(2):
                        bank = 2 * i + (ci % 2)
                        ps_ = (2 * ci + i) % 4
                        S.op('act', lambda e, bank=bank, ps_=ps_: e.activation(
                            out=PT[:, ps_, 0:n], in_=PS(bank)[:, 0:n], func=AF.Exp, scale=SCALE),
                            [('ps', bank)], [('pt', ps_)])

                def pv(c, ci):
                    first, last = (ci == 0), (ci == len(chunks) - 1)
                    for i in range(2):
                        ps_ = (2 * ci + i) % 4
                        if False:
                            pass
                        else:
                            S.op('pe', lambda e, c=c, ps_=ps_, i=i: e.matmul(PS(4 + i)[:, 0:n], lhsT=vA[:, c, :], rhs=PT[:, ps_, 0:n],
                                                                             start=first, stop=last), [ak, ('pt', ps_)], [('ps', 4 + i)])
                            S.op('pe', lambda e, ps_=ps_, i=i: e.matmul(PS(6 + i)[:, 0:n], lhsT=ONB[:, :], rhs=PT[:, ps_, 0:n],
                                                                        start=first, stop=last), ['onb', ('pt', ps_)], [('ps', 6 + i)])

                nch = len(chunks)
                qk(chunks[0], 0)
                ex(chunks[0], 0)
                for ci in range(nch):
                    if ci + 1 < nch:
                        qk(chunks[ci + 1], ci + 1)
                        ex(chunks[ci + 1], ci + 1)
                    pv(chunks[ci], ci)
                if l == 0:
                    for i in range(2):
                        r0 = i * 64
                        RD = STG[:, i * 512:i * 512 + 512]
                        S.op('dve', lambda e, r0=r0, RD=RD, i=i: e.reciprocal(out=RD[r0:r0 + 64, 0:n], in_=PS(6 + i)[r0:r0 + 64, 0:n]),
                             [('ps', 6 + i)], [('rd', i)])
                        S.op('dve', lambda e, i=i, r0=r0, RD=RD: e.tensor_tensor(
                            out=catT[r0:r0 + 64, u, q0:q0 + n], in0=PS(4 + i)[r0:r0 + 64, 0:n], in1=RD[r0:r0 + 64, 0:n], op=ALU.mult),
                            [('ps', 4 + i), ('rd', i)], [('cat', u)])
                else:
                    T0, T1_ = STG[:, 0:512], STG[:, 512:1024]
                    SQB = TT[:, 0, :]
                    for i, Ti in ((0, T0), (1, T1_)):
                        S.op('dve', lambda e, i=i, Ti=Ti: e.reciprocal(out=Ti[:, 0:n], in_=PS(6 + i)[:, 0:n]), [('ps', 6 + i)], [('T', i)])
                        S.op('dve', lambda e, i=i, Ti=Ti: e.tensor_tensor(out=Ti[:, 0:n], in0=PS(4 + i)[:, 0:n], in1=Ti[:, 0:n], op=ALU.mult),
                             [('ps', 4 + i), ('T', i)], [('T', i)])
                    S.op('dve', lambda e: e.scalar_tensor_tensor(out=T0[:, 0:n], in0=T1_[:, 0:n], scalar=LAM[:, 4:5], in1=T0[:, 0:n],
                                                                 op0=ALU.mult, op1=ALU.add), [('T', 0), ('T', 1), 'lam'], [('T', 0)])
                    S.op('dve', lambda e: e.tensor_tensor(out=SQB[:, 0:n], in0=T0[:, 0:n], in1=T0[:, 0:n], op=ALU.mult),
                         [('T', 0)], [('tt', 0)])
                    S.op('pe', lambda e: e.matmul(PS(6)[:, 0:n], lhsT=ONB[:, :], rhs=SQB[:, 0:n], start=True, stop=True),
                         ['onb', ('tt', 0)], [('ps', 6)])
                    S.op('act', lambda e: e.activation(out=T1_[:, 0:n], in_=PS(6)[:, 0:n], func=AF.Ln, scale=1.0 / 128, bias=SM[:, 80:81]),
                         [('ps', 6), 'sm_eps'], [('T', 1)])
                    S.op('act', lambda e: e.activation(out=T1_[:, 0:n], in_=T1_[:, 0:n], func=AF.Exp, scale=-0.5), [('T', 1)], [('T', 1)])
                    S.op('dve', lambda e: e.scalar_tensor_tensor(out=catT[:, u, q0:q0 + n], in0=T0[:, 0:n], scalar=LAM[:, 5:6], in1=T1_[:, 0:n],
                                                                 op0=ALU.mult, op1=ALU.mult), [('T', 0), ('T', 1), 'lam'], [('cat', u)])

    def ln_update(t, Y, gbc, gkey, lng, lnb, lnkeys, ykeys):
        k = nxt('lnstg', 2)
        stg = STG[:, k * 1024:(k + 1) * 1024]
        sk = ('lnstg', k)
        so = 96 + k * 32
        stats = SM[:, so:so + 12]
        mv = SM[:, so + 12:so + 14]
        lnv = SM[:, so + 14:so + 15]
        rstd = SM[:, so + 15:so + 16]
        nb = SM[:, so + 16:so + 17]
        smk = ('lnsm', k)
        xk = ('x', t)
        S.op('dve', lambda e: e.tensor_tensor(out=stg, in0=Y, in1=gbc, op=ALU.mult), list(ykeys) + [gkey], [sk])
        S.op('dve', lambda e: e.scalar_tensor_tensor(out=X[:, t, :], in0=X[:, t, :], scalar=ALPHA, in1=stg,
                                                     op0=ALU.mult, op1=ALU.add), [xk, sk], [xk])
        for hh in range(2):
            S.op('dve', lambda e, hh=hh: e.bn_stats(out=stats[:, hh * 6:(hh + 1) * 6], in_=X[:, t, hh * 512:(hh + 1) * 512]),
                 [xk], [smk])
        S.op('dve', lambda e: e.bn_aggr(out=mv, in_=stats), [smk], [smk])
        S.op('act', lambda e: e.activation(out=lnv, in_=mv[:, 1:2], func=AF.Ln, bias=SM[:, 80:81]), [smk, 'sm_eps'], [smk])
        S.op('act', lambda e: e.activation(out=rstd, in_=lnv, func=AF.Exp, scale=-0.5), [smk], [smk])
        S.op('dve', lambda e: e.scalar_tensor_tensor(out=nb, in0=mv[:, 0:1], scalar=-1.0, in1=rstd, op0=ALU.mult, op1=ALU.mult),
             [smk], [smk])
        S.op('act', lambda e: e.activation(out=stg, in_=X[:, t, :], func=AF.Identity, bias=nb, scale=rstd), [xk, smk], [sk])
        S.op('dve', lambda e: e.tensor_tensor(out=stg, in0=stg, in1=lng, op=ALU.mult), [sk, lnkeys[0]], [sk])
        S.op('dve', lambda e: e.tensor_tensor(out=X[:, t, :], in0=stg, in1=lnb, op=ALU.add), [sk, lnkeys[1]], [xk])

    def phase_p4(l):
        need(f'wo{l}')
        S.dma('sp', lambda e: e.dma_start(out=WOUT[:, :, :], in_=wo_s[l].rearrange("(k p) c -> p k c", p=128)),
              [f'wo{l}'], ['wout'], 'wout')
        build_gate_bc(BCT[0], l, 2, 0, ('bct', 0))
        if l == 0:
            build_gate_bc(BCT[1], l, 2, 1, ('bct', 1))
        load_ln_bc(BCT[2], l * 4 + 0, ('bct', 2))
        load_ln_bc(BCT[3], l * 4 + 1, ('bct', 3))
        tiles = range(NT) if l == 0 else range(2, NT)
        for t in tiles:
            if t % 3 == 0:
                pump()
            p4_tile(l, t)

    def p4_tile(l, t):
        if True:
            pb = nxt('p4pair', 2)
            w = 1 if t < 2 else 0
            for nh in range(2):
                bank = 2 * pb + nh
                for kc in range(8):
                    S.op('pe', lambda e, kc=kc, nh=nh, bank=bank: e.matmul(
                        PS(bank), lhsT=catT[:, kc, t * 128:(t + 1) * 128], rhs=WOUT[:, kc, nh * 512:(nh + 1) * 512],
                        start=(kc == 0), stop=(kc == 7)), [('cat', kc), 'wout'], [('ps', bank)])
            ln_update(t, PSB[pb][:, :], BCT[w], ('bct', w), BCT[2], BCT[3], [('bct', 2), ('bct', 3)],
                      [('ps', 2 * pb), ('ps', 2 * pb + 1)])

    def phase_p5(l, last):
        need(f'fi{l}')
        need(f'fo{l}')
        S.dma('sp', lambda e: e.dma_start(out=WO[:, 0:11, :], in_=fo_s[l][0:1408, :].rearrange("(j p) c -> p j c", p=128)),
              [f'fo{l}'], ['wo'], 'wo')
        S.dma('sp', lambda e: e.dma_start(out=WO[:, 11:22, :], in_=fo_s[l][1408:2816, :].rearrange("(j p) c -> p j c", p=128)),
              [f'fo{l}'], ['wo'], 'wo')
        build_gate_bc(BCT[0], l, 5, 0, ('bct', 0))
        if l == 0:
            build_gate_bc(BCT[1], l, 5, 1, ('bct', 1))
        load_ln_bc(BCT[2], l * 4 + 2, ('bct', 2))
        load_ln_bc(BCT[3], l * 4 + 3, ('bct', 3))
        sbs = BLOCKS if l == 0 else BLOCKS[1:]
        for tiles, w in sbs:
            p5_sb(l, last, tiles, w)

    def p5_sb(l, last, tiles, w):
        if True:
            n = 128 * len(tiles)
            for kc in range(8):
                bank = 4 + nxt('p5tb', 4)
                for i, t in enumerate(tiles):
                    S.op('pe', lambda e, i=i, t=t, kc=kc, bank=bank: e.transpose(
                        out=PS(bank)[:, i * 128:(i + 1) * 128], in_=X[:, t, kc * 128:(kc + 1) * 128], identity=ident),
                        [('x', t), 'cst'], [('ps', bank)])
                S.op('act', lambda e, kc=kc, bank=bank: e.activation(
                    out=h2T[:, kc, 0:n], in_=PS(bank)[:, 0:n], func=AF.Identity, bias=modv(l, 3, kc, w), scale=modv(l, 4, kc, w)),
                    [('ps', bank), ('mod', l, 1)], [('h2T', kc)])
            for jp in range(11):
                if jp % 4 == 0:
                    pump()
                slot = nxt('fring', 2)
                sa, sg_ = FRING[slot]
                S.dma('sp', lambda e, jp=jp, sa=sa: e.dma_start(
                    out=sa[:, :, :], in_=fi_s[l][:, jp * 256:(jp + 1) * 256].rearrange("(k p) c -> p k c", p=128)),
                    [f'fi{l}'], [('fring', slot)], f'fring{slot}')
                S.dma('sp', lambda e, jp=jp, sg_=sg_: e.dma_start(
                    out=sg_[:, :, :], in_=fi_s[l][:, FFN_H + jp * 256:FFN_H + (jp + 1) * 256].rearrange("(k p) c -> p k c", p=128)),
                    [f'fi{l}'], [('fring', slot)], f'fring{slot}')
                for jj in range(2):
                    j = 2 * jp + jj
                    pr = nxt('p5ab', 2)
                    ba, bg = 2 * pr, 2 * pr + 1
                    for (bank, slab) in ((ba, sa), (bg, sg_)):
                        for kc in range(8):
                            S.op('pe', lambda e, kc=kc, bank=bank, slab=slab, jj=jj: e.matmul(
                                PS(bank)[:, 0:n], lhsT=slab[:, kc, jj * 128:(jj + 1) * 128], rhs=h2T[:, kc, 0:n],
                                start=(kc == 0), stop=(kc == 7)), [('fring', slot), ('h2T', kc)], [('ps', bank)])
                    sgs = nxt('sgs', 2)
                    S.op('act', lambda e, bg=bg, sgs=sgs: e.activation(out=SG[:, sgs, 0:n], in_=PS(bg)[:, 0:n], func=AF.Silu),
                         [('ps', bg)], [('sg', sgs)])
                    S.op('dve', lambda e, ba=ba, sgs=sgs, j=j: e.tensor_tensor(out=hidT[:, j, 0:n], in0=PS(ba)[:, 0:n],
                                                                              in1=SG[:, sgs, 0:n], op=ALU.mult),
                         [('ps', ba), ('sg', sgs)], [('hid', j)])
            for ti, t in enumerate(tiles):
                pb = 2 + nxt('p5pair', 2)
                for nh in range(2):
                    bank = 2 * pb + nh
                    for j in range(22):
                        S.op('pe', lambda e, j=j, nh=nh, bank=bank, ti=ti: e.matmul(
                            PS(bank), lhsT=hidT[:, j, ti * 128:(ti + 1) * 128], rhs=WO[:, j, nh * 512:(nh + 1) * 512],
                            start=(j == 0), stop=(j == 21)), [('hid', j), 'wo'], [('ps', bank)])
                ln_update(t, PSB[pb][:, :], BCT[w], ('bct', w), BCT[2], BCT[3], [('bct', 2), ('bct', 3)],
                          [('ps', 2 * pb), ('ps', 2 * pb + 1)])
                if last:
                    S.dma('sp', lambda e, t=t: e.dma_start(out=out_d[(t - 2) * 128:(t - 1) * 128, :], in_=X[:, t, :]),
                          [('x', t)], [('out', t)], 'outw', sbuf=False)

    S.op('dve', lambda e: e.memset(SM[:, 80:81], EPS), [], ['sm_eps'])

    def dump_all():
        S.barrier()
        for t in range(NT):
            S.dma('sp', lambda e, t=t: e.dma_start(out=dbg_d[t * 128:(t + 1) * 128, :], in_=X[:, t, :]),
                  [('x', t)], [('dbg', t)], 'dbgw', sbuf=False)
        S.dma('sp', lambda e: e.dma_start(out=dbg_hb, in_=HB[:, 0:18432]), [], ['dbg_hb'], 'dbgw', sbuf=False)
        S.dma('sp', lambda e: e.dma_start(out=dbg_mod, in_=MOD[:, :, :, :].rearrange("p l o w -> p (l o w)")), [], ['dbg_mod'], 'dbgw', sbuf=False)
        S.dma('sp', lambda e: e.dma_start(out=dbg_r3, in_=R3[:, :]), [], ['dbg_r3'], 'dbgw', sbuf=False)

    def run_all():
        for l in range(2):
            compute_mod(l, 0)
            S.barrier()
            if stop == f'mod{l}':
                return
            phase_p0(l)
            if stop == f'p0{l}':
                return
            if l == 0:
                S.op('dve', lambda e: e.memset(R3[:, :], 0.0), [], ['uT'])
            phase_p1(l)
            S.barrier()
            compute_mod(l, 1)
            if stop == f'p1{l}':
                return
            if l == 0:
                phase_pool()
            else:
                phase_conv()
            S.barrier()
            if stop == f'p2{l}':
                return
            phase_att(l)
            S.barrier()
            if stop == f'p3{l}':
                return
            phase_p4(l)
            S.barrier()
            if stop == f'p4{l}':
                return
            phase_p5(l, last=(l == 1))
            S.barrier()
            if stop == f'p5{l}':
                return
    run_all()
    if debug:
        dump_all()
    S.wait_all('sp', ['outw', 'dbgw'])

    with nc.Block() as block:
        S.emit(block)
    es.close()
    return nc


def _rope_tables():
    n = np.arange(2048)
    row = (n // 64).astype(np.float32)
    col = (n % 64).astype(np.float32)
    freqs = (np.float32(10000.0) ** (-np.arange(0, 32, 2, dtype=np.float32) / np.float32(32))).astype(np.float32)
    ang = np.concatenate([row[:, None] * freqs, col[:, None] * freqs], axis=-1).astype(np.float32)
    cos, sin = np.cos(ang).astype(np.float32), np.sin(ang).astype(np.float32)
    cr, cc, sr, sc = cos[:, :16], cos[:, 16:], sin[:, :16], sin[:, 16:]
    C = np.concatenate([cr, cr, cc, cc], axis=-1)
    Sg = np.concatenate([-sr, sr, -sc, sc], axis=-1)
    C = C.reshape(16, 128, 64).transpose(1, 0, 2)
    Sg = Sg.reshape(16, 128, 64).transpose(1, 0, 2)
    return np.ascontiguousarray(C), np.ascontiguousarray(Sg)


def _consts():
    cst = np.zeros((128, NCST), np.float32)
    cst[:, C_ID:C_ID + 128] = np.eye(128, dtype=np.float32)
    C, Sg = _rope_tables()
    cst[:, C_RC:C_RC + 1024] = C.reshape(128, 1024)
    cst[:, C_RS:C_RS + 1024] = Sg.reshape(128, 1024)
    pe = np.zeros((128, 2, 2, 8), np.float32)
    for p in range(128):
        for c in range(2):
            w = POOL_W[2 * c + p // 64]
            for i in range(8):
                pe[p, c, 0, i] = 1.0 / ((i + w // 2) - max(i - w // 2, 0))
                pe[p, c, 1, i] = 1.0 / min(w, 8 - i + w // 2)
    cst[:, C_PE:C_PE + 32] = pe.reshape(128, 32)
    return cst


def _colvec(v):
    v = np.asarray(v, np.float32).reshape(-1, 128)
    return v.T


_NC_CACHE = {}


def kernel(x, c, ctx, c_ctx, ab_w_in, ab_q_gain, ab_k_gain, ab_w_pool, ab_pool_scale, ab_w_out,
           cd_w_in, cd_lambda_q1, cd_lambda_k1, cd_lambda_q2, cd_lambda_k2, cd_subln_gain,
           cd_conv_w, cd_conv_b, cd_conv_ln_g, cd_conv_ln_b, cd_w_out,
           ada_w, ada_b, ln1_g, ln1_b, ln2_g, ln2_b, ffn_w_in, ffn_w_out, _debug=False, _stop=None):
    f = lambda a: np.ascontiguousarray(np.asarray(a, dtype=np.float32))
    x, c, ctx, c_ctx = f(x), f(c), f(ctx), f(c_ctx)
    if (_debug, _stop) not in _NC_CACHE:
        _NC_CACHE[(_debug, _stop)] = build_program(debug=_debug, stop=_stop)
    nc = _NC_CACHE[(_debug, _stop)]
    cst = _consts()
    lnbc = np.stack([f(ln1_g)[0], f(ln1_b)[0], f(ln2_g)[0], f(ln2_b)[0],
                     f(ln1_g)[1], f(ln1_b)[1], f(ln2_g)[1], f(ln2_b)[1]], axis=0)
    gains = np.stack([f(ab_q_gain)[0], f(ab_k_gain)[0]], axis=0)
    shared = {
        "consts": cst, "lnbc": np.ascontiguousarray(lnbc), "gains": np.ascontiguousarray(gains),
        "ab_w_in": f(ab_w_in)[0], "cd_w_in": f(cd_w_in)[0], "ab_w_out": f(ab_w_out)[0], "cd_w_out": f(cd_w_out)[0],
        "ab_w_pool": f(ab_w_pool)[0].reshape(256, 64), "ada_w": f(ada_w).reshape(2 * D, 6 * D),
        "ffn_w_in": f(ffn_w_in).reshape(2 * D, 2 * FFN_H), "ffn_w_out": f(ffn_w_out).reshape(2 * FFN_H, D),
    }
    in_maps = []
    for b in range(NCORES):
        vecs = np.zeros((128, NV), np.float32)
        vecs[:, 0:8] = _colvec(c[b])
        vecs[:, 8:16] = _colvec(c_ctx)
        vecs[:, 16:64] = _colvec(f(ada_b)[0])
        vecs[:, 64:112] = _colvec(f(ada_b)[1])
        vecs[:, 112:114] = _colvec(f(ab_pool_scale)[0])
        vecs[:, 114:116] = _colvec(f(cd_conv_b)[0])
        vecs[:, 116:118] = _colvec(f(cd_conv_ln_g)[0])
        vecs[:, 118:120] = _colvec(f(cd_conv_ln_b)[0])
        vecs[:, 120:121] = _colvec(f(cd_subln_gain)[0])
        vecs[0:64, 121] = f(cd_lambda_q1)[0]
        vecs[0:64, 122] = f(cd_lambda_k1)[0]
        vecs[0:64, 123] = f(cd_lambda_q2)[0]
        vecs[0:64, 124] = f(cd_lambda_k2)[0]
        cw = f(cd_conv_w)[0]
        for j in range(31):
            vecs[:, 125 + 2 * j:127 + 2 * j] = _colvec(cw[j])
        m = dict(shared)
        m["x"] = x[b]
        m["ctx"] = ctx[b]
        m["vecs"] = vecs
        in_maps.append(m)
    res = run_bass_kernel_spmd(nc, in_maps, core_ids=list(range(NCORES)))
    out = np.stack([np.asarray(r["out"], dtype=np.float32) for r in res.results], axis=0)
    if _debug:
        return out, res.results
    return out
```

```python
import math
from contextlib import ExitStack

import numpy as np
import concourse.bass as bass
import concourse.mybir as mybir
from concourse.bass_utils import run_bass_kernel_spmd

F32 = mybir.dt.float32
BF16 = mybir.dt.bfloat16
AF = mybir.ActivationFunctionType
ALU = mybir.AluOpType
AX = mybir.AxisListType

NCORES = 8
D = 1024
NT = 18
TOK = 2304
ALPHA = 4.0 ** 0.25
EPS = 1e-6
FFN_H = 2816
LAM_INIT1 = 0.8 - 0.6 * math.exp(-0.3)
POOL_W = (2, 4, 8, 16)
NV = 192
C_ID = 0
C_RC = 128
C_RS = C_RC + 16 * 64
C_PE = C_RS + 16 * 64
NCST = C_PE + 32
UW = 2336


class Sched:
    def __init__(self, nc, es):
        self.nc = nc
        self.es = es
        self.engs = ['pe', 'act', 'dve', 'pool', 'sp']
        self.prog = {e: [] for e in self.engs}
        self.cnt = {e: 0 for e in self.engs}
        self.seen = {e: {} for e in self.engs}
        self.lastw = {}
        self.readers = {}
        self.sems = {}
        self.dcum = {}
        self.sbuf_dma = set()
        for e in self.engs:
            self.sems[e] = es.enter_context(nc.semaphore("sem_" + e))

    def _deps(self, reads, writes):
        deps = {}
        raw = {}

        def add(dct, s, v):
            if dct.get(s, 0) < v:
                dct[s] = v
        for k in reads:
            t = self.lastw.get(k)
            if t is not None:
                add(deps, *t)
                add(raw, *t)
        for k in writes:
            t = self.lastw.get(k)
            if t is not None:
                add(deps, *t)
            for s, v in self.readers.get(k, {}).items():
                add(deps, s, v)
        self._raw = raw
        return deps

    def _commit(self, tok, reads, writes):
        s, v = tok
        for k in reads:
            r = self.readers.setdefault(k, {})
            if r.get(s, 0) < v:
                r[s] = v
        for k in writes:
            self.lastw[k] = tok
            self.readers[k] = {}

    def _waits(self, eng, deps):
        w = []
        for s, v in deps.items():
            if s == eng and eng in ('pe', 'sp'):
                continue
            if self.seen[eng].get(s, 0) >= v:
                continue
            self.seen[eng][s] = v
            w.append((s, v))
        return w

    def op(self, eng, fn, reads=(), writes=()):
        deps = self._deps(reads, writes)
        waits = self._waits(eng, deps)
        self.cnt[eng] += 1
        self.prog[eng].append((waits, fn, (eng, 1)))
        self._commit((eng, self.cnt[eng]), reads, writes)

    def dma(self, eng, fn, reads, writes, semkey, sbuf=True):
        if semkey not in self.sems:
            self.sems[semkey] = self.es.enter_context(self.nc.semaphore("d_" + semkey))
            self.dcum[semkey] = 0
        if sbuf:
            self.sbuf_dma.add(semkey)
        deps = self._deps(reads, writes)
        waits = self._waits(eng, deps)
        self.dcum[semkey] += 16
        self.prog[eng].append((waits, fn, (semkey, 16)))
        self._commit((semkey, self.dcum[semkey]), reads, writes)

    def barrier(self, engs=('pe', 'act', 'dve', 'sp', 'pool')):
        toks = {e: self.cnt[e] for e in ('pe', 'act', 'dve', 'pool') if self.cnt[e] > 0}
        for k in self.sbuf_dma:
            toks[k] = self.dcum[k]
        self._raw = dict(toks)
        for e in engs:
            w = self._waits(e, toks)
            if w:
                self.prog[e].append((w, None, None))

    def gate(self, eng, on):
        toks = {on: self.cnt[on]}
        self._raw = dict(toks)
        w = self._waits(eng, toks)
        if w:
            self.prog[eng].append((w, None, None))

    def wait_all(self, eng, semkeys):
        toks = {k: self.dcum[k] for k in semkeys if k in self.dcum}
        self._raw = dict(toks)
        w = self._waits(eng, toks)
        if w:
            self.prog[eng].append((w, None, None))

    def emit(self, block):
        emap = {'pe': block.tensor, 'act': block.scalar, 'dve': block.vector,
                'pool': block.gpsimd, 'sp': block.sync}
        for e in self.engs:
            prog = self.prog[e]

            def body(eng, prog=prog, ename=e):
                for waits, fn, inc in prog:
                    attach = None
                    if fn is not None and ename in ('act', 'dve') and waits:
                        attach = waits[-1]
                        waits = waits[:-1]
                    for s, v in waits:
                        eng.wait_ge(self.sems[s], v)
                    if fn is not None:
                        ins = fn(eng)
                        if attach is not None:
                            ins._wait_ge(self.sems[attach[0]], attach[1])
                        ins.then_inc(self.sems[inc[0]], inc[1])
            emap[e](body)


def build_program(debug=False, stop=None):
    nc = bass.Bass("TRN2", target_bir_lowering=False, dynamic_dma_scratch_size=4096)
    es = ExitStack()

    def din(name, shape, dt=F32):
        return nc.dram_tensor(name, list(shape), dt, kind="ExternalInput").ap()

    x_d = din("x", [2048, D])
    ctx_d = din("ctx", [256, D])
    vecs_d = din("vecs", [128, NV])
    cst_d = din("consts", [128, NCST])
    lnbc_d = din("lnbc", [8, D])
    gains_d = din("gains", [2, 64])
    wq_d = [din("ab_w_in", [D, 1536]), din("cd_w_in", [D, 2816])]
    wo_d = [din("ab_w_out", [D, D]), din("cd_w_out", [D, D])]
    wpool_d = din("ab_w_pool", [256, 64])
    ada_d = din("ada_w", [2 * D, 6 * D])
    fi_d = din("ffn_w_in", [2 * D, 2 * FFN_H])
    fo_d = din("ffn_w_out", [2 * FFN_H, D])
    out_d = nc.dram_tensor("out", [2048, D], F32, kind="ExternalOutput").ap()
    dbg_d = None
    if debug:
        dbg_d = nc.dram_tensor("dbg", [TOK, D], F32, kind="ExternalOutput").ap()

    def dscr(name, shape):
        if debug and name in ("qkT0_s", "v0_s", "qkT1_s", "v1_s"):
            return nc.dram_tensor(name, list(shape), BF16, kind="ExternalOutput").ap()
        return nc.dram_tensor(name, list(shape), BF16).ap()

    if debug:
        dbg_hb = nc.dram_tensor("dbg_hb", [128, 18432], BF16, kind="ExternalOutput").ap()
        dbg_mod = nc.dram_tensor("dbg_mod", [128, 192], F32, kind="ExternalOutput").ap()
        dbg_r3 = nc.dram_tensor("dbg_r3", [128, 4672], F32, kind="ExternalOutput").ap()
        dbg_att = nc.dram_tensor("dbg_att", [128, 1024], F32, kind="ExternalOutput").ap()
        dbg_stg = nc.dram_tensor("dbg_stg", [128, 2048], F32, kind="ExternalOutput").ap()
        dbg_pt = nc.dram_tensor("dbg_pt", [128, 2048], BF16, kind="ExternalOutput").ap()

    wq_s = [dscr("wq0_s", [D, 1536]), dscr("wq1_s", [D, 2816])]
    wo_s = [dscr("wo0_s", [D, D]), dscr("wo1_s", [D, D])]
    ada_s = [dscr("ada0_s", [D, 6 * D]), dscr("ada1_s", [D, 6 * D])]
    fi_s = [dscr("fi0_s", [D, 2 * FFN_H]), dscr("fi1_s", [D, 2 * FFN_H])]
    fo_s = [dscr("fo0_s", [FFN_H, D]), dscr("fo1_s", [FFN_H, D])]
    NQK = [1024, 1536]
    VW = [256, 768]
    qkT_s = [dscr("qkT0_s", [NQK[0], TOK]), dscr("qkT1_s", [NQK[1], TOK])]
    v_s = [dscr("v0_s", [TOK, VW[0]]), dscr("v1_s", [TOK, VW[1]])]

    def sb(name, shape, dt):
        return es.enter_context(nc.sbuf_tensor(name, list(shape), dt))

    X = sb("X", [128, NT, D], F32)
    HB = sb("HB", [128, 30720], BF16)
    RA = sb("RA", [128, 16384], BF16)
    R3 = sb("R3", [128, 4672], F32)
    CST = sb("CST", [128, NCST], F32)
    VEC = sb("VEC", [128, NV], F32)
    MOD = sb("MOD", [128, 2, 48, 2], F32)
    SILC = sb("SILC", [128, 8, 2], BF16)
    IDB = sb("IDB", [128, 128], BF16)
    ONB = sb("ONB", [128, 128], BF16)
    ONF = sb("ONF", [128, 128], F32)
    STG = sb("STG", [128, 2048], F32)
    TT = sb("TT", [128, 2, 512], BF16)
    VST = sb("VST", [128, 2, 512], BF16)
    PT = sb("PT", [128, 4, 512], BF16)
    SG = sb("SG", [128, 2, 512], F32)
    GBC = sb("GBC", [128, 2, 64], F32)
    BD = sb("BD", [128, 2, 128], BF16)
    DG = sb("DG", [128, 2, 128], F32)
    SM = sb("SM", [128, 256], F32)
    LAM = sb("LAM", [128, 8], F32)

    PSB = [es.enter_context(nc.psum_tensor(f"psb{i}", [128, 1024], F32)) for i in range(4)]

    def PS(i):
        return PSB[i // 2][:, (i % 2) * 512:(i % 2) * 512 + 512]

    ident = CST[:, C_ID:C_ID + 128]
    ropeC = CST[:, C_RC:C_RC + 1024].rearrange("p (t d) -> p t d", d=64)
    ropeS = CST[:, C_RS:C_RS + 1024].rearrange("p (t d) -> p t d", d=64)
    pedge = CST[:, C_PE:C_PE + 32].rearrange("p (c s i) -> p c s i", c=2, s=2)

    hT_all = HB[:, 0:8 * TOK].rearrange("p (k t) -> p k t", k=8)
    catT = hT_all
    WSLAB = [HB[:, 18432 + s * 4096: 18432 + (s + 1) * 4096].rearrange("p (k c) -> p k c", k=8) for s in range(2)]
    WOUT = HB[:, 18432:18432 + 8192].rearrange("p (k c) -> p k c", k=8)
    WO = HB[:, 0:22528].rearrange("p (j c) -> p j c", j=22)
    FRING = [(HB[:, 22528 + s * 4096: 22528 + s * 4096 + 2048].rearrange("p (k c) -> p k c", k=8),
              HB[:, 22528 + s * 4096 + 2048: 22528 + (s + 1) * 4096].rearrange("p (k c) -> p k c", k=8))
             for s in range(2)]
    hidT = RA[:, 0:11264].rearrange("p (j t) -> p j t", j=22)
    h2T = RA[:, 11264:15360].rearrange("p (k t) -> p k t", k=8)
    BCT = [R3[:, i * 1024:(i + 1) * 1024] for i in range(4)]
    uT = R3[:, 0:2 * UW].rearrange("p (c w) -> p c w", c=2)
    aT = R3[:, 0:4096].rearrange("p (c t) -> p c t", c=2)
    RAF = RA[:, :].bitcast(F32)

    S = Sched(nc, es)
    rot = {}

    def nxt(name, n):
        v = rot.get(name, 0)
        rot[name] = (v + 1) % n
        return v

    SCALE = 0.125

    def cast2d(dst, src, key):
        r, c = src.shape
        if c > 2048:
            if c % 2048 == 0:
                d2 = dst.rearrange("a (b c) -> a b c", c=2048)
                s2 = src.rearrange("a (b c) -> a b c", c=2048)
            else:
                d2 = dst.rearrange("a b -> (a b)").rearrange("(n c) -> n c", c=2048)
                s2 = src.rearrange("a b -> (a b)").rearrange("(n c) -> n c", c=2048)
        else:
            d2, s2 = dst, src
        S.dma('pool', lambda e: e.dma_start(out=d2, in_=s2), [], [key], semkey="c_" + key, sbuf=False)

    CASTQ = []

    def qcast(dst, src, key, rows_per):
        r = src.shape[0]
        for r0 in range(0, r, rows_per):
            r1 = min(r, r0 + rows_per)
            CASTQ.append((key, lambda dst=dst, src=src, r0=r0, r1=r1, key=key: cast2d(dst[r0:r1, :], src[r0:r1, :], key)))

    def pump(n=1):
        for _ in range(n):
            if not CASTQ:
                return
            S.gate('pool', 'pe')
            CASTQ.pop(0)[1]()

    def need(key):
        while any(k == key for k, _ in CASTQ):
            CASTQ.pop(0)[1]()

    cast2d(ada_s[0][:, 0:2048], ada_d[0:D, 0:2048], "ada0a")
    cast2d(wq_s[0], wq_d[0], "wq0")
    cast2d(ada_s[0][:, 2048:6144], ada_d[0:D, 2048:6144], "ada0b")
    qcast(wo_s[0], wo_d[0], "wo0", 1024)
    qcast(fi_s[0], fi_d[0:D, :], "fi0", 128)
    qcast(fo_s[0], fo_d[0:FFN_H, :], "fo0", 704)
    qcast(ada_s[1], ada_d[D:2 * D, :], "ada1", 128)
    qcast(wq_s[1], wq_d[1], "wq1", 256)
    qcast(wo_s[1], wo_d[1], "wo1", 1024)
    qcast(fi_s[1], fi_d[D:2 * D, :], "fi1", 128)
    qcast(fo_s[1], fo_d[FFN_H:2 * FFN_H, :], "fo1", 704)

    S.dma('sp', lambda e: e.dma_start(out=CST[:, :], in_=cst_d), [], ['cst'], 'l_cst')
    S.dma('sp', lambda e: e.dma_start(out=VEC[:, :], in_=vecs_d), [], ['vec'], 'l_vec')
    S.dma('sp', lambda e: e.dma_start(out=GBC[:, :, :], in_=gains_d.partition_broadcast(128)), [], ['gbc'], 'l_gbc')
    S.dma('sp', lambda e: e.dma_start(out=X[:, 0:2, :], in_=ctx_d.rearrange("(t p) d -> p t d", p=128)),
          [], [('x', 0), ('x', 1)], 'l_ctx')
    for i in range(4):
        S.dma('sp', lambda e, i=i: e.dma_start(out=X[:, 2 + 4 * i:6 + 4 * i, :],
                                               in_=x_d[i * 512:(i + 1) * 512, :].rearrange("(t p) d -> p t d", p=128)),
              [], [('x', 2 + 4 * i + j) for j in range(4)], f'l_x{i}')

    S.op('dve', lambda e: e.memset(BD[:, :, :], 0.0), [], ['bd'])
    for c in range(2):
        for hh in range(2):
            g = 2 * c + hh
            S.dma('pool', lambda e, c=c, hh=hh, g=g: e.dma_start(out=BD[hh * 64:(hh + 1) * 64, c, hh * 64:(hh + 1) * 64],
                                                                 in_=wpool_d[g * 64:(g + 1) * 64, :]),
                  [], ['bd'], 'l_bd')
    S.op('dve', lambda e: e.memset(ONB[:, :], 1.0), [], ['onb'])
    S.op('dve', lambda e: e.memset(ONF[:, :], 1.0), [], ['onf'])
    S.op('dve', lambda e: e.tensor_copy(out=IDB[:, :], in_=ident), ['cst'], ['idb'])
    S.op('act', lambda e: e.activation(out=SILC[:, :, :].rearrange("p k w -> p w k"),
                                       in_=VEC[:, 0:16].rearrange("p (w k) -> p w k", w=2), func=AF.Silu),
         ['vec'], ['silc'])

    def compute_mod(l, part):
        if l == 0:
            keys_a = ['ada0a'] if part == 0 else ['ada0b']
        else:
            need('ada1')
            keys_a = ['ada1']
        srange = range(0, 4) if part == 0 else range(4, 12)
        oc0, oc1 = (0, 16) if part == 0 else (16, 48)
        for s_ in srange:
            slot = nxt('wslab', 2)
            S.dma('sp', lambda e, s_=s_, slot=slot: e.dma_start(
                out=WSLAB[slot][:, :, :], in_=ada_s[l][:, s_ * 512:(s_ + 1) * 512].rearrange("(k p) c -> p k c", p=128)),
                keys_a, [('wslab', slot)], f'wslab{slot}')
            for o4 in range(4):
                oc = s_ * 4 + o4
                for kc in range(8):
                    S.op('pe', lambda e, slot=slot, o4=o4, kc=kc, oc=oc: e.matmul(
                        PS(0)[:, oc * 2:oc * 2 + 2], lhsT=WSLAB[slot][:, kc, o4 * 128:(o4 + 1) * 128],
                        rhs=SILC[:, kc, :], start=(kc == 0), stop=(kc == 7)),
                        [('wslab', slot), 'silc'], [('ps', 0)])
        mps = PS(0)[:, 0:96].rearrange("p (o w) -> p o w", w=2)
        mk = ('mod', l, part)
        for w in range(2):
            S.op('dve', lambda e, w=w: e.tensor_tensor(out=MOD[:, l, oc0:oc1, w], in0=mps[:, oc0:oc1, w],
                                                       in1=VEC[:, 16 + 48 * l + oc0:16 + 48 * l + oc1], op=ALU.add),
                 [('ps', 0), 'vec'], [mk])
        s0 = 8 if part == 0 else 32
        S.op('dve', lambda e: e.tensor_scalar(out=MOD[:, l, s0:s0 + 8, :], in0=MOD[:, l, s0:s0 + 8, :],
                                              scalar1=1.0, scalar2=None, op0=ALU.add), [mk], [mk])

    def modv(l, s_, kc, w):
        return MOD[:, l, s_ * 8 + kc, w:w + 1]

    def build_gate_bc(dst, l, s_, w, key):
        for half in range(2):
            bank = nxt('bcbank', 2)
            for k4 in range(4):
                kc = half * 4 + k4
                dslot = nxt('dg', 2)
                S.op('dve', lambda e, kc=kc, dslot=dslot: e.tensor_scalar(
                    out=DG[:, dslot, :], in0=ident, scalar1=modv(l, s_, kc, w), scalar2=None, op0=ALU.mult),
                    ['cst', ('mod', l, 1)], [('dg', dslot)])
                S.op('pe', lambda e, k4=k4, dslot=dslot, bank=bank: e.matmul(
                    PS(bank)[:, k4 * 128:(k4 + 1) * 128], lhsT=ONF[:, :], rhs=DG[:, dslot, :], start=True, stop=True),
                    ['onf', ('dg', dslot)], [('ps', bank)])
            S.op('act', lambda e, half=half, bank=bank: e.activation(
                out=dst[:, half * 512:(half + 1) * 512], in_=PS(bank), func=AF.Copy),
                [('ps', bank)], [key])

    def load_ln_bc(dst, row, key):
        S.dma('sp', lambda e: e.dma_start(out=dst.unsqueeze(1), in_=lnbc_d[row:row + 1, :].partition_broadcast(128)),
              [], [key], 'l_' + str(key[1]))

    BLOCKS = [([0, 1], 1)] + [([2 + 4 * i + j for j in range(4)], 0) for i in range(4)]

    def phase_p0(l):
        for tiles, w in BLOCKS:
            n = 128 * len(tiles)
            for kc in range(8):
                bank = nxt('p0bank', 4)
                for i, t in enumerate(tiles):
                    S.op('pe', lambda e, i=i, t=t, kc=kc, bank=bank: e.transpose(
                        out=PS(bank)[:, i * 128:(i + 1) * 128], in_=X[:, t, kc * 128:(kc + 1) * 128], identity=ident),
                        [('x', t), 'cst'], [('ps', bank)])
                S.op('act', lambda e, kc=kc, bank=bank, n=n, t0=tiles[0], w=w: e.activation(
                    out=hT_all[:, kc, t0 * 128:t0 * 128 + n], in_=PS(bank)[:, 0:n], func=AF.Identity,
                    bias=modv(l, 0, kc, w), scale=modv(l, 1, kc, w)),
                    [('ps', bank), ('mod', l, 0)], [('hT', tiles[0], kc)])

    PTF = PT[:, :, :].rearrange("p a b -> p (a b)").bitcast(F32)
    D_ = PTF[:, 0:256]
    SGF = SG[:, :, :].rearrange("p a b -> p (a b)")
    STSETS = [(STG[:, 0:512], STG[:, 512:1024], STG[:, 1024:1536], 0),
              (STG[:, 1536:2048], SGF[:, 0:512], SGF[:, 512:1024], 160)]

    def hv(ap, n):
        return ap.rearrange("p (h d) -> p h d", d=64)

    def post_qk(l, t, bank, cs, n, row0, gain):
        si_ = nxt('stset', 2)
        A_, B_, C_, so_ = STSETS[si_]
        kA, kB, kC, kS = f'stgA{si_}', f'stgB{si_}', f'stgC{si_}', f'sm{si_}'
        src = PS(bank)[:, cs:cs + n]
        NH = n // 64
        pk = ('ps', bank)
        if l == 0:
            S.op('act', lambda e: e.activation(out=C_[:, 0:n], in_=src, func=AF.Square), [pk], [kC])
            S.op('dve', lambda e: e.tensor_reduce(out=SM[:, so_:so_ + NH], in_=hv(C_[:, 0:n], n), axis=AX.X, op=ALU.add),
                 [kC], [kS])
            S.op('dve', lambda e: e.tensor_scalar(out=SM[:, so_:so_ + NH], in0=SM[:, so_:so_ + NH], scalar1=1.0 / 64, scalar2=EPS,
                                                  op0=ALU.mult, op1=ALU.add), [kS], [kS])
            S.op('act', lambda e: e.activation(out=SM[:, so_ + 16:so_ + 16 + NH], in_=SM[:, so_:so_ + NH], func=AF.Ln), [kS], [kS])
            S.op('act', lambda e: e.activation(out=SM[:, so_ + 32:so_ + 32 + NH], in_=SM[:, so_ + 16:so_ + 16 + NH], func=AF.Exp, scale=-0.5),
                 [kS], [kS])
            S.op('dve', lambda e: e.tensor_tensor(out=hv(A_[:, 0:n], n), in0=hv(src, n),
                                                  in1=SM[:, so_ + 32:so_ + 32 + NH].unsqueeze(2).to_broadcast([128, NH, 64]), op=ALU.mult),
                 [pk, kS], [kA])
            S.op('dve', lambda e: e.tensor_tensor(out=hv(A_[:, 0:n], n), in0=hv(A_[:, 0:n], n),
                                                  in1=GBC[:, gain, :].unsqueeze(1).to_broadcast([128, NH, 64]), op=ALU.mult),
                 [kA, 'gbc'], [kA])
            xin, xk = A_[:, 0:n], kA
        else:
            xin, xk = src, pk
        if t < 2:
            if l == 0:
                res, rk = A_[:, 0:n], kA
            else:
                S.op('act', lambda e: e.activation(out=B_[:, 0:n], in_=src, func=AF.Copy), [pk], [kB])
                res, rk = B_[:, 0:n], kB
        else:
            tt = t - 2
            S.op('dve', lambda e: e.tensor_tensor(out=hv(B_[:, 0:n], n), in0=hv(xin, n),
                                                  in1=ropeC[:, tt, :].unsqueeze(1).to_broadcast([128, NH, 64]), op=ALU.mult),
                 [xk, 'cst'], [kB])

            def v4(ap):
                return ap.rearrange("p (h a two s) -> p h a two s", a=2, two=2, s=16)
            sv = ropeS[:, tt, :].rearrange("p (a two s) -> p a two s", a=2, two=2)
            for two in range(2):
                S.op('dve', lambda e, two=two: e.tensor_tensor(
                    out=v4(C_[:, 0:n])[:, :, :, two, :], in0=v4(xin)[:, :, :, 1 - two, :],
                    in1=sv[:, :, two, :].unsqueeze(1).to_broadcast([128, NH, 2, 16]), op=ALU.mult),
                    [xk, 'cst'], [kC])
            S.op('dve', lambda e: e.tensor_tensor(out=B_[:, 0:n], in0=B_[:, 0:n], in1=C_[:, 0:n], op=ALU.add),
                 [kB, kC], [kB])
            res, rk = B_[:, 0:n], kB
        tb = 3 + nxt('p1tb', 2)
        for i in range(n // 128):
            S.op('pe', lambda e, i=i, tb=tb: e.transpose(out=PS(tb)[:, i * 128:(i + 1) * 128],
                                                         in_=res[:, i * 128:(i + 1) * 128], identity=ident),
                 [rk, 'cst'], [('ps', tb)])
        ts = nxt('tt', 2)
        S.op('act', lambda e, tb=tb, ts=ts: e.activation(out=TT[:, ts, 0:n], in_=PS(tb)[:, 0:n], func=AF.Copy),
             [('ps', tb)], [('tt', ts)])
        S.dma('sp', lambda e, ts=ts: e.dma_start(
            out=qkT_s[l][row0:row0 + n, t * 128:(t + 1) * 128].rearrange("(i p) c -> p i c", p=128),
            in_=TT[:, ts, 0:n].rearrange("p (i c) -> p i c", c=128)),
            [('tt', ts)], [('qkT', l, t, row0)], f'ttw{ts}', sbuf=False)

    def post_v(l, t, bank, cs, n, vc0):
        vs = nxt('vst', 2)
        S.op('act', lambda e: e.activation(out=VST[:, vs, 0:n], in_=PS(bank)[:, cs:cs + n], func=AF.Copy),
             [('ps', bank)], [('vst', vs)])
        S.dma('sp', lambda e: e.dma_start(out=v_s[l][t * 128:(t + 1) * 128, vc0:vc0 + n], in_=VST[:, vs, 0:n]),
              [('vst', vs)], [('vs', l, t, vc0)], f'vstw{vs}', sbuf=False)

    def post_u_T(t, bank, cs):
        S.op('act', lambda e: e.activation(out=D_[:, 0:256], in_=PS(bank)[:, cs:cs + 256], func=AF.Copy),
             [('ps', bank)], ['stgD'])
        tb = 3 + nxt('p1tb', 2)
        for c in range(2):
            S.op('pe', lambda e, c=c, tb=tb: e.transpose(out=PS(tb)[:, c * 128:(c + 1) * 128],
                                                         in_=D_[:, c * 128:(c + 1) * 128], identity=ident),
                 ['stgD', 'cst'], [('ps', tb)])
        return tb

    def uoff(t):
        return 8 + t * 128 if t < 2 else 280 + (t - 2) * 128

    def post_u0(t, bank, cs):
        tb = post_u_T(t, bank, cs)
        S.op('act', lambda e: e.activation(out=uT[:, :, uoff(t):uoff(t) + 128],
                                           in_=PS(tb)[:, 0:256].rearrange("p (c t) -> p c t", c=2), func=AF.Copy),
             [('ps', tb)], ['uT'])

    def post_ua(t, bank, cs):
        tb = post_u_T(t, bank, cs)
        S.op('act', lambda e: e.activation(out=aT[:, :, (t - 2) * 128:(t - 1) * 128],
                                           in_=PS(tb)[:, 0:256].rearrange("p (c t) -> p c t", c=2), func=AF.Copy),
             [('ps', tb)], [('aT', t)])

    zbf = RA[:, 0:4160].rearrange("p (c w) -> p c w", c=2)

    def post_ug(t, bank, cs):
        tb = post_u_T(t, bank, cs)
        S.op('act', lambda e: e.activation(out=PTF[:, 256:512], in_=PS(tb)[:, 0:256], func=AF.Sigmoid),
             [('ps', tb)], ['sg0'])
        o = 15 + (t - 2) * 128
        S.op('dve', lambda e: e.tensor_tensor(out=zbf[:, :, o:o + 128], in0=aT[:, :, (t - 2) * 128:(t - 1) * 128],
                                              in1=PTF[:, 256:512].rearrange("p (c t) -> p c t", c=2), op=ALU.mult),
             [('aT', t), 'sg0'], ['zbf'])

    def segs_for(l, si):
        if l == 0:
            return [[('qk', 0, 512, (0, 0))],
                    [('qk', 0, 256, (512, 0)), ('qk', 256, 256, (768, 1))],
                    [('v', 0, 256, 0), ('u0', 256, 256, None)]][si]
        return [[('qk', 0, 512, (0, None))],
                [('qk', 0, 256, (512, None)), ('qk', 256, 256, (768, None))],
                [('qk', 0, 512, (1024, None))],
                [('v', 0, 512, 0)],
                [('v', 0, 256, 512), ('ua', 256, 256, None)],
                [('ug', 0, 256, None)]][si]

    def phase_p1(l):
        ncols = 1536 if l == 0 else 2816
        slabs = [(c0, min(512, ncols - c0)) for c0 in range(0, ncols, 512)]
        if l == 1:
            S.op('dve', lambda e: e.memset(zbf[:, :, :], 0.0), [], ['zbf'])
        need(f'wq{l}')
        for si, (c0, ncs) in enumerate(slabs):
            slot = nxt('wslab', 2)
            S.dma('sp', lambda e, c0=c0, ncs=ncs, slot=slot: e.dma_start(
                out=WSLAB[slot][:, :, 0:ncs], in_=wq_s[l][:, c0:c0 + ncs].rearrange("(k p) c -> p k c", p=128)),
                [f'wq{l}'], [('wslab', slot)], f'wslab{slot}')
            segs = segs_for(l, si)
            for t in range(NT):
                if t < 2 and all(k in ('ua', 'ug') for k, _, _, _ in segs):
                    continue
                bank = nxt('p1bank', 3)
                if t % 6 == 5:
                    pump()
                t0 = 0 if t < 2 else 2 + 4 * ((t - 2) // 4)
                for kc in range(8):
                    S.op('pe', lambda e, kc=kc, t=t, bank=bank, slot=slot, ncs=ncs: e.matmul(
                        PS(bank)[:, 0:ncs], lhsT=hT_all[:, kc, t * 128:(t + 1) * 128], rhs=WSLAB[slot][:, kc, 0:ncs],
                        start=(kc == 0), stop=(kc == 7)),
                        [('hT', t0, kc), ('wslab', slot)], [('ps', bank)])
                for kind, cs, n, ex in segs:
                    if kind == 'qk':
                        post_qk(l, t, bank, cs, n, ex[0], ex[1])
                    elif kind == 'v':
                        post_v(l, t, bank, cs, n, ex)
                    elif kind == 'u0':
                        post_u0(t, bank, cs)
                    elif kind == 'ua' and t >= 2:
                        post_ua(t, bank, cs)
                    elif kind == 'ug' and t >= 2:
                        post_ug(t, bank, cs)

    def phase_pool():
        T1 = [RAF[:, 0:UW], RAF[:, UW:2 * UW]]
        pooled = RA[:, 9472:9472 + 2 * TOK].rearrange("p (c t) -> p c t", c=2)
        segsP = [(8, 0, 256), (280, 256, 2048)]

        def emit_group(c, hh, w, Aw):
            P0_, P1_ = hh * 64, hh * 64 + 64
            hw = w // 2
            for (ps_, ts_, L) in segsP:
                S.op('dve', lambda e, ps_=ps_, ts_=ts_, L=L: e.scalar_tensor_tensor(
                    out=pooled[P0_:P1_, c, ts_:ts_ + L], in0=Aw[P0_:P1_, ps_ - hw:ps_ - hw + L], scalar=1.0 / w,
                    in1=uT[P0_:P1_, c, ps_:ps_ + L], op0=ALU.mult, op1=ALU.subtract),
                    ['uT', 'poolA'], ['pooled'])
                for side in range(2):
                    po = ps_ if side == 0 else ps_ + L - 8
                    to = ts_ if side == 0 else ts_ + L - 8
                    S.op('dve', lambda e, po=po, side=side: e.tensor_tensor(
                        out=SM[P0_:P1_, 64:72], in0=Aw[P0_:P1_, po - hw:po - hw + 8], in1=pedge[P0_:P1_, c, side, :], op=ALU.mult),
                        ['poolA', 'cst'], ['sm_pe'])
                    S.op('dve', lambda e, po=po, to=to: e.tensor_tensor(
                        out=pooled[P0_:P1_, c, to:to + 8], in0=SM[P0_:P1_, 64:72], in1=uT[P0_:P1_, c, po:po + 8], op=ALU.subtract),
                        ['sm_pe', 'uT'], ['pooled'])

        for c in range(2):
            src = uT[:, c, :]
            cur = 0
            S.op('dve', lambda e, src=src: e.tensor_tensor(out=T1[0][:, 0:UW - 1], in0=src[:, 0:UW - 1], in1=src[:, 1:UW], op=ALU.add),
                 ['uT', 'pooled'], ['poolA'])
            if c == 0:
                emit_group(0, 0, 2, T1[0])
            S.op('dve', lambda e: e.tensor_tensor(out=T1[1][:, 0:UW - 3], in0=T1[0][:, 0:UW - 3], in1=T1[0][:, 2:UW - 1], op=ALU.add),
                 ['poolA'], ['poolA'])
            if c == 0:
                emit_group(0, 1, 4, T1[1])
            else:
                S.op('dve', lambda e: e.tensor_tensor(out=T1[0][:, 0:UW - 7], in0=T1[1][:, 0:UW - 7], in1=T1[1][:, 4:UW - 3], op=ALU.add),
                     ['poolA'], ['poolA'])
                emit_group(1, 0, 8, T1[0])
                S.op('dve', lambda e: e.tensor_tensor(out=T1[1][:, 0:UW - 15], in0=T1[0][:, 0:UW - 15], in1=T1[0][:, 8:UW - 7], op=ALU.add),
                     ['poolA'], ['poolA'])
                emit_group(1, 1, 16, T1[1])
        for c in range(2):
            for t0 in range(0, TOK, 512):
                n = min(512, TOK - t0)
                bank = nxt('p2bank', 2)
                S.op('pe', lambda e, c=c, t0=t0, n=n, bank=bank: e.matmul(
                    PS(bank)[:, 0:n], lhsT=BD[:, c, :], rhs=pooled[:, c, t0:t0 + n], start=True, stop=True),
                    ['bd', 'pooled'], [('ps', bank)])
                S.op('act', lambda e, c=c, t0=t0, n=n, bank=bank: e.activation(
                    out=catT[:, 6 + c, t0:t0 + n], in_=PS(bank)[:, 0:n], func=AF.Identity, scale=VEC[:, 112 + c:113 + c]),
                    [('ps', bank), 'vec'], [('cat', 6 + c)])

    def phase_conv():
        DIAG = RA[:, 4160:4160 + 62 * 128].rearrange("p (j m) -> p j m", m=128)
        sts = RAF[:, 6048:6048 + 2048]
        MEAN, VAR, TMP = sts[:, 0:512], sts[:, 512:1024], sts[:, 1024:1536]
        Y = STG[:, 0:1024].rearrange("p (c t) -> p c t", c=2)
        YSQ = STG[:, 1024:2048].rearrange("p (c t) -> p c t", c=2)
        for j in range(31):
            for c in range(2):
                S.op('dve', lambda e, j=j, c=c: e.tensor_scalar(
                    out=DIAG[:, 2 * j + c, :], in0=IDB[:, :], scalar1=VEC[:, 125 + 2 * j + c:126 + 2 * j + c],
                    scalar2=None, op0=ALU.mult), ['idb', 'vec'], ['diag'])
        for b in range(4):
            for c in range(2):
                bank = c
                for j in range(31):
                    S.op('pe', lambda e, j=j, c=c, b=b, bank=bank: e.matmul(
                        PS(bank), lhsT=DIAG[:, 2 * j + c, :], rhs=zbf[:, c, b * 512 + j:b * 512 + j + 512],
                        start=(j == 0), stop=(j == 30)), ['diag', 'zbf'], [('ps', bank)])
                S.op('act', lambda e, c=c, bank=bank: e.activation(
                    out=Y[:, c, :], in_=PS(bank), func=AF.Identity, bias=VEC[:, 114 + c:115 + c]),
                    [('ps', bank), 'vec'], [('cy', c)])
                S.op('dve', lambda e, c=c: e.tensor_tensor(out=YSQ[:, c, :], in0=Y[:, c, :], in1=Y[:, c, :], op=ALU.mult),
                     [('cy', c)], [('cysq', c)])
            for c in range(2):
                S.op('pe', lambda e, c=c: e.matmul(PS(2), lhsT=ONF[:, :], rhs=Y[:, c, :], start=(c == 0), stop=(c == 1)),
                     ['onf', ('cy', c)], [('ps', 2)])
            for c in range(2):
                S.op('pe', lambda e, c=c: e.matmul(PS(3), lhsT=ONF[:, :], rhs=YSQ[:, c, :], start=(c == 0), stop=(c == 1)),
                     ['onf', ('cysq', c)], [('ps', 3)])
            S.op('dve', lambda e: e.tensor_scalar(out=MEAN, in0=PS(2), scalar1=1.0 / 256, scalar2=None, op0=ALU.mult),
                 [('ps', 2)], ['cmean'])
            S.op('dve', lambda e: e.tensor_tensor(out=TMP, in0=MEAN, in1=MEAN, op=ALU.mult), ['cmean'], ['ctmp'])
            S.op('dve', lambda e: e.scalar_tensor_tensor(out=VAR, in0=PS(3), scalar=1.0 / 256, in1=TMP,
                                                         op0=ALU.mult, op1=ALU.subtract), [('ps', 3), 'ctmp'], ['cvar'])
            S.op('dve', lambda e: e.tensor_scalar(out=VAR, in0=VAR, scalar1=EPS, scalar2=None, op0=ALU.add), ['cvar'], ['cvar'])
            S.op('act', lambda e: e.activation(out=TMP, in_=VAR, func=AF.Ln), ['cvar'], ['ctmp'])
            S.op('act', lambda e: e.activation(out=VAR, in_=TMP, func=AF.Exp, scale=-0.5), ['ctmp'], ['cvar'])
            for c in range(2):
                S.op('dve', lambda e, c=c: e.tensor_tensor(out=Y[:, c, :], in0=Y[:, c, :], in1=MEAN, op=ALU.subtract),
                     [('cy', c), 'cmean'], [('cy', c)])
                S.op('dve', lambda e, c=c: e.tensor_tensor(out=Y[:, c, :], in0=Y[:, c, :], in1=VAR, op=ALU.mult),
                     [('cy', c), 'cvar'], [('cy', c)])
                S.op('act', lambda e, c=c, b=b: e.activation(
                    out=catT[:, 6 + c, 256 + b * 512:256 + (b + 1) * 512], in_=Y[:, c, :], func=AF.Silu,
                    bias=VEC[:, 118 + c:119 + c], scale=VEC[:, 116 + c:117 + c]),
                    [('cy', c), 'vec'], [('cat', 6 + c)])

    def att_views(l, slot):
        base = slot * 8192
        Q = RA[:, base:base + TOK]
        K = RA[:, base + TOK:base + 2 * TOK]
        V = RA[:, base + 4608:base + 4608 + 18 * 128].rearrange("p (c d) -> p c d", d=128)
        return Q, K, V, V

    def phase_att(l):
        if l == 0:
            pass
        else:
            S.op('dve', lambda e: e.tensor_tensor(out=LAM[:, 0:2].unsqueeze(2), in0=VEC[:, 121:125].rearrange("p (a b) -> p a b", b=2)[:, :, 0:1],
                                                  in1=VEC[:, 121:125].rearrange("p (a b) -> p a b", b=2)[:, :, 1:2], op=ALU.mult),
                 ['vec'], ['lam'])
            S.op('pe', lambda e: e.matmul(PS(0)[:, 0:2], lhsT=ONF[:, :], rhs=LAM[:, 0:2], start=True, stop=True),
                 ['onf', 'lam'], [('ps', 0)])
            S.op('act', lambda e: e.activation(out=LAM[:, 2:4], in_=PS(0)[:, 0:2], func=AF.Exp), [('ps', 0)], ['lam'])
            S.op('dve', lambda e: e.scalar_tensor_tensor(out=LAM[:, 4:5], in0=LAM[:, 3:4], scalar=-LAM_INIT1, in1=LAM[:, 2:3],
                                                         op0=ALU.add, op1=ALU.subtract), ['lam'], ['lam'])
            S.op('dve', lambda e: e.tensor_scalar(out=LAM[:, 5:6], in0=VEC[:, 120:121], scalar1=1.0 - LAM_INIT1, scalar2=None,
                                                  op0=ALU.mult), ['vec'], ['lam'])
        nunits = 6

        def load_unit(u):
            slot = u % 2
            Q, K, vA, vB = att_views(l, slot)
            rk = [('qkT', l, t, r) for t in range(NT) for r in ()]
            wk = [('att', slot)]
            deps = [k for k in S.lastw.keys() if isinstance(k, tuple) and k[0] in ('qkT', 'vs') and k[1] == l]
            sk = f'att{slot}'
            if l == 0:
                hA, hB = 2 * u, 2 * u + 1
                gA, gB = hA // 3, hB // 3
                for half, h, g in ((0, hA, gA), (1, hB, gB)):
                    S.dma('sp', lambda e, half=half, h=h: e.dma_start(out=Q[half * 64:(half + 1) * 64, :],
                                                                     in_=qkT_s[0][h * 64:(h + 1) * 64, :]), deps, wk, sk)
                    S.dma('sp', lambda e, half=half, g=g: e.dma_start(out=K[half * 64:(half + 1) * 64, :],
                                                                     in_=qkT_s[0][768 + g * 64:768 + (g + 1) * 64, :]), deps, wk, sk)
                S.dma('sp', lambda e: e.dma_start(out=vA[:, :, 0:64],
                                                  in_=v_s[0][:, gA * 64:(gA + 1) * 64].rearrange("(c p) d -> p c d", p=128)), deps, wk, sk)
                S.dma('sp', lambda e: e.dma_start(out=vA[:, :, 64:128],
                                                  in_=v_s[0][:, gB * 64:(gB + 1) * 64].rearrange("(c p) d -> p c d", p=128)), deps, wk, sk)
            else:
                h = u
                S.dma('sp', lambda e: e.dma_start(out=Q[:, :], in_=qkT_s[1][h * 128:(h + 1) * 128, :]), deps, wk, sk)
                S.dma('sp', lambda e: e.dma_start(out=K[:, :], in_=qkT_s[1][768 + h * 128:768 + (h + 1) * 128, :]), deps, wk, sk)
                S.dma('sp', lambda e: e.dma_start(out=vA[:, :, :],
                                                  in_=v_s[1][:, h * 128:(h + 1) * 128].rearrange("(c p) d -> p c d", p=128)), deps, wk, sk)

        qblocks = ([(0, 256, [0, 1])] if l == 0 else []) + [(256 + qb * 512, 512, list(range(18))) for qb in range(4)]
        load_unit(0)
        for u in range(nunits):
            if u + 1 < nunits:
                load_unit(u + 1)
            slot = u % 2
            Q, K, vA, vB = att_views(l, slot)
            ak = ('att', slot)
            for (q0_, n_, chunks_) in qblocks:
                pump()
                att_block(l, u, Q, K, vA, ak, q0_, n_, chunks_)

    def att_block(l, u, Q, K, vA, ak, q0, n, chunks):
            if True:
                def qk(c, ci):
                    for i in range(2):
                        bank = 2 * i + (ci % 2)
                        S.op('pe', lambda e, i=i, c=c, bank=bank: e.matmul(
                            PS(bank)[:, 0:n], lhsT=K[i * 64:(i + 1) * 64, c * 128:(c + 1) * 128],
                            rhs=Q[i * 64:(i + 1) * 64, q0:q0 + n], start=True, stop=True), [ak], [('ps', bank)])

                def ex(c, ci):
                    for i in range(2):
                        bank = 2 * i + (ci % 2)
                        ps_ = (2 * ci + i) % 4
                        S.op('act', lambda e, bank=bank, ps_=ps_: e.activation(
                            out=PT[:, ps_, 0:n], in_=PS(bank)[:, 0:n], func=AF.Exp, scale=SCALE),
                            [('ps', bank)], [('pt', ps_)])

                def pv(c, ci):
                    first, last = (ci == 0), (ci == len(chunks) - 1)
                    for i in range(2):
                        ps_ = (2 * ci + i) % 4
                        if False:
                            pass
                        else:
                            S.op('pe', lambda e, c=c, ps_=ps_, i=i: e.matmul(PS(4 + i)[:, 0:n], lhsT=vA[:, c, :], rhs=PT[:, ps_, 0:n],
                                                                             start=first, stop=last), [ak, ('pt', ps_)], [('ps', 4 + i)])
                            S.op('pe', lambda e, ps_=ps_, i=i: e.matmul(PS(6 + i)[:, 0:n], lhsT=ONB[:, :], rhs=PT[:, ps_, 0:n],
                                                                        start=first, stop=last), ['onb', ('pt', ps_)], [('ps', 6 + i)])

                nch = len(chunks)
                qk(chunks[0], 0)
                ex(chunks[0], 0)
                for ci in range(nch):
                    if ci + 1 < nch:
                        qk(chunks[ci + 1], ci + 1)
                        ex(chunks[ci + 1], ci + 1)
                    pv(chunks[ci], ci)
                OSs = [STG[:, 0:512], STG[:, 512:1024]]
                DSs = [STG[:, 1024:1536], STG[:, 1536:2048]]
                for i in range(2):
                    r0, r1 = (i * 64, i * 64 + 64) if l == 0 else (0, 128)
                    S.op('act', lambda e, i=i, r0=r0, r1=r1: e.activation(out=OSs[i][r0:r1, 0:n], in_=PS(4 + i)[r0:r1, 0:n], func=AF.Copy),
                         [('ps', 4 + i)], [('os', i)])
                    S.op('dve', lambda e, i=i, r0=r0, r1=r1: e.tensor_copy(out=DSs[i][r0:r1, 0:n], in_=PS(6 + i)[r0:r1, 0:n]),
                         [('ps', 6 + i)], [('ds', i)])
                for i in range(2):
                    r0, r1 = (i * 64, i * 64 + 64) if l == 0 else (0, 128)
                    S.op('dve', lambda e, i=i, r0=r0, r1=r1: e.reciprocal(out=DSs[i][r0:r1, 0:n], in_=DSs[i][r0:r1, 0:n]),
                         [('ds', i)], [('ds', i)])
                    if l == 0:
                        S.op('dve', lambda e, i=i, r0=r0, r1=r1: e.tensor_tensor(
                            out=catT[r0:r1, u, q0:q0 + n], in0=OSs[i][r0:r1, 0:n], in1=DSs[i][r0:r1, 0:n], op=ALU.mult),
                            [('os', i), ('ds', i)], [('cat', u)])
                    else:
                        S.op('dve', lambda e, i=i: e.tensor_tensor(out=OSs[i][:, 0:n], in0=OSs[i][:, 0:n], in1=DSs[i][:, 0:n], op=ALU.mult),
                             [('os', i), ('ds', i)], [('os', i)])
                if l == 1:
                    T0, T1_ = OSs[0], OSs[1]
                    R1 = DSs[1]
                    SQB = TT[:, 0, :]
                    S.op('dve', lambda e: e.scalar_tensor_tensor(out=T0[:, 0:n], in0=T1_[:, 0:n], scalar=LAM[:, 4:5], in1=T0[:, 0:n],
                                                                 op0=ALU.mult, op1=ALU.add), [('os', 0), ('os', 1), 'lam'], [('os', 0)])
                    S.op('dve', lambda e: e.tensor_tensor(out=SQB[:, 0:n], in0=T0[:, 0:n], in1=T0[:, 0:n], op=ALU.mult),
                         [('os', 0)], [('tt', 0)])
                    S.op('pe', lambda e: e.matmul(PS(0)[:, 0:n], lhsT=ONB[:, :], rhs=SQB[:, 0:n], start=True, stop=True),
                         ['onb', ('tt', 0)], [('ps', 0)])
                    S.op('act', lambda e: e.activation(out=R1[:, 0:n], in_=PS(0)[:, 0:n], func=AF.Ln, scale=1.0 / 128, bias=SM[:, 80:81]),
                         [('ps', 0), 'sm_eps'], [('ds', 1)])
                    S.op('act', lambda e: e.activation(out=R1[:, 0:n], in_=R1[:, 0:n], func=AF.Exp, scale=-0.5), [('ds', 1)], [('ds', 1)])
                    S.op('dve', lambda e: e.scalar_tensor_tensor(out=catT[:, u, q0:q0 + n], in0=T0[:, 0:n], scalar=LAM[:, 5:6], in1=R1[:, 0:n],
                                                                 op0=ALU.mult, op1=ALU.mult), [('os', 0), ('ds', 1), 'lam'], [('cat', u)])

    def ln_update(t, Y, gbc, gkey, lng, lnb, lnkeys, ykeys):
        k = nxt('lnstg', 2)
        stg = STG[:, k * 1024:(k + 1) * 1024]
        sk = ('lnstg', k)
        so = 96 + k * 32
        stats = SM[:, so:so + 12]
        mv = SM[:, so + 12:so + 14]
        lnv = SM[:, so + 14:so + 15]
        rstd = SM[:, so + 15:so + 16]
        nb = SM[:, so + 16:so + 17]
        smk = ('lnsm', k)
        xk = ('x', t)
        S.op('dve', lambda e: e.tensor_tensor(out=stg, in0=Y, in1=gbc, op=ALU.mult), list(ykeys) + [gkey], [sk])
        S.op('dve', lambda e: e.scalar_tensor_tensor(out=X[:, t, :], in0=X[:, t, :], scalar=ALPHA, in1=stg,
                                                     op0=ALU.mult, op1=ALU.add), [xk, sk], [xk])
        for hh in range(2):
            S.op('dve', lambda e, hh=hh: e.bn_stats(out=stats[:, hh * 6:(hh + 1) * 6], in_=X[:, t, hh * 512:(hh + 1) * 512]),
                 [xk], [smk])
        S.op('dve', lambda e: e.bn_aggr(out=mv, in_=stats), [smk], [smk])
        S.op('act', lambda e: e.activation(out=lnv, in_=mv[:, 1:2], func=AF.Ln, bias=SM[:, 80:81]), [smk, 'sm_eps'], [smk])
        S.op('act', lambda e: e.activation(out=rstd, in_=lnv, func=AF.Exp, scale=-0.5), [smk], [smk])
        S.op('dve', lambda e: e.scalar_tensor_tensor(out=nb, in0=mv[:, 0:1], scalar=-1.0, in1=rstd, op0=ALU.mult, op1=ALU.mult),
             [smk], [smk])
        S.op('act', lambda e: e.activation(out=stg, in_=X[:, t, :], func=AF.Identity, bias=nb, scale=rstd), [xk, smk], [sk])
        S.op('dve', lambda e: e.tensor_tensor(out=stg, in0=stg, in1=lng, op=ALU.mult), [sk, lnkeys[0]], [sk])
        S.op('dve', lambda e: e.tensor_tensor(out=X[:, t, :], in0=stg, in1=lnb, op=ALU.add), [sk, lnkeys[1]], [xk])

    def phase_p4(l):
        need(f'wo{l}')
        S.dma('sp', lambda e: e.dma_start(out=WOUT[:, :, :], in_=wo_s[l].rearrange("(k p) c -> p k c", p=128)),
              [f'wo{l}'], ['wout'], 'wout')
        build_gate_bc(BCT[0], l, 2, 0, ('bct', 0))
        if l == 0:
            build_gate_bc(BCT[1], l, 2, 1, ('bct', 1))
        load_ln_bc(BCT[2], l * 4 + 0, ('bct', 2))
        load_ln_bc(BCT[3], l * 4 + 1, ('bct', 3))
        tiles = range(NT) if l == 0 else range(2, NT)
        for t in tiles:
            if t % 3 == 0:
                pump()
            p4_tile(l, t)

    def p4_tile(l, t):
        if True:
            pb = nxt('p4pair', 2)
            w = 1 if t < 2 else 0
            for nh in range(2):
                bank = 2 * pb + nh
                for kc in range(8):
                    S.op('pe', lambda e, kc=kc, nh=nh, bank=bank: e.matmul(
                        PS(bank), lhsT=catT[:, kc, t * 128:(t + 1) * 128], rhs=WOUT[:, kc, nh * 512:(nh + 1) * 512],
                        start=(kc == 0), stop=(kc == 7)), [('cat', kc), 'wout'], [('ps', bank)])
            ln_update(t, PSB[pb][:, :], BCT[w], ('bct', w), BCT[2], BCT[3], [('bct', 2), ('bct', 3)],
                      [('ps', 2 * pb), ('ps', 2 * pb + 1)])

    def phase_p5(l, last):
        need(f'fi{l}')
        need(f'fo{l}')
        S.dma('sp', lambda e: e.dma_start(out=WO[:, 0:11, :], in_=fo_s[l][0:1408, :].rearrange("(j p) c -> p j c", p=128)),
              [f'fo{l}'], ['wo'], 'wo')
        S.dma('sp', lambda e: e.dma_start(out=WO[:, 11:22, :], in_=fo_s[l][1408:2816, :].rearrange("(j p) c -> p j c", p=128)),
              [f'fo{l}'], ['wo'], 'wo')
        build_gate_bc(BCT[0], l, 5, 0, ('bct', 0))
        if l == 0:
            build_gate_bc(BCT[1], l, 5, 1, ('bct', 1))
        load_ln_bc(BCT[2], l * 4 + 2, ('bct', 2))
        load_ln_bc(BCT[3], l * 4 + 3, ('bct', 3))
        sbs = BLOCKS if l == 0 else BLOCKS[1:]
        for tiles, w in sbs:
            p5_sb(l, last, tiles, w)

    def p5_sb(l, last, tiles, w):
        if True:
            n = 128 * len(tiles)
            for kc in range(8):
                bank = 4 + nxt('p5tb', 4)
                for i, t in enumerate(tiles):
                    S.op('pe', lambda e, i=i, t=t, kc=kc, bank=bank: e.transpose(
                        out=PS(bank)[:, i * 128:(i + 1) * 128], in_=X[:, t, kc * 128:(kc + 1) * 128], identity=ident),
                        [('x', t), 'cst'], [('ps', bank)])
                S.op('act', lambda e, kc=kc, bank=bank: e.activation(
                    out=h2T[:, kc, 0:n], in_=PS(bank)[:, 0:n], func=AF.Identity, bias=modv(l, 3, kc, w), scale=modv(l, 4, kc, w)),
                    [('ps', bank), ('mod', l, 1)], [('h2T', kc)])
            for jp in range(11):
                if jp % 4 == 0:
                    pump()
                slot = nxt('fring', 2)
                sa, sg_ = FRING[slot]
                S.dma('sp', lambda e, jp=jp, sa=sa: e.dma_start(
                    out=sa[:, :, :], in_=fi_s[l][:, jp * 256:(jp + 1) * 256].rearrange("(k p) c -> p k c", p=128)),
                    [f'fi{l}'], [('fring', slot)], f'fring{slot}')
                S.dma('sp', lambda e, jp=jp, sg_=sg_: e.dma_start(
                    out=sg_[:, :, :], in_=fi_s[l][:, FFN_H + jp * 256:FFN_H + (jp + 1) * 256].rearrange("(k p) c -> p k c", p=128)),
                    [f'fi{l}'], [('fring', slot)], f'fring{slot}')
                for jj in range(2):
                    j = 2 * jp + jj
                    pr = nxt('p5ab', 2)
                    ba, bg = 2 * pr, 2 * pr + 1
                    for (bank, slab) in ((ba, sa), (bg, sg_)):
                        for kc in range(8):
                            S.op('pe', lambda e, kc=kc, bank=bank, slab=slab, jj=jj: e.matmul(
                                PS(bank)[:, 0:n], lhsT=slab[:, kc, jj * 128:(jj + 1) * 128], rhs=h2T[:, kc, 0:n],
                                start=(kc == 0), stop=(kc == 7)), [('fring', slot), ('h2T', kc)], [('ps', bank)])
                    sgs = nxt('sgs', 2)
                    S.op('act', lambda e, bg=bg, sgs=sgs: e.activation(out=SG[:, sgs, 0:n], in_=PS(bg)[:, 0:n], func=AF.Silu),
                         [('ps', bg)], [('sg', sgs)])
                    S.op('dve', lambda e, ba=ba, sgs=sgs, j=j: e.tensor_tensor(out=hidT[:, j, 0:n], in0=PS(ba)[:, 0:n],
                                                                              in1=SG[:, sgs, 0:n], op=ALU.mult),
                         [('ps', ba), ('sg', sgs)], [('hid', j)])
            for ti, t in enumerate(tiles):
                pb = 2 + nxt('p5pair', 2)
                for nh in range(2):
                    bank = 2 * pb + nh
                    for j in range(22):
                        S.op('pe', lambda e, j=j, nh=nh, bank=bank, ti=ti: e.matmul(
                            PS(bank), lhsT=hidT[:, j, ti * 128:(ti + 1) * 128], rhs=WO[:, j, nh * 512:(nh + 1) * 512],
                            start=(j == 0), stop=(j == 21)), [('hid', j), 'wo'], [('ps', bank)])
                ln_update(t, PSB[pb][:, :], BCT[w], ('bct', w), BCT[2], BCT[3], [('bct', 2), ('bct', 3)],
                          [('ps', 2 * pb), ('ps', 2 * pb + 1)])
                if last:
                    S.dma('sp', lambda e, t=t: e.dma_start(out=out_d[(t - 2) * 128:(t - 1) * 128, :], in_=X[:, t, :]),
                          [('x', t)], [('out', t)], 'outw', sbuf=False)

    S.op('dve', lambda e: e.memset(SM[:, 80:81], EPS), [], ['sm_eps'])

    def dump_all():
        S.barrier()
        for t in range(NT):
            S.dma('sp', lambda e, t=t: e.dma_start(out=dbg_d[t * 128:(t + 1) * 128, :], in_=X[:, t, :]),
                  [('x', t)], [('dbg', t)], 'dbgw', sbuf=False)
        S.dma('sp', lambda e: e.dma_start(out=dbg_hb, in_=HB[:, 0:18432]), [], ['dbg_hb'], 'dbgw', sbuf=False)
        S.dma('sp', lambda e: e.dma_start(out=dbg_mod, in_=MOD[:, :, :, :].rearrange("p l o w -> p (l o w)")), [], ['dbg_mod'], 'dbgw', sbuf=False)
        S.dma('sp', lambda e: e.dma_start(out=dbg_r3, in_=R3[:, :]), [], ['dbg_r3'], 'dbgw', sbuf=False)

    def run_all():
        for l in range(2):
            compute_mod(l, 0)
            S.barrier()
            if stop == f'mod{l}':
                return
            phase_p0(l)
            if stop == f'p0{l}':
                return
            if l == 0:
                S.op('dve', lambda e: e.memset(R3[:, :], 0.0), [], ['uT'])
            phase_p1(l)
            S.barrier()
            compute_mod(l, 1)
            if stop == f'p1{l}':
                return
            if l == 0:
                phase_pool()
            else:
                phase_conv()
            S.barrier()
            if stop == f'p2{l}':
                return
            phase_att(l)
            S.barrier()
            if stop == f'p3{l}':
                return
            phase_p4(l)
            S.barrier()
            if stop == f'p4{l}':
                return
            phase_p5(l, last=(l == 1))
            S.barrier()
            if stop == f'p5{l}':
                return
    run_all()
    if debug:
        dump_all()
    S.wait_all('sp', ['outw', 'dbgw'])

    with nc.Block() as block:
        S.emit(block)
    es.close()
    return nc


def _rope_tables():
    n = np.arange(2048)
    row = (n // 64).astype(np.float32)
    col = (n % 64).astype(np.float32)
    freqs = (np.float32(10000.0) ** (-np.arange(0, 32, 2, dtype=np.float32) / np.float32(32))).astype(np.float32)
    ang = np.concatenate([row[:, None] * freqs, col[:, None] * freqs], axis=-1).astype(np.float32)
    cos, sin = np.cos(ang).astype(np.float32), np.sin(ang).astype(np.float32)
    cr, cc, sr, sc = cos[:, :16], cos[:, 16:], sin[:, :16], sin[:, 16:]
    C = np.concatenate([cr, cr, cc, cc], axis=-1)
    Sg = np.concatenate([-sr, sr, -sc, sc], axis=-1)
    C = C.reshape(16, 128, 64).transpose(1, 0, 2)
    Sg = Sg.reshape(16, 128, 64).transpose(1, 0, 2)
    return np.ascontiguousarray(C), np.ascontiguousarray(Sg)


def _consts():
    cst = np.zeros((128, NCST), np.float32)
    cst[:, C_ID:C_ID + 128] = np.eye(128, dtype=np.float32)
    C, Sg = _rope_tables()
    cst[:, C_RC:C_RC + 1024] = C.reshape(128, 1024)
    cst[:, C_RS:C_RS + 1024] = Sg.reshape(128, 1024)
    pe = np.zeros((128, 2, 2, 8), np.float32)
    for p in range(128):
        for c in range(2):
            w = POOL_W[2 * c + p // 64]
            for i in range(8):
                pe[p, c, 0, i] = 1.0 / ((i + w // 2) - max(i - w // 2, 0))
                pe[p, c, 1, i] = 1.0 / min(w, 8 - i + w // 2)
    cst[:, C_PE:C_PE + 32] = pe.reshape(128, 32)
    return cst


def _colvec(v):
    v = np.asarray(v, np.float32).reshape(-1, 128)
    return v.T


_NC_CACHE = {}


def kernel(x, c, ctx, c_ctx, ab_w_in, ab_q_gain, ab_k_gain, ab_w_pool, ab_pool_scale, ab_w_out,
           cd_w_in, cd_lambda_q1, cd_lambda_k1, cd_lambda_q2, cd_lambda_k2, cd_subln_gain,
           cd_conv_w, cd_conv_b, cd_conv_ln_g, cd_conv_ln_b, cd_w_out,
           ada_w, ada_b, ln1_g, ln1_b, ln2_g, ln2_b, ffn_w_in, ffn_w_out, _debug=False, _stop=None):
    f = lambda a: np.ascontiguousarray(np.asarray(a, dtype=np.float32))
    x, c, ctx, c_ctx = f(x), f(c), f(ctx), f(c_ctx)
    if (_debug, _stop) not in _NC_CACHE:
        _NC_CACHE[(_debug, _stop)] = build_program(debug=_debug, stop=_stop)
    nc = _NC_CACHE[(_debug, _stop)]
    cst = _consts()
    lnbc = np.stack([f(ln1_g)[0], f(ln1_b)[0], f(ln2_g)[0], f(ln2_b)[0],
                     f(ln1_g)[1], f(ln1_b)[1], f(ln2_g)[1], f(ln2_b)[1]], axis=0)
    gains = np.stack([f(ab_q_gain)[0], f(ab_k_gain)[0]], axis=0)
    shared = {
        "consts": cst, "lnbc": np.ascontiguousarray(lnbc), "gains": np.ascontiguousarray(gains),
        "ab_w_in": f(ab_w_in)[0], "cd_w_in": f(cd_w_in)[0], "ab_w_out": f(ab_w_out)[0], "cd_w_out": f(cd_w_out)[0],
        "ab_w_pool": f(ab_w_pool)[0].reshape(256, 64), "ada_w": f(ada_w).reshape(2 * D, 6 * D),
        "ffn_w_in": f(ffn_w_in).reshape(2 * D, 2 * FFN_H), "ffn_w_out": f(ffn_w_out).reshape(2 * FFN_H, D),
    }
    in_maps = []
    for b in range(NCORES):
        vecs = np.zeros((128, NV), np.float32)
        vecs[:, 0:8] = _colvec(c[b])
        vecs[:, 8:16] = _colvec(c_ctx)
        vecs[:, 16:64] = _colvec(f(ada_b)[0])
        vecs[:, 64:112] = _colvec(f(ada_b)[1])
        vecs[:, 112:114] = _colvec(f(ab_pool_scale)[0])
        vecs[:, 114:116] = _colvec(f(cd_conv_b)[0])
        vecs[:, 116:118] = _colvec(f(cd_conv_ln_g)[0])
        vecs[:, 118:120] = _colvec(f(cd_conv_ln_b)[0])
        vecs[:, 120:121] = _colvec(f(cd_subln_gain)[0])
        vecs[0:64, 121] = f(cd_lambda_q1)[0]
        vecs[0:64, 122] = f(cd_lambda_k1)[0]
        vecs[0:64, 123] = f(cd_lambda_q2)[0]
        vecs[0:64, 124] = f(cd_lambda_k2)[0]
        cw = f(cd_conv_w)[0]
        for j in range(31):
            vecs[:, 125 + 2 * j:127 + 2 * j] = _colvec(cw[j])
        m = dict(shared)
        m["x"] = x[b]
        m["ctx"] = ctx[b]
        m["vecs"] = vecs
        in_maps.append(m)
    res = run_bass_kernel_spmd(nc, in_maps, core_ids=list(range(NCORES)))
    out = np.stack([np.asarray(r["out"], dtype=np.float32) for r in res.results], axis=0)
    if _debug:
        return out, res.results
    return out
```

```python
import math
from contextlib import ExitStack

import numpy as np
import concourse.bass as bass
import concourse.mybir as mybir
from concourse.bass_utils import run_bass_kernel_spmd

F32 = mybir.dt.float32
BF16 = mybir.dt.bfloat16
AF = mybir.ActivationFunctionType
ALU = mybir.AluOpType
AX = mybir.AxisListType

NCORES = 8
D = 1024
NT = 18
TOK = 2304
ALPHA = 4.0 ** 0.25
EPS = 1e-6
FFN_H = 2816
LAM_INIT1 = 0.8 - 0.6 * math.exp(-0.3)
POOL_W = (2, 4, 8, 16)
NV = 192
C_ID = 0
C_RC = 128
C_RS = C_RC + 16 * 64
C_PE = C_RS + 16 * 64
NCST = C_PE + 32
UW = 2336


class Sched:
    def __init__(self, nc, es):
        self.nc = nc
        self.es = es
        self.engs = ['pe', 'act', 'dve', 'pool', 'sp']
        self.prog = {e: [] for e in self.engs}
        self.cnt = {e: 0 for e in self.engs}
        self.seen = {e: {} for e in self.engs}
        self.lastw = {}
        self.readers = {}
        self.sems = {}
        self.dcum = {}
        self.sbuf_dma = set()
        for e in self.engs:
            self.sems[e] = es.enter_context(nc.semaphore("sem_" + e))

    def _deps(self, reads, writes):
        deps = {}
        raw = {}

        def add(dct, s, v):
            if dct.get(s, 0) < v:
                dct[s] = v
        for k in reads:
            t = self.lastw.get(k)
            if t is not None:
                add(deps, *t)
                add(raw, *t)
        for k in writes:
            t = self.lastw.get(k)
            if t is not None:
                add(deps, *t)
            for s, v in self.readers.get(k, {}).items():
                add(deps, s, v)
        self._raw = raw
        return deps

    def _commit(self, tok, reads, writes):
        s, v = tok
        for k in reads:
            r = self.readers.setdefault(k, {})
            if r.get(s, 0) < v:
                r[s] = v
        for k in writes:
            self.lastw[k] = tok
            self.readers[k] = {}

    def _waits(self, eng, deps):
        w = []
        for s, v in deps.items():
            if s == eng and eng in ('pe', 'sp'):
                continue
            if self.seen[eng].get(s, 0) >= v:
                continue
            self.seen[eng][s] = v
            w.append((s, v))
        return w

    def op(self, eng, fn, reads=(), writes=()):
        deps = self._deps(reads, writes)
        waits = self._waits(eng, deps)
        self.cnt[eng] += 1
        self.prog[eng].append((waits, fn, (eng, 1)))
        self._commit((eng, self.cnt[eng]), reads, writes)

    def dma(self, eng, fn, reads, writes, semkey, sbuf=True):
        if semkey not in self.sems:
            self.sems[semkey] = self.es.enter_context(self.nc.semaphore("d_" + semkey))
            self.dcum[semkey] = 0
        if sbuf:
            self.sbuf_dma.add(semkey)
        deps = self._deps(reads, writes)
        waits = self._waits(eng, deps)
        self.dcum[semkey] += 16
        self.prog[eng].append((waits, fn, (semkey, 16)))
        self._commit((semkey, self.dcum[semkey]), reads, writes)

    def barrier(self, engs=('pe', 'act', 'dve', 'sp', 'pool')):
        toks = {e: self.cnt[e] for e in ('pe', 'act', 'dve', 'pool') if self.cnt[e] > 0}
        for k in self.sbuf_dma:
            toks[k] = self.dcum[k]
        self._raw = dict(toks)
        for e in engs:
            w = self._waits(e, toks)
            if w:
                self.prog[e].append((w, None, None))

    def gate(self, eng, on):
        toks = {on: self.cnt[on]}
        self._raw = dict(toks)
        w = self._waits(eng, toks)
        if w:
            self.prog[eng].append((w, None, None))

    def wait_all(self, eng, semkeys):
        toks = {k: self.dcum[k] for k in semkeys if k in self.dcum}
        self._raw = dict(toks)
        w = self._waits(eng, toks)
        if w:
            self.prog[eng].append((w, None, None))

    def emit(self, block):
        emap = {'pe': block.tensor, 'act': block.scalar, 'dve': block.vector,
                'pool': block.gpsimd, 'sp': block.sync}
        for e in self.engs:
            prog = self.prog[e]

            def body(eng, prog=prog, ename=e):
                for waits, fn, inc in prog:
                    attach = None
                    if fn is not None and ename in ('act', 'dve') and waits:
                        attach = waits[-1]
                        waits = waits[:-1]
                    for s, v in waits:
                        eng.wait_ge(self.sems[s], v)
                    if fn is not None:
                        ins = fn(eng)
                        if attach is not None:
                            ins._wait_ge(self.sems[attach[0]], attach[1])
                        ins.then_inc(self.sems[inc[0]], inc[1])
            emap[e](body)


def build_program(debug=False, stop=None):
    nc = bass.Bass("TRN2", target_bir_lowering=False, dynamic_dma_scratch_size=4096)
    es = ExitStack()

    def din(name, shape, dt=F32):
        return nc.dram_tensor(name, list(shape), dt, kind="ExternalInput").ap()

    x_d = din("x", [2048, D])
    ctx_d = din("ctx", [256, D])
    vecs_d = din("vecs", [128, NV])
    cst_d = din("consts", [128, NCST])
    lnbc_d = din("lnbc", [8, D])
    gains_d = din("gains", [2, 64])
    wq_d = [din("ab_w_in", [D, 1536]), din("cd_w_in", [D, 2816])]
    wo_d = [din("ab_w_out", [D, D]), din("cd_w_out", [D, D])]
    wpool_d = din("ab_w_pool", [256, 64])
    ada_d = din("ada_w", [2 * D, 6 * D])
    fi_d = din("ffn_w_in", [2 * D, 2 * FFN_H])
    fo_d = din("ffn_w_out", [2 * FFN_H, D])
    out_d = nc.dram_tensor("out", [2048, D], F32, kind="ExternalOutput").ap()
    dbg_d = None
    if debug:
        dbg_d = nc.dram_tensor("dbg", [TOK, D], F32, kind="ExternalOutput").ap()

    def dscr(name, shape):
        if debug and name in ("qkT0_s", "v0_s", "qkT1_s", "v1_s"):
            return nc.dram_tensor(name, list(shape), BF16, kind="ExternalOutput").ap()
        return nc.dram_tensor(name, list(shape), BF16).ap()

    if debug:
        dbg_hb = nc.dram_tensor("dbg_hb", [128, 18432], BF16, kind="ExternalOutput").ap()
        dbg_mod = nc.dram_tensor("dbg_mod", [128, 192], F32, kind="ExternalOutput").ap()
        dbg_r3 = nc.dram_tensor("dbg_r3", [128, 4672], F32, kind="ExternalOutput").ap()
        dbg_att = nc.dram_tensor("dbg_att", [128, 1024], F32, kind="ExternalOutput").ap()
        dbg_stg = nc.dram_tensor("dbg_stg", [128, 2048], F32, kind="ExternalOutput").ap()
        dbg_pt = nc.dram_tensor("dbg_pt", [128, 2048], BF16, kind="ExternalOutput").ap()

    wq_s = [dscr("wq0_s", [D, 1536]), dscr("wq1_s", [D, 2816])]
    wo_s = [dscr("wo0_s", [D, D]), dscr("wo1_s", [D, D])]
    ada_s = [dscr("ada0_s", [D, 6 * D]), dscr("ada1_s", [D, 6 * D])]
    fi_s = [dscr("fi0_s", [D, 2 * FFN_H]), dscr("fi1_s", [D, 2 * FFN_H])]
    fo_s = [dscr("fo0_s", [FFN_H, D]), dscr("fo1_s", [FFN_H, D])]
    NQK = [1024, 1536]
    VW = [256, 768]
    qkT_s = [dscr("qkT0_s", [NQK[0], TOK]), dscr("qkT1_s", [NQK[1], TOK])]
    v_s = [dscr("v0_s", [TOK, VW[0]]), dscr("v1_s", [TOK, VW[1]])]

    def sb(name, shape, dt):
        return es.enter_context(nc.sbuf_tensor(name, list(shape), dt))

    X = sb("X", [128, NT, D], F32)
    HB = sb("HB", [128, 30720], BF16)
    RA = sb("RA", [128, 16384], BF16)
    R3 = sb("R3", [128, 4672], F32)
    CST = sb("CST", [128, NCST], F32)
    VEC = sb("VEC", [128, NV], F32)
    MOD = sb("MOD", [128, 2, 48, 2], F32)
    SILC = sb("SILC", [128, 8, 2], BF16)
    IDB = sb("IDB", [128, 128], BF16)
    ONB = sb("ONB", [128, 128], BF16)
    ONF = sb("ONF", [128, 128], F32)
    STG = sb("STG", [128, 2048], F32)
    TT = sb("TT", [128, 2, 512], BF16)
    VST = sb("VST", [128, 2, 512], BF16)
    PT = sb("PT", [128, 4, 512], BF16)
    SG = sb("SG", [128, 2, 512], F32)
    GBC = sb("GBC", [128, 2, 64], F32)
    BD = sb("BD", [128, 2, 128], BF16)
    DG = sb("DG", [128, 2, 128], F32)
    SM = sb("SM", [128, 256], F32)
    LAM = sb("LAM", [128, 8], F32)

    PSB = [es.enter_context(nc.psum_tensor(f"psb{i}", [128, 1024], F32)) for i in range(4)]

    def PS(i):
        return PSB[i // 2][:, (i % 2) * 512:(i % 2) * 512 + 512]

    ident = CST[:, C_ID:C_ID + 128]
    ropeC = CST[:, C_RC:C_RC + 1024].rearrange("p (t d) -> p t d", d=64)
    ropeS = CST[:, C_RS:C_RS + 1024].rearrange("p (t d) -> p t d", d=64)
    pedge = CST[:, C_PE:C_PE + 32].rearrange("p (c s i) -> p c s i", c=2, s=2)

    hT_all = HB[:, 0:8 * TOK].rearrange("p (k t) -> p k t", k=8)
    catT = hT_all
    WSLAB = [HB[:, 18432 + s * 4096: 18432 + (s + 1) * 4096].rearrange("p (k c) -> p k c", k=8) for s in range(2)]
    WOUT = HB[:, 18432:18432 + 8192].rearrange("p (k c) -> p k c", k=8)
    WO = HB[:, 0:22528].rearrange("p (j c) -> p j c", j=22)
    FRING = [(HB[:, 22528 + s * 4096: 22528 + s * 4096 + 2048].rearrange("p (k c) -> p k c", k=8),
              HB[:, 22528 + s * 4096 + 2048: 22528 + (s + 1) * 4096].rearrange("p (k c) -> p k c", k=8))
             for s in range(2)]
    hidT = RA[:, 0:11264].rearrange("p (j t) -> p j t", j=22)
    h2T = RA[:, 11264:15360].rearrange("p (k t) -> p k t", k=8)
    BCT = [R3[:, i * 1024:(i + 1) * 1024] for i in range(4)]
    uT = R3[:, 0:2 * UW].rearrange("p (c w) -> p c w", c=2)
    aT = R3[:, 0:4096].rearrange("p (c t) -> p c t", c=2)
    RAF = RA[:, :].bitcast(F32)

    S = Sched(nc, es)
    rot = {}

    def nxt(name, n):
        v = rot.get(name, 0)
        rot[name] = (v + 1) % n
        return v

    SCALE = 0.125

    def cast2d(dst, src, key):
        r, c = src.shape
        if c > 2048:
            if c % 2048 == 0:
                d2 = dst.rearrange("a (b c) -> a b c", c=2048)
                s2 = src.rearrange("a (b c) -> a b c", c=2048)
            else:
                d2 = dst.rearrange("a b -> (a b)").rearrange("(n c) -> n c", c=2048)
                s2 = src.rearrange("a b -> (a b)").rearrange("(n c) -> n c", c=2048)
        else:
            d2, s2 = dst, src
        S.dma('pool', lambda e: e.dma_start(out=d2, in_=s2), [], [key], semkey="c_" + key, sbuf=False)

    CASTQ = []

    def qcast(dst, src, key, rows_per):
        r = src.shape[0]
        for r0 in range(0, r, rows_per):
            r1 = min(r, r0 + rows_per)
            CASTQ.append((key, lambda dst=dst, src=src, r0=r0, r1=r1, key=key: cast2d(dst[r0:r1, :], src[r0:r1, :], key)))

    def pump(n=1):
        for _ in range(n):
            if not CASTQ:
                return
            S.gate('pool', 'pe')
            CASTQ.pop(0)[1]()

    def need(key):
        while any(k == key for k, _ in CASTQ):
            CASTQ.pop(0)[1]()

    cast2d(ada_s[0][:, 0:2048], ada_d[0:D, 0:2048], "ada0a")
    cast2d(wq_s[0], wq_d[0], "wq0")
    cast2d(ada_s[0][:, 2048:6144], ada_d[0:D, 2048:6144], "ada0b")
    qcast(wo_s[0], wo_d[0], "wo0", 1024)
    qcast(fi_s[0], fi_d[0:D, :], "fi0", 128)
    qcast(fo_s[0], fo_d[0:FFN_H, :], "fo0", 704)
    qcast(ada_s[1], ada_d[D:2 * D, :], "ada1", 128)
    qcast(wq_s[1], wq_d[1], "wq1", 256)
    qcast(wo_s[1], wo_d[1], "wo1", 1024)
    qcast(fi_s[1], fi_d[D:2 * D, :], "fi1", 128)
    qcast(fo_s[1], fo_d[FFN_H:2 * FFN_H, :], "fo1", 704)

    S.dma('sp', lambda e: e.dma_start(out=CST[:, :], in_=cst_d), [], ['cst'], 'l_cst')
    S.dma('sp', lambda e: e.dma_start(out=VEC[:, :], in_=vecs_d), [], ['vec'], 'l_vec')
    S.dma('sp', lambda e: e.dma_start(out=GBC[:, :, :], in_=gains_d.partition_broadcast(128)), [], ['gbc'], 'l_gbc')
    S.dma('sp', lambda e: e.dma_start(out=X[:, 0:2, :], in_=ctx_d.rearrange("(t p) d -> p t d", p=128)),
          [], [('x', 0), ('x', 1)], 'l_ctx')
    for i in range(4):
        S.dma('sp', lambda e, i=i: e.dma_start(out=X[:, 2 + 4 * i:6 + 4 * i, :],
                                               in_=x_d[i * 512:(i + 1) * 512, :].rearrange("(t p) d -> p t d", p=128)),
              [], [('x', 2 + 4 * i + j) for j in range(4)], f'l_x{i}')

    S.op('dve', lambda e: e.memset(BD[:, :, :], 0.0), [], ['bd'])
    for c in range(2):
        for hh in range(2):
            g = 2 * c + hh
            S.dma('pool', lambda e, c=c, hh=hh, g=g: e.dma_start(out=BD[hh * 64:(hh + 1) * 64, c, hh * 64:(hh + 1) * 64],
                                                                 in_=wpool_d[g * 64:(g + 1) * 64, :]),
                  [], ['bd'], 'l_bd')
    S.op('dve', lambda e: e.memset(ONB[:, :], 1.0), [], ['onb'])
    S.op('dve', lambda e: e.memset(ONF[:, :], 1.0), [], ['onf'])
    S.op('dve', lambda e: e.tensor_copy(out=IDB[:, :], in_=ident), ['cst'], ['idb'])
    S.op('act', lambda e: e.activation(out=SILC[:, :, :].rearrange("p k w -> p w k"),
                                       in_=VEC[:, 0:16].rearrange("p (w k) -> p w k", w=2), func=AF.Silu),
         ['vec'], ['silc'])

    def compute_mod(l, part):
        if l == 0:
            keys_a = ['ada0a'] if part == 0 else ['ada0b']
        else:
            need('ada1')
            keys_a = ['ada1']
        srange = range(0, 4) if part == 0 else range(4, 12)
        oc0, oc1 = (0, 16) if part == 0 else (16, 48)
        for s_ in srange:
            slot = nxt('wslab', 2)
            S.dma('sp', lambda e, s_=s_, slot=slot: e.dma_start(
                out=WSLAB[slot][:, :, :], in_=ada_s[l][:, s_ * 512:(s_ + 1) * 512].rearrange("(k p) c -> p k c", p=128)),
                keys_a, [('wslab', slot)], f'wslab{slot}')
            for o4 in range(4):
                oc = s_ * 4 + o4
                for kc in range(8):
                    S.op('pe', lambda e, slot=slot, o4=o4, kc=kc, oc=oc: e.matmul(
                        PS(0)[:, oc * 2:oc * 2 + 2], lhsT=WSLAB[slot][:, kc, o4 * 128:(o4 + 1) * 128],
                        rhs=SILC[:, kc, :], start=(kc == 0), stop=(kc == 7)),
                        [('wslab', slot), 'silc'], [('ps', 0)])
        mps = PS(0)[:, 0:96].rearrange("p (o w) -> p o w", w=2)
        mk = ('mod', l, part)
        for w in range(2):
            S.op('dve', lambda e, w=w: e.tensor_tensor(out=MOD[:, l, oc0:oc1, w], in0=mps[:, oc0:oc1, w],
                                                       in1=VEC[:, 16 + 48 * l + oc0:16 + 48 * l + oc1], op=ALU.add),
                 [('ps', 0), 'vec'], [mk])
        s0 = 8 if part == 0 else 32
        S.op('dve', lambda e: e.tensor_scalar(out=MOD[:, l, s0:s0 + 8, :], in0=MOD[:, l, s0:s0 + 8, :],
                                              scalar1=1.0, scalar2=None, op0=ALU.add), [mk], [mk])

    def modv(l, s_, kc, w):
        return MOD[:, l, s_ * 8 + kc, w:w + 1]

    def build_gate_bc(dst, l, s_, w, key):
        for half in range(2):
            bank = nxt('bcbank', 2)
            for k4 in range(4):
                kc = half * 4 + k4
                dslot = nxt('dg', 2)
                S.op('dve', lambda e, kc=kc, dslot=dslot: e.tensor_scalar(
                    out=DG[:, dslot, :], in0=ident, scalar1=modv(l, s_, kc, w), scalar2=None, op0=ALU.mult),
                    ['cst', ('mod', l, 1)], [('dg', dslot)])
                S.op('pe', lambda e, k4=k4, dslot=dslot, bank=bank: e.matmul(
                    PS(bank)[:, k4 * 128:(k4 + 1) * 128], lhsT=ONF[:, :], rhs=DG[:, dslot, :], start=True, stop=True),
                    ['onf', ('dg', dslot)], [('ps', bank)])
            S.op('act', lambda e, half=half, bank=bank: e.activation(
                out=dst[:, half * 512:(half + 1) * 512], in_=PS(bank), func=AF.Copy),
                [('ps', bank)], [key])

    def load_ln_bc(dst, row, key):
        S.dma('sp', lambda e: e.dma_start(out=dst.unsqueeze(1), in_=lnbc_d[row:row + 1, :].partition_broadcast(128)),
              [], [key], 'l_' + str(key[1]))

    BLOCKS = [([0, 1], 1)] + [([2 + 4 * i + j for j in range(4)], 0) for i in range(4)]

    def phase_p0(l):
        for tiles, w in BLOCKS:
            n = 128 * len(tiles)
            for kc in range(8):
                bank = nxt('p0bank', 4)
                for i, t in enumerate(tiles):
                    S.op('pe', lambda e, i=i, t=t, kc=kc, bank=bank: e.transpose(
                        out=PS(bank)[:, i * 128:(i + 1) * 128], in_=X[:, t, kc * 128:(kc + 1) * 128], identity=ident),
                        [('x', t), 'cst'], [('ps', bank)])
                S.op('act', lambda e, kc=kc, bank=bank, n=n, t0=tiles[0], w=w: e.activation(
                    out=hT_all[:, kc, t0 * 128:t0 * 128 + n], in_=PS(bank)[:, 0:n], func=AF.Identity,
                    bias=modv(l, 0, kc, w), scale=modv(l, 1, kc, w)),
                    [('ps', bank), ('mod', l, 0)], [('hT', tiles[0], kc)])

    PTF = PT[:, :, :].rearrange("p a b -> p (a b)").bitcast(F32)
    D_ = PTF[:, 0:256]
    SGF = SG[:, :, :].rearrange("p a b -> p (a b)")
    STSETS = [(STG[:, 0:512], STG[:, 512:1024], STG[:, 1024:1536], 0),
              (STG[:, 1536:2048], SGF[:, 0:512], SGF[:, 512:1024], 160)]

    def hv(ap, n):
        return ap.rearrange("p (h d) -> p h d", d=64)

    def post_qk(l, t, bank, cs, n, row0, gain):
        si_ = nxt('stset', 2)
        A_, B_, C_, so_ = STSETS[si_]
        kA, kB, kC, kS = f'stgA{si_}', f'stgB{si_}', f'stgC{si_}', f'sm{si_}'
        src = PS(bank)[:, cs:cs + n]
        NH = n // 64
        pk = ('ps', bank)
        if l == 0:
            S.op('act', lambda e: e.activation(out=C_[:, 0:n], in_=src, func=AF.Square), [pk], [kC])
            yield
            S.op('dve', lambda e: e.tensor_reduce(out=SM[:, so_:so_ + NH], in_=hv(C_[:, 0:n], n), axis=AX.X, op=ALU.add),
                 [kC], [kS])
            S.op('dve', lambda e: e.tensor_scalar(out=SM[:, so_:so_ + NH], in0=SM[:, so_:so_ + NH], scalar1=1.0 / 64, scalar2=EPS,
                                                  op0=ALU.mult, op1=ALU.add), [kS], [kS])
            yield
            S.op('act', lambda e: e.activation(out=SM[:, so_ + 16:so_ + 16 + NH], in_=SM[:, so_:so_ + NH], func=AF.Ln), [kS], [kS])
            S.op('act', lambda e: e.activation(out=SM[:, so_ + 32:so_ + 32 + NH], in_=SM[:, so_ + 16:so_ + 16 + NH], func=AF.Exp, scale=-0.5),
                 [kS], [kS])
            yield
            S.op('dve', lambda e: e.tensor_tensor(out=hv(A_[:, 0:n], n), in0=hv(src, n),
                                                  in1=SM[:, so_ + 32:so_ + 32 + NH].unsqueeze(2).to_broadcast([128, NH, 64]), op=ALU.mult),
                 [pk, kS], [kA])
            S.op('dve', lambda e: e.tensor_tensor(out=hv(A_[:, 0:n], n), in0=hv(A_[:, 0:n], n),
                                                  in1=GBC[:, gain, :].unsqueeze(1).to_broadcast([128, NH, 64]), op=ALU.mult),
                 [kA, 'gbc'], [kA])
            xin, xk = A_[:, 0:n], kA
        else:
            xin, xk = src, pk
        if t < 2:
            if l == 0:
                res, rk = A_[:, 0:n], kA
            else:
                S.op('act', lambda e: e.activation(out=B_[:, 0:n], in_=src, func=AF.Copy), [pk], [kB])
                res, rk = B_[:, 0:n], kB
        else:
            tt = t - 2
            S.op('dve', lambda e: e.tensor_tensor(out=hv(B_[:, 0:n], n), in0=hv(xin, n),
                                                  in1=ropeC[:, tt, :].unsqueeze(1).to_broadcast([128, NH, 64]), op=ALU.mult),
                 [xk, 'cst'], [kB])

            def v4(ap):
                return ap.rearrange("p (h a two s) -> p h a two s", a=2, two=2, s=16)
            sv = ropeS[:, tt, :].rearrange("p (a two s) -> p a two s", a=2, two=2)
            for two in range(2):
                S.op('dve', lambda e, two=two: e.tensor_tensor(
                    out=v4(C_[:, 0:n])[:, :, :, two, :], in0=v4(xin)[:, :, :, 1 - two, :],
                    in1=sv[:, :, two, :].unsqueeze(1).to_broadcast([128, NH, 2, 16]), op=ALU.mult),
                    [xk, 'cst'], [kC])
            S.op('dve', lambda e: e.tensor_tensor(out=B_[:, 0:n], in0=B_[:, 0:n], in1=C_[:, 0:n], op=ALU.add),
                 [kB, kC], [kB])
            res, rk = B_[:, 0:n], kB
        yield
        tb = 3 + nxt('p1tb', 2)
        for i in range(n // 128):
            S.op('pe', lambda e, i=i, tb=tb: e.transpose(out=PS(tb)[:, i * 128:(i + 1) * 128],
                                                         in_=res[:, i * 128:(i + 1) * 128], identity=ident),
                 [rk, 'cst'], [('ps', tb)])
        yield
        ts = nxt('tt', 2)
        S.op('act', lambda e, tb=tb, ts=ts: e.activation(out=TT[:, ts, 0:n], in_=PS(tb)[:, 0:n], func=AF.Copy),
             [('ps', tb)], [('tt', ts)])
        S.dma('sp', lambda e, ts=ts: e.dma_start(
            out=qkT_s[l][row0:row0 + n, t * 128:(t + 1) * 128].rearrange("(i p) c -> p i c", p=128),
            in_=TT[:, ts, 0:n].rearrange("p (i c) -> p i c", c=128)),
            [('tt', ts)], [('qkT', l, t, row0)], f'ttw{ts}', sbuf=False)

    def post_v(l, t, bank, cs, n, vc0):
        vs = nxt('vst', 2)
        S.op('act', lambda e: e.activation(out=VST[:, vs, 0:n], in_=PS(bank)[:, cs:cs + n], func=AF.Copy),
             [('ps', bank)], [('vst', vs)])
        S.dma('sp', lambda e: e.dma_start(out=v_s[l][t * 128:(t + 1) * 128, vc0:vc0 + n], in_=VST[:, vs, 0:n]),
              [('vst', vs)], [('vs', l, t, vc0)], f'vstw{vs}', sbuf=False)

    def post_u_T(t, bank, cs):
        S.op('act', lambda e: e.activation(out=D_[:, 0:256], in_=PS(bank)[:, cs:cs + 256], func=AF.Copy),
             [('ps', bank)], ['stgD'])
        tb = 3 + nxt('p1tb', 2)
        for c in range(2):
            S.op('pe', lambda e, c=c, tb=tb: e.transpose(out=PS(tb)[:, c * 128:(c + 1) * 128],
                                                         in_=D_[:, c * 128:(c + 1) * 128], identity=ident),
                 ['stgD', 'cst'], [('ps', tb)])
        return tb

    def uoff(t):
        return 8 + t * 128 if t < 2 else 280 + (t - 2) * 128

    def post_u0(t, bank, cs):
        tb = post_u_T(t, bank, cs)
        S.op('act', lambda e: e.activation(out=uT[:, :, uoff(t):uoff(t) + 128],
                                           in_=PS(tb)[:, 0:256].rearrange("p (c t) -> p c t", c=2), func=AF.Copy),
             [('ps', tb)], ['uT'])

    def post_ua(t, bank, cs):
        tb = post_u_T(t, bank, cs)
        S.op('act', lambda e: e.activation(out=aT[:, :, (t - 2) * 128:(t - 1) * 128],
                                           in_=PS(tb)[:, 0:256].rearrange("p (c t) -> p c t", c=2), func=AF.Copy),
             [('ps', tb)], [('aT', t)])

    zbf = RA[:, 0:4160].rearrange("p (c w) -> p c w", c=2)

    def post_ug(t, bank, cs):
        tb = post_u_T(t, bank, cs)
        S.op('act', lambda e: e.activation(out=PTF[:, 256:512], in_=PS(tb)[:, 0:256], func=AF.Sigmoid),
             [('ps', tb)], ['sg0'])
        o = 15 + (t - 2) * 128
        S.op('dve', lambda e: e.tensor_tensor(out=zbf[:, :, o:o + 128], in0=aT[:, :, (t - 2) * 128:(t - 1) * 128],
                                              in1=PTF[:, 256:512].rearrange("p (c t) -> p c t", c=2), op=ALU.mult),
             [('aT', t), 'sg0'], ['zbf'])

    INFL = []

    def pipe_add(gen):
        for g_ in list(INFL):
            try:
                next(g_)
            except StopIteration:
                INFL.remove(g_)
        while len(INFL) >= 2:
            g_ = INFL[0]
            try:
                next(g_)
            except StopIteration:
                INFL.remove(g_)
        INFL.append(gen)
        try:
            next(gen)
        except StopIteration:
            INFL.remove(gen)

    def pipe_flush():
        while INFL:
            for g_ in list(INFL):
                try:
                    next(g_)
                except StopIteration:
                    INFL.remove(g_)

    def segs_for(l, si):
        if l == 0:
            return [[('qk', 0, 512, (0, 0))],
                    [('qk', 0, 256, (512, 0)), ('qk', 256, 256, (768, 1))],
                    [('v', 0, 256, 0), ('u0', 256, 256, None)]][si]
        return [[('qk', 0, 512, (0, None))],
                [('qk', 0, 256, (512, None)), ('qk', 256, 256, (768, None))],
                [('qk', 0, 512, (1024, None))],
                [('v', 0, 512, 0)],
                [('v', 0, 256, 512), ('ua', 256, 256, None)],
                [('ug', 0, 256, None)]][si]

    def phase_p1(l):
        ncols = 1536 if l == 0 else 2816
        slabs = [(c0, min(512, ncols - c0)) for c0 in range(0, ncols, 512)]
        if l == 1:
            S.op('dve', lambda e: e.memset(zbf[:, :, :], 0.0), [], ['zbf'])
        need(f'wq{l}')
        for si, (c0, ncs) in enumerate(slabs):
            slot = nxt('wslab', 2)
            S.dma('sp', lambda e, c0=c0, ncs=ncs, slot=slot: e.dma_start(
                out=WSLAB[slot][:, :, 0:ncs], in_=wq_s[l][:, c0:c0 + ncs].rearrange("(k p) c -> p k c", p=128)),
                [f'wq{l}'], [('wslab', slot)], f'wslab{slot}')
            segs = segs_for(l, si)
            for t in range(NT):
                if t < 2 and all(k in ('ua', 'ug') for k, _, _, _ in segs):
                    continue
                bank = nxt('p1bank', 3)
                if t % 6 == 5:
                    pump()
                t0 = 0 if t < 2 else 2 + 4 * ((t - 2) // 4)
                for kc in range(8):
                    S.op('pe', lambda e, kc=kc, t=t, bank=bank, slot=slot, ncs=ncs: e.matmul(
                        PS(bank)[:, 0:ncs], lhsT=hT_all[:, kc, t * 128:(t + 1) * 128], rhs=WSLAB[slot][:, kc, 0:ncs],
                        start=(kc == 0), stop=(kc == 7)),
                        [('hT', t0, kc), ('wslab', slot)], [('ps', bank)])
                for kind, cs, n, ex in segs:
                    if kind == 'qk':
                        pipe_add(post_qk(l, t, bank, cs, n, ex[0], ex[1]))
                    elif kind == 'v':
                        post_v(l, t, bank, cs, n, ex)
                    elif kind == 'u0':
                        post_u0(t, bank, cs)
                    elif kind == 'ua' and t >= 2:
                        post_ua(t, bank, cs)
                    elif kind == 'ug' and t >= 2:
                        post_ug(t, bank, cs)
            pipe_flush()

    def phase_pool():
        T1 = [RAF[:, 0:UW], RAF[:, UW:2 * UW]]
        pooled = RA[:, 9472:9472 + 2 * TOK].rearrange("p (c t) -> p c t", c=2)
        segsP = [(8, 0, 256), (280, 256, 2048)]

        def emit_group(c, hh, w, Aw):
            P0_, P1_ = hh * 64, hh * 64 + 64
            hw = w // 2
            for (ps_, ts_, L) in segsP:
                S.op('dve', lambda e, ps_=ps_, ts_=ts_, L=L: e.scalar_tensor_tensor(
                    out=pooled[P0_:P1_, c, ts_:ts_ + L], in0=Aw[P0_:P1_, ps_ - hw:ps_ - hw + L], scalar=1.0 / w,
                    in1=uT[P0_:P1_, c, ps_:ps_ + L], op0=ALU.mult, op1=ALU.subtract),
                    ['uT', 'poolA'], ['pooled'])
                for side in range(2):
                    po = ps_ if side == 0 else ps_ + L - 8
                    to = ts_ if side == 0 else ts_ + L - 8
                    S.op('dve', lambda e, po=po, side=side: e.tensor_tensor(
                        out=SM[P0_:P1_, 64:72], in0=Aw[P0_:P1_, po - hw:po - hw + 8], in1=pedge[P0_:P1_, c, side, :], op=ALU.mult),
                        ['poolA', 'cst'], ['sm_pe'])
                    S.op('dve', lambda e, po=po, to=to: e.tensor_tensor(
                        out=pooled[P0_:P1_, c, to:to + 8], in0=SM[P0_:P1_, 64:72], in1=uT[P0_:P1_, c, po:po + 8], op=ALU.subtract),
                        ['sm_pe', 'uT'], ['pooled'])

        for c in range(2):
            src = uT[:, c, :]
            cur = 0
            S.op('dve', lambda e, src=src: e.tensor_tensor(out=T1[0][:, 0:UW - 1], in0=src[:, 0:UW - 1], in1=src[:, 1:UW], op=ALU.add),
                 ['uT', 'pooled'], ['poolA'])
            if c == 0:
                emit_group(0, 0, 2, T1[0])
            S.op('dve', lambda e: e.tensor_tensor(out=T1[1][:, 0:UW - 3], in0=T1[0][:, 0:UW - 3], in1=T1[0][:, 2:UW - 1], op=ALU.add),
                 ['poolA'], ['poolA'])
            if c == 0:
                emit_group(0, 1, 4, T1[1])
            else:
                S.op('dve', lambda e: e.tensor_tensor(out=T1[0][:, 0:UW - 7], in0=T1[1][:, 0:UW - 7], in1=T1[1][:, 4:UW - 3], op=ALU.add),
                     ['poolA'], ['poolA'])
                emit_group(1, 0, 8, T1[0])
                S.op('dve', lambda e: e.tensor_tensor(out=T1[1][:, 0:UW - 15], in0=T1[0][:, 0:UW - 15], in1=T1[0][:, 8:UW - 7], op=ALU.add),
                     ['poolA'], ['poolA'])
                emit_group(1, 1, 16, T1[1])
        for c in range(2):
            for t0 in range(0, TOK, 512):
                n = min(512, TOK - t0)
                bank = nxt('p2bank', 2)
                S.op('pe', lambda e, c=c, t0=t0, n=n, bank=bank: e.matmul(
                    PS(bank)[:, 0:n], lhsT=BD[:, c, :], rhs=pooled[:, c, t0:t0 + n], start=True, stop=True),
                    ['bd', 'pooled'], [('ps', bank)])
                S.op('act', lambda e, c=c, t0=t0, n=n, bank=bank: e.activation(
                    out=catT[:, 6 + c, t0:t0 + n], in_=PS(bank)[:, 0:n], func=AF.Identity, scale=VEC[:, 112 + c:113 + c]),
                    [('ps', bank), 'vec'], [('cat', 6 + c)])

    def phase_conv():
        DIAG = RA[:, 4160:4160 + 62 * 128].rearrange("p (j m) -> p j m", m=128)
        sts = RAF[:, 6048:6048 + 2048]
        MEAN, VAR, TMP = sts[:, 0:512], sts[:, 512:1024], sts[:, 1024:1536]
        Y = STG[:, 0:1024].rearrange("p (c t) -> p c t", c=2)
        YSQ = STG[:, 1024:2048].rearrange("p (c t) -> p c t", c=2)
        for j in range(31):
            for c in range(2):
                S.op('dve', lambda e, j=j, c=c: e.tensor_scalar(
                    out=DIAG[:, 2 * j + c, :], in0=IDB[:, :], scalar1=VEC[:, 125 + 2 * j + c:126 + 2 * j + c],
                    scalar2=None, op0=ALU.mult), ['idb', 'vec'], ['diag'])
        for b in range(4):
            for c in range(2):
                bank = c
                for j in range(31):
                    S.op('pe', lambda e, j=j, c=c, b=b, bank=bank: e.matmul(
                        PS(bank), lhsT=DIAG[:, 2 * j + c, :], rhs=zbf[:, c, b * 512 + j:b * 512 + j + 512],
                        start=(j == 0), stop=(j == 30)), ['diag', 'zbf'], [('ps', bank)])
                S.op('act', lambda e, c=c, bank=bank: e.activation(
                    out=Y[:, c, :], in_=PS(bank), func=AF.Identity, bias=VEC[:, 114 + c:115 + c]),
                    [('ps', bank), 'vec'], [('cy', c)])
                S.op('dve', lambda e, c=c: e.tensor_tensor(out=YSQ[:, c, :], in0=Y[:, c, :], in1=Y[:, c, :], op=ALU.mult),
                     [('cy', c)], [('cysq', c)])
            for c in range(2):
                S.op('pe', lambda e, c=c: e.matmul(PS(2), lhsT=ONF[:, :], rhs=Y[:, c, :], start=(c == 0), stop=(c == 1)),
                     ['onf', ('cy', c)], [('ps', 2)])
            for c in range(2):
                S.op('pe', lambda e, c=c: e.matmul(PS(3), lhsT=ONF[:, :], rhs=YSQ[:, c, :], start=(c == 0), stop=(c == 1)),
                     ['onf', ('cysq', c)], [('ps', 3)])
            S.op('dve', lambda e: e.tensor_scalar(out=MEAN, in0=PS(2), scalar1=1.0 / 256, scalar2=None, op0=ALU.mult),
                 [('ps', 2)], ['cmean'])
            S.op('dve', lambda e: e.tensor_tensor(out=TMP, in0=MEAN, in1=MEAN, op=ALU.mult), ['cmean'], ['ctmp'])
            S.op('dve', lambda e: e.scalar_tensor_tensor(out=VAR, in0=PS(3), scalar=1.0 / 256, in1=TMP,
                                                         op0=ALU.mult, op1=ALU.subtract), [('ps', 3), 'ctmp'], ['cvar'])
            S.op('dve', lambda e: e.tensor_scalar(out=VAR, in0=VAR, scalar1=EPS, scalar2=None, op0=ALU.add), ['cvar'], ['cvar'])
            S.op('act', lambda e: e.activation(out=TMP, in_=VAR, func=AF.Ln), ['cvar'], ['ctmp'])
            S.op('act', lambda e: e.activation(out=VAR, in_=TMP, func=AF.Exp, scale=-0.5), ['ctmp'], ['cvar'])
            for c in range(2):
                S.op('dve', lambda e, c=c: e.tensor_tensor(out=Y[:, c, :], in0=Y[:, c, :], in1=MEAN, op=ALU.subtract),
                     [('cy', c), 'cmean'], [('cy', c)])
                S.op('dve', lambda e, c=c: e.tensor_tensor(out=Y[:, c, :], in0=Y[:, c, :], in1=VAR, op=ALU.mult),
                     [('cy', c), 'cvar'], [('cy', c)])
                S.op('act', lambda e, c=c, b=b: e.activation(
                    out=catT[:, 6 + c, 256 + b * 512:256 + (b + 1) * 512], in_=Y[:, c, :], func=AF.Silu,
                    bias=VEC[:, 118 + c:119 + c], scale=VEC[:, 116 + c:117 + c]),
                    [('cy', c), 'vec'], [('cat', 6 + c)])

    def att_views(l, slot):
        base = slot * 8192
        Q = RA[:, base:base + TOK]
        K = RA[:, base + TOK:base + 2 * TOK]
        V = RA[:, base + 4608:base + 4608 + 18 * 128].rearrange("p (c d) -> p c d", d=128)
        return Q, K, V, V

    def phase_att(l):
        if l == 0:
            pass
        else:
            S.op('dve', lambda e: e.tensor_tensor(out=LAM[:, 0:2].unsqueeze(2), in0=VEC[:, 121:125].rearrange("p (a b) -> p a b", b=2)[:, :, 0:1],
                                                  in1=VEC[:, 121:125].rearrange("p (a b) -> p a b", b=2)[:, :, 1:2], op=ALU.mult),
                 ['vec'], ['lam'])
            S.op('pe', lambda e: e.matmul(PS(0)[:, 0:2], lhsT=ONF[:, :], rhs=LAM[:, 0:2], start=True, stop=True),
                 ['onf', 'lam'], [('ps', 0)])
            S.op('act', lambda e: e.activation(out=LAM[:, 2:4], in_=PS(0)[:, 0:2], func=AF.Exp), [('ps', 0)], ['lam'])
            S.op('dve', lambda e: e.scalar_tensor_tensor(out=LAM[:, 4:5], in0=LAM[:, 3:4], scalar=-LAM_INIT1, in1=LAM[:, 2:3],
                                                         op0=ALU.add, op1=ALU.subtract), ['lam'], ['lam'])
            S.op('dve', lambda e: e.tensor_scalar(out=LAM[:, 5:6], in0=VEC[:, 120:121], scalar1=1.0 - LAM_INIT1, scalar2=None,
                                                  op0=ALU.mult), ['vec'], ['lam'])
        nunits = 6

        def load_unit(u):
            slot = u % 2
            Q, K, vA, vB = att_views(l, slot)
            rk = [('qkT', l, t, r) for t in range(NT) for r in ()]
            wk = [('att', slot)]
            deps = [k for k in S.lastw.keys() if isinstance(k, tuple) and k[0] in ('qkT', 'vs') and k[1] == l]
            sk = f'att{slot}'
            if l == 0:
                hA, hB = 2 * u, 2 * u + 1
                gA, gB = hA // 3, hB // 3
                for half, h, g in ((0, hA, gA), (1, hB, gB)):
                    S.dma('sp', lambda e, half=half, h=h: e.dma_start(out=Q[half * 64:(half + 1) * 64, :],
                                                                     in_=qkT_s[0][h * 64:(h + 1) * 64, :]), deps, wk, sk)
                    S.dma('sp', lambda e, half=half, g=g: e.dma_start(out=K[half * 64:(half + 1) * 64, :],
                                                                     in_=qkT_s[0][768 + g * 64:768 + (g + 1) * 64, :]), deps, wk, sk)
                S.dma('sp', lambda e: e.dma_start(out=vA[:, :, 0:64],
                                                  in_=v_s[0][:, gA * 64:(gA + 1) * 64].rearrange("(c p) d -> p c d", p=128)), deps, wk, sk)
                S.dma('sp', lambda e: e.dma_start(out=vA[:, :, 64:128],
                                                  in_=v_s[0][:, gB * 64:(gB + 1) * 64].rearrange("(c p) d -> p c d", p=128)), deps, wk, sk)
            else:
                h = u
                S.dma('sp', lambda e: e.dma_start(out=Q[:, :], in_=qkT_s[1][h * 128:(h + 1) * 128, :]), deps, wk, sk)
                S.dma('sp', lambda e: e.dma_start(out=K[:, :], in_=qkT_s[1][768 + h * 128:768 + (h + 1) * 128, :]), deps, wk, sk)
                S.dma('sp', lambda e: e.dma_start(out=vA[:, :, :],
                                                  in_=v_s[1][:, h * 128:(h + 1) * 128].rearrange("(c p) d -> p c d", p=128)), deps, wk, sk)

        qblocks = ([(0, 256, [0, 1])] if l == 0 else []) + [(256 + qb * 512, 512, list(range(18))) for qb in range(4)]
        load_unit(0)
        for u in range(nunits):
            if u + 1 < nunits:
                load_unit(u + 1)
            slot = u % 2
            Q, K, vA, vB = att_views(l, slot)
            ak = ('att', slot)
            for (q0_, n_, chunks_) in qblocks:
                pump()
                att_block(l, u, Q, K, vA, ak, q0_, n_, chunks_)

    def att_block(l, u, Q, K, vA, ak, q0, n, chunks):
            if True:
                def qk(c, ci):
                    for i in range(2):
                        bank = 2 * i + (ci % 2)
                        S.op('pe', lambda e, i=i, c=c, bank=bank: e.matmul(
                            PS(bank)[:, 0:n], lhsT=K[i * 64:(i + 1) * 64, c * 128:(c + 1) * 128],
                            rhs=Q[i * 64:(i + 1) * 64, q0:q0 + n], start=True, stop=True), [ak], [('ps', bank)])

                def ex(c, ci):
                    for i in range(2):
                        bank = 2 * i + (ci % 2)
                        ps_ = (2 * ci + i) % 4
                        S.op('act', lambda e, bank=bank, ps_=ps_: e.activation(
                            out=PT[:, ps_, 0:n], in_=PS(bank)[:, 0:n], func=AF.Exp, scale=SCALE),
                            [('ps', bank)], [('pt', ps_)])

                def pv(c, ci):
                    first, last = (ci == 0), (ci == len(chunks) - 1)
                    for i in range(2):
                        ps_ = (2 * ci + i) % 4
                        if False:
                            pass
                        else:
                            S.op('pe', lambda e, c=c, ps_=ps_, i=i: e.matmul(PS(4 + i)[:, 0:n], lhsT=vA[:, c, :], rhs=PT[:, ps_, 0:n],
                                                                             start=first, stop=last), [ak, ('pt', ps_)], [('ps', 4 + i)])
                            S.op('pe', lambda e, ps_=ps_, i=i: e.matmul(PS(6 + i)[:, 0:n], lhsT=ONB[:, :], rhs=PT[:, ps_, 0:n],
                                                                        start=first, stop=last), ['onb', ('pt', ps_)], [('ps', 6 + i)])

                nch = len(chunks)
                qk(chunks[0], 0)
                ex(chunks[0], 0)
                for ci in range(nch):
                    if ci + 1 < nch:
                        qk(chunks[ci + 1], ci + 1)
                        ex(chunks[ci + 1], ci + 1)
                    pv(chunks[ci], ci)
                OSs = [STG[:, 0:512], STG[:, 512:1024]]
                DSs = [STG[:, 1024:1536], STG[:, 1536:2048]]
                for i in range(2):
                    r0, r1 = (i * 64, i * 64 + 64) if l == 0 else (0, 128)
                    S.op('act', lambda e, i=i, r0=r0, r1=r1: e.activation(out=OSs[i][r0:r1, 0:n], in_=PS(4 + i)[r0:r1, 0:n], func=AF.Copy),
                         [('ps', 4 + i)], [('os', i)])
                    S.op('dve', lambda e, i=i, r0=r0, r1=r1: e.tensor_copy(out=DSs[i][r0:r1, 0:n], in_=PS(6 + i)[r0:r1, 0:n]),
                         [('ps', 6 + i)], [('ds', i)])
                for i in range(2):
                    r0, r1 = (i * 64, i * 64 + 64) if l == 0 else (0, 128)
                    S.op('dve', lambda e, i=i, r0=r0, r1=r1: e.reciprocal(out=DSs[i][r0:r1, 0:n], in_=DSs[i][r0:r1, 0:n]),
                         [('ds', i)], [('ds', i)])
                    if l == 0:
                        S.op('dve', lambda e, i=i, r0=r0, r1=r1: e.tensor_tensor(
                            out=catT[r0:r1, u, q0:q0 + n], in0=OSs[i][r0:r1, 0:n], in1=DSs[i][r0:r1, 0:n], op=ALU.mult),
                            [('os', i), ('ds', i)], [('cat', u)])
                    else:
                        S.op('dve', lambda e, i=i: e.tensor_tensor(out=OSs[i][:, 0:n], in0=OSs[i][:, 0:n], in1=DSs[i][:, 0:n], op=ALU.mult),
                             [('os', i), ('ds', i)], [('os', i)])
                if l == 1:
                    T0, T1_ = OSs[0], OSs[1]
                    R1 = DSs[1]
                    SQB = TT[:, 0, :]
                    S.op('dve', lambda e: e.scalar_tensor_tensor(out=T0[:, 0:n], in0=T1_[:, 0:n], scalar=LAM[:, 4:5], in1=T0[:, 0:n],
                                                                 op0=ALU.mult, op1=ALU.add), [('os', 0), ('os', 1), 'lam'], [('os', 0)])
                    S.op('dve', lambda e: e.tensor_tensor(out=SQB[:, 0:n], in0=T0[:, 0:n], in1=T0[:, 0:n], op=ALU.mult),
                         [('os', 0)], [('tt', 0)])
                    S.op('pe', lambda e: e.matmul(PS(0)[:, 0:n], lhsT=ONB[:, :], rhs=SQB[:, 0:n], start=True, stop=True),
                         ['onb', ('tt', 0)], [('ps', 0)])
                    S.op('act', lambda e: e.activation(out=R1[:, 0:n], in_=PS(0)[:, 0:n], func=AF.Ln, scale=1.0 / 128, bias=SM[:, 80:81]),
                         [('ps', 0), 'sm_eps'], [('ds', 1)])
                    S.op('act', lambda e: e.activation(out=R1[:, 0:n], in_=R1[:, 0:n], func=AF.Exp, scale=-0.5), [('ds', 1)], [('ds', 1)])
                    S.op('dve', lambda e: e.scalar_tensor_tensor(out=catT[:, u, q0:q0 + n], in0=T0[:, 0:n], scalar=LAM[:, 5:6], in1=R1[:, 0:n],
                                                                 op0=ALU.mult, op1=ALU.mult), [('os', 0), ('ds', 1), 'lam'], [('cat', u)])

    def ln_update(t, Y, gbc, gkey, lng, lnb, lnkeys, ykeys):
        k = nxt('lnstg', 2)
        stg = STG[:, k * 1024:(k + 1) * 1024]
        sk = ('lnstg', k)
        so = 96 + k * 32
        stats = SM[:, so:so + 12]
        mv = SM[:, so + 12:so + 14]
        lnv = SM[:, so + 14:so + 15]
        rstd = SM[:, so + 15:so + 16]
        nb = SM[:, so + 16:so + 17]
        smk = ('lnsm', k)
        xk = ('x', t)
        S.op('dve', lambda e: e.tensor_tensor(out=stg, in0=Y, in1=gbc, op=ALU.mult), list(ykeys) + [gkey], [sk])
        S.op('dve', lambda e: e.scalar_tensor_tensor(out=X[:, t, :], in0=X[:, t, :], scalar=ALPHA, in1=stg,
                                                     op0=ALU.mult, op1=ALU.add), [xk, sk], [xk])
        for hh in range(2):
            S.op('dve', lambda e, hh=hh: e.bn_stats(out=stats[:, hh * 6:(hh + 1) * 6], in_=X[:, t, hh * 512:(hh + 1) * 512]),
                 [xk], [smk])
        S.op('dve', lambda e: e.bn_aggr(out=mv, in_=stats), [smk], [smk])
        S.op('act', lambda e: e.activation(out=lnv, in_=mv[:, 1:2], func=AF.Ln, bias=SM[:, 80:81]), [smk, 'sm_eps'], [smk])
        S.op('act', lambda e: e.activation(out=rstd, in_=lnv, func=AF.Exp, scale=-0.5), [smk], [smk])
        S.op('dve', lambda e: e.scalar_tensor_tensor(out=nb, in0=mv[:, 0:1], scalar=-1.0, in1=rstd, op0=ALU.mult, op1=ALU.mult),
             [smk], [smk])
        S.op('act', lambda e: e.activation(out=stg, in_=X[:, t, :], func=AF.Identity, bias=nb, scale=rstd), [xk, smk], [sk])
        S.op('dve', lambda e: e.tensor_tensor(out=stg, in0=stg, in1=lng, op=ALU.mult), [sk, lnkeys[0]], [sk])
        S.op('dve', lambda e: e.tensor_tensor(out=X[:, t, :], in0=stg, in1=lnb, op=ALU.add), [sk, lnkeys[1]], [xk])

    def phase_p4(l):
        need(f'wo{l}')
        S.dma('sp', lambda e: e.dma_start(out=WOUT[:, :, :], in_=wo_s[l].rearrange("(k p) c -> p k c", p=128)),
              [f'wo{l}'], ['wout'], 'wout')
        build_gate_bc(BCT[0], l, 2, 0, ('bct', 0))
        if l == 0:
            build_gate_bc(BCT[1], l, 2, 1, ('bct', 1))
        load_ln_bc(BCT[2], l * 4 + 0, ('bct', 2))
        load_ln_bc(BCT[3], l * 4 + 1, ('bct', 3))
        tiles = range(NT) if l == 0 else range(2, NT)
        for t in tiles:
            if t % 3 == 0:
                pump()
            p4_tile(l, t)

    def p4_tile(l, t):
        if True:
            pb = nxt('p4pair', 2)
            w = 1 if t < 2 else 0
            for nh in range(2):
                bank = 2 * pb + nh
                for kc in range(8):
                    S.op('pe', lambda e, kc=kc, nh=nh, bank=bank: e.matmul(
                        PS(bank), lhsT=catT[:, kc, t * 128:(t + 1) * 128], rhs=WOUT[:, kc, nh * 512:(nh + 1) * 512],
                        start=(kc == 0), stop=(kc == 7)), [('cat', kc), 'wout'], [('ps', bank)])
            ln_update(t, PSB[pb][:, :], BCT[w], ('bct', w), BCT[2], BCT[3], [('bct', 2), ('bct', 3)],
                      [('ps', 2 * pb), ('ps', 2 * pb + 1)])

    def phase_p5(l, last):
        need(f'fi{l}')
        need(f'fo{l}')
        S.dma('sp', lambda e: e.dma_start(out=WO[:, 0:11, :], in_=fo_s[l][0:1408, :].rearrange("(j p) c -> p j c", p=128)),
              [f'fo{l}'], ['wo'], 'wo')
        S.dma('sp', lambda e: e.dma_start(out=WO[:, 11:22, :], in_=fo_s[l][1408:2816, :].rearrange("(j p) c -> p j c", p=128)),
              [f'fo{l}'], ['wo'], 'wo')
        build_gate_bc(BCT[0], l, 5, 0, ('bct', 0))
        if l == 0:
            build_gate_bc(BCT[1], l, 5, 1, ('bct', 1))
        load_ln_bc(BCT[2], l * 4 + 2, ('bct', 2))
        load_ln_bc(BCT[3], l * 4 + 3, ('bct', 3))
        sbs = BLOCKS if l == 0 else BLOCKS[1:]
        for tiles, w in sbs:
            p5_sb(l, last, tiles, w)

    def p5_sb(l, last, tiles, w):
        if True:
            n = 128 * len(tiles)
            for kc in range(8):
                bank = 4 + nxt('p5tb', 4)
                for i, t in enumerate(tiles):
                    S.op('pe', lambda e, i=i, t=t, kc=kc, bank=bank: e.transpose(
                        out=PS(bank)[:, i * 128:(i + 1) * 128], in_=X[:, t, kc * 128:(kc + 1) * 128], identity=ident),
                        [('x', t), 'cst'], [('ps', bank)])
                S.op('act', lambda e, kc=kc, bank=bank: e.activation(
                    out=h2T[:, kc, 0:n], in_=PS(bank)[:, 0:n], func=AF.Identity, bias=modv(l, 3, kc, w), scale=modv(l, 4, kc, w)),
                    [('ps', bank), ('mod', l, 1)], [('h2T', kc)])
            for jp in range(11):
                if jp % 4 == 0:
                    pump()
                slot = nxt('fring', 2)
                sa, sg_ = FRING[slot]
                S.dma('sp', lambda e, jp=jp, sa=sa: e.dma_start(
                    out=sa[:, :, :], in_=fi_s[l][:, jp * 256:(jp + 1) * 256].rearrange("(k p) c -> p k c", p=128)),
                    [f'fi{l}'], [('fring', slot)], f'fring{slot}')
                S.dma('sp', lambda e, jp=jp, sg_=sg_: e.dma_start(
                    out=sg_[:, :, :], in_=fi_s[l][:, FFN_H + jp * 256:FFN_H + (jp + 1) * 256].rearrange("(k p) c -> p k c", p=128)),
                    [f'fi{l}'], [('fring', slot)], f'fring{slot}')
                for jj in range(2):
                    j = 2 * jp + jj
                    pr = nxt('p5ab', 2)
                    ba, bg = 2 * pr, 2 * pr + 1
                    for (bank, slab) in ((ba, sa), (bg, sg_)):
                        for kc in range(8):
                            S.op('pe', lambda e, kc=kc, bank=bank, slab=slab, jj=jj: e.matmul(
                                PS(bank)[:, 0:n], lhsT=slab[:, kc, jj * 128:(jj + 1) * 128], rhs=h2T[:, kc, 0:n],
                                start=(kc == 0), stop=(kc == 7)), [('fring', slot), ('h2T', kc)], [('ps', bank)])
                    sgs = nxt('sgs', 2)
                    S.op('act', lambda e, bg=bg, sgs=sgs: e.activation(out=SG[:, sgs, 0:n], in_=PS(bg)[:, 0:n], func=AF.Silu),
                         [('ps', bg)], [('sg', sgs)])
                    S.op('dve', lambda e, ba=ba, sgs=sgs, j=j: e.tensor_tensor(out=hidT[:, j, 0:n], in0=PS(ba)[:, 0:n],
                                                                              in1=SG[:, sgs, 0:n], op=ALU.mult),
                         [('ps', ba), ('sg', sgs)], [('hid', j)])
            for ti, t in enumerate(tiles):
                pb = 2 + nxt('p5pair', 2)
                for nh in range(2):
                    bank = 2 * pb + nh
                    for j in range(22):
                        S.op('pe', lambda e, j=j, nh=nh, bank=bank, ti=ti: e.matmul(
                            PS(bank), lhsT=hidT[:, j, ti * 128:(ti + 1) * 128], rhs=WO[:, j, nh * 512:(nh + 1) * 512],
                            start=(j == 0), stop=(j == 21)), [('hid', j), 'wo'], [('ps', bank)])
                ln_update(t, PSB[pb][:, :], BCT[w], ('bct', w), BCT[2], BCT[3], [('bct', 2), ('bct', 3)],
                          [('ps', 2 * pb), ('ps', 2 * pb + 1)])
                if last:
                    S.dma('sp', lambda e, t=t: e.dma_start(out=out_d[(t - 2) * 128:(t - 1) * 128, :], in_=X[:, t, :]),
                          [('x', t)], [('out', t)], 'outw', sbuf=False)

    S.op('dve', lambda e: e.memset(SM[:, 80:81], EPS), [], ['sm_eps'])

    def dump_all():
        S.barrier()
        for t in range(NT):
            S.dma('sp', lambda e, t=t: e.dma_start(out=dbg_d[t * 128:(t + 1) * 128, :], in_=X[:, t, :]),
                  [('x', t)], [('dbg', t)], 'dbgw', sbuf=False)
        S.dma('sp', lambda e: e.dma_start(out=dbg_hb, in_=HB[:, 0:18432]), [], ['dbg_hb'], 'dbgw', sbuf=False)
        S.dma('sp', lambda e: e.dma_start(out=dbg_mod, in_=MOD[:, :, :, :].rearrange("p l o w -> p (l o w)")), [], ['dbg_mod'], 'dbgw', sbuf=False)
        S.dma('sp', lambda e: e.dma_start(out=dbg_r3, in_=R3[:, :]), [], ['dbg_r3'], 'dbgw', sbuf=False)

    def run_all():
        for l in range(2):
            compute_mod(l, 0)
            S.barrier()
            if stop == f'mod{l}':
                return
            phase_p0(l)
            if stop == f'p0{l}':
                return
            if l == 0:
                S.op('dve', lambda e: e.memset(R3[:, :], 0.0), [], ['uT'])
            phase_p1(l)
            S.barrier()
            compute_mod(l, 1)
            if stop == f'p1{l}':
                return
            if l == 0:
                phase_pool()
            else:
                phase_conv()
            S.barrier()
            if stop == f'p2{l}':
                return
            phase_att(l)
            S.barrier()
            if stop == f'p3{l}':
                return
            phase_p4(l)
            S.barrier()
            if stop == f'p4{l}':
                return
            phase_p5(l, last=(l == 1))
            S.barrier()
            if stop == f'p5{l}':
                return
    run_all()
    if debug:
        dump_all()
    S.wait_all('sp', ['outw', 'dbgw'])

    with nc.Block() as block:
        S.emit(block)
    es.close()
    return nc


def _rope_tables():
    n = np.arange(2048)
    row = (n // 64).astype(np.float32)
    col = (n % 64).astype(np.float32)
    freqs = (np.float32(10000.0) ** (-np.arange(0, 32, 2, dtype=np.float32) / np.float32(32))).astype(np.float32)
    ang = np.concatenate([row[:, None] * freqs, col[:, None] * freqs], axis=-1).astype(np.float32)
    cos, sin = np.cos(ang).astype(np.float32), np.sin(ang).astype(np.float32)
    cr, cc, sr, sc = cos[:, :16], cos[:, 16:], sin[:, :16], sin[:, 16:]
    C = np.concatenate([cr, cr, cc, cc], axis=-1)
    Sg = np.concatenate([-sr, sr, -sc, sc], axis=-1)
    C = C.reshape(16, 128, 64).transpose(1, 0, 2)
    Sg = Sg.reshape(16, 128, 64).transpose(1, 0, 2)
    return np.ascontiguousarray(C), np.ascontiguousarray(Sg)


def _consts():
    cst = np.zeros((128, NCST), np.float32)
    cst[:, C_ID:C_ID + 128] = np.eye(128, dtype=np.float32)
    C, Sg = _rope_tables()
    cst[:, C_RC:C_RC + 1024] = C.reshape(128, 1024)
    cst[:, C_RS:C_RS + 1024] = Sg.reshape(128, 1024)
    pe = np.zeros((128, 2, 2, 8), np.float32)
    for p in range(128):
        for c in range(2):
            w = POOL_W[2 * c + p // 64]
            for i in range(8):
                pe[p, c, 0, i] = 1.0 / ((i + w // 2) - max(i - w // 2, 0))
                pe[p, c, 1, i] = 1.0 / min(w, 8 - i + w // 2)
    cst[:, C_PE:C_PE + 32] = pe.reshape(128, 32)
    return cst


def _colvec(v):
    v = np.asarray(v, np.float32).reshape(-1, 128)
    return v.T


_NC_CACHE = {}


def kernel(x, c, ctx, c_ctx, ab_w_in, ab_q_gain, ab_k_gain, ab_w_pool, ab_pool_scale, ab_w_out,
           cd_w_in, cd_lambda_q1, cd_lambda_k1, cd_lambda_q2, cd_lambda_k2, cd_subln_gain,
           cd_conv_w, cd_conv_b, cd_conv_ln_g, cd_conv_ln_b, cd_w_out,
           ada_w, ada_b, ln1_g, ln1_b, ln2_g, ln2_b, ffn_w_in, ffn_w_out, _debug=False, _stop=None):
    f = lambda a: np.ascontiguousarray(np.asarray(a, dtype=np.float32))
    x, c, ctx, c_ctx = f(x), f(c), f(ctx), f(c_ctx)
    if (_debug, _stop) not in _NC_CACHE:
        _NC_CACHE[(_debug, _stop)] = build_program(debug=_debug, stop=_stop)
    nc = _NC_CACHE[(_debug, _stop)]
    cst = _consts()
    lnbc = np.stack([f(ln1_g)[0], f(ln1_b)[0], f(ln2_g)[0], f(ln2_b)[0],
                     f(ln1_g)[1], f(ln1_b)[1], f(ln2_g)[1], f(ln2_b)[1]], axis=0)
    gains = np.stack([f(ab_q_gain)[0], f(ab_k_gain)[0]], axis=0)
    shared = {
        "consts": cst, "lnbc": np.ascontiguousarray(lnbc), "gains": np.ascontiguousarray(gains),
        "ab_w_in": f(ab_w_in)[0], "cd_w_in": f(cd_w_in)[0], "ab_w_out": f(ab_w_out)[0], "cd_w_out": f(cd_w_out)[0],
        "ab_w_pool": f(ab_w_pool)[0].reshape(256, 64), "ada_w": f(ada_w).reshape(2 * D, 6 * D),
        "ffn_w_in": f(ffn_w_in).reshape(2 * D, 2 * FFN_H), "ffn_w_out": f(ffn_w_out).reshape(2 * FFN_H, D),
    }
    in_maps = []
    for b in range(NCORES):
        vecs = np.zeros((128, NV), np.float32)
        vecs[:, 0:8] = _colvec(c[b])
        vecs[:, 8:16] = _colvec(c_ctx)
        vecs[:, 16:64] = _colvec(f(ada_b)[0])
        vecs[:, 64:112] = _colvec(f(ada_b)[1])
        vecs[:, 112:114] = _colvec(f(ab_pool_scale)[0])
        vecs[:, 114:116] = _colvec(f(cd_conv_b)[0])
        vecs[:, 116:118] = _colvec(f(cd_conv_ln_g)[0])
        vecs[:, 118:120] = _colvec(f(cd_conv_ln_b)[0])
        vecs[:, 120:121] = _colvec(f(cd_subln_gain)[0])
        vecs[0:64, 121] = f(cd_lambda_q1)[0]
        vecs[0:64, 122] = f(cd_lambda_k1)[0]
        vecs[0:64, 123] = f(cd_lambda_q2)[0]
        vecs[0:64, 124] = f(cd_lambda_k2)[0]
        cw = f(cd_conv_w)[0]
        for j in range(31):
            vecs[:, 125 + 2 * j:127 + 2 * j] = _colvec(cw[j])
        m = dict(shared)
        m["x"] = x[b]
        m["ctx"] = ctx[b]
        m["vecs"] = vecs
        in_maps.append(m)
    res = run_bass_kernel_spmd(nc, in_maps, core_ids=list(range(NCORES)))
    out = np.stack([np.asarray(r["out"], dtype=np.float32) for r in res.results], axis=0)
    if _debug:
        return out, res.results
    return out
```

```python
import math
from contextlib import ExitStack

import numpy as np
import concourse.bass as bass
import concourse.mybir as mybir
from concourse.bass_utils import run_bass_kernel_spmd

F32 = mybir.dt.float32
BF16 = mybir.dt.bfloat16
AF = mybir.ActivationFunctionType
ALU = mybir.AluOpType
AX = mybir.AxisListType

NCORES = 8
D = 1024
NT = 18
TOK = 2304
ALPHA = 4.0 ** 0.25
EPS = 1e-6
FFN_H = 2816
LAM_INIT1 = 0.8 - 0.6 * math.exp(-0.3)
POOL_W = (2, 4, 8, 16)
NV = 192
C_ID = 0
C_RC = 128
C_RS = C_RC + 16 * 64
C_PE = C_RS + 16 * 64
NCST = C_PE + 32
UW = 2336


class Sched:
    def __init__(self, nc, es):
        self.nc = nc
        self.es = es
        self.engs = ['pe', 'act', 'dve', 'pool', 'sp']
        self.prog = {e: [] for e in self.engs}
        self.cnt = {e: 0 for e in self.engs}
        self.seen = {e: {} for e in self.engs}
        self.lastw = {}
        self.readers = {}
        self.sems = {}
        self.dcum = {}
        self.sbuf_dma = set()
        for e in self.engs:
            self.sems[e] = es.enter_context(nc.semaphore("sem_" + e))

    def _deps(self, reads, writes):
        deps = {}
        raw = {}

        def add(dct, s, v):
            if dct.get(s, 0) < v:
                dct[s] = v
        for k in reads:
            t = self.lastw.get(k)
            if t is not None:
                add(deps, *t)
                add(raw, *t)
        for k in writes:
            t = self.lastw.get(k)
            if t is not None:
                add(deps, *t)
            for s, v in self.readers.get(k, {}).items():
                add(deps, s, v)
        self._raw = raw
        return deps

    def _commit(self, tok, reads, writes):
        s, v = tok
        for k in reads:
            r = self.readers.setdefault(k, {})
            if r.get(s, 0) < v:
                r[s] = v
        for k in writes:
            self.lastw[k] = tok
            self.readers[k] = {}

    def _waits(self, eng, deps):
        w = []
        for s, v in deps.items():
            if s == eng and eng in ('pe', 'sp'):
                continue
            if self.seen[eng].get(s, 0) >= v:
                continue
            self.seen[eng][s] = v
            w.append((s, v))
        return w

    def op(self, eng, fn, reads=(), writes=()):
        deps = self._deps(reads, writes)
        waits = self._waits(eng, deps)
        self.cnt[eng] += 1
        self.prog[eng].append((waits, fn, (eng, 1)))
        self._commit((eng, self.cnt[eng]), reads, writes)

    def dma(self, eng, fn, reads, writes, semkey, sbuf=True):
        if semkey not in self.sems:
            self.sems[semkey] = self.es.enter_context(self.nc.semaphore("d_" + semkey))
            self.dcum[semkey] = 0
        if sbuf:
            self.sbuf_dma.add(semkey)
        deps = self._deps(reads, writes)
        waits = self._waits(eng, deps)
        self.dcum[semkey] += 16
        self.prog[eng].append((waits, fn, (semkey, 16)))
        self._commit((semkey, self.dcum[semkey]), reads, writes)

    def barrier(self, engs=('pe', 'act', 'dve', 'sp', 'pool')):
        toks = {e: self.cnt[e] for e in ('pe', 'act', 'dve', 'pool') if self.cnt[e] > 0}
        for k in self.sbuf_dma:
            toks[k] = self.dcum[k]
        self._raw = dict(toks)
        for e in engs:
            w = self._waits(e, toks)
            if w:
                self.prog[e].append((w, None, None))

    def gate(self, eng, on):
        toks = {on: self.cnt[on]}
        self._raw = dict(toks)
        w = self._waits(eng, toks)
        if w:
            self.prog[eng].append((w, None, None))

    def wait_all(self, eng, semkeys):
        toks = {k: self.dcum[k] for k in semkeys if k in self.dcum}
        self._raw = dict(toks)
        w = self._waits(eng, toks)
        if w:
            self.prog[eng].append((w, None, None))

    def emit(self, block):
        emap = {'pe': block.tensor, 'act': block.scalar, 'dve': block.vector,
                'pool': block.gpsimd, 'sp': block.sync}
        for e in self.engs:
            prog = self.prog[e]

            def body(eng, prog=prog, ename=e):
                for waits, fn, inc in prog:
                    attach = None
                    if fn is not None and ename in ('act', 'dve') and waits:
                        attach = waits[-1]
                        waits = waits[:-1]
                    for s, v in waits:
                        eng.wait_ge(self.sems[s], v)
                    if fn is not None:
                        ins = fn(eng)
                        if attach is not None:
                            ins._wait_ge(self.sems[attach[0]], attach[1])
                        ins.then_inc(self.sems[inc[0]], inc[1])
            emap[e](body)


def build_program(debug=False, stop=None):
    nc = bass.Bass("TRN2", target_bir_lowering=False, dynamic_dma_scratch_size=4096)
    es = ExitStack()

    def din(name, shape, dt=F32):
        return nc.dram_tensor(name, list(shape), dt, kind="ExternalInput").ap()

    x_d = din("x", [2048, D])
    ctx_d = din("ctx", [256, D])
    vecs_d = din("vecs", [128, NV])
    cst_d = din("consts", [128, NCST])
    lnbc_d = din("lnbc", [8, D])
    gains_d = din("gains", [2, 64])
    wq_d = [din("ab_w_in", [D, 1536]), din("cd_w_in", [D, 2816])]
    wo_d = [din("ab_w_out", [D, D]), din("cd_w_out", [D, D])]
    wpool_d = din("ab_w_pool", [256, 64])
    ada_d = din("ada_w", [2 * D, 6 * D])
    fi_d = din("ffn_w_in", [2 * D, 2 * FFN_H])
    fo_d = din("ffn_w_out", [2 * FFN_H, D])
    out_d = nc.dram_tensor("out", [2048, D], F32, kind="ExternalOutput").ap()
    dbg_d = None
    if debug:
        dbg_d = nc.dram_tensor("dbg", [TOK, D], F32, kind="ExternalOutput").ap()

    def dscr(name, shape):
        if debug and name in ("qkT0_s", "v0_s", "qkT1_s", "v1_s"):
            return nc.dram_tensor(name, list(shape), BF16, kind="ExternalOutput").ap()
        return nc.dram_tensor(name, list(shape), BF16).ap()

    if debug:
        dbg_hb = nc.dram_tensor("dbg_hb", [128, 18432], BF16, kind="ExternalOutput").ap()
        dbg_mod = nc.dram_tensor("dbg_mod", [128, 192], F32, kind="ExternalOutput").ap()
        dbg_r3 = nc.dram_tensor("dbg_r3", [128, 4672], F32, kind="ExternalOutput").ap()
        dbg_att = nc.dram_tensor("dbg_att", [128, 1024], F32, kind="ExternalOutput").ap()
        dbg_stg = nc.dram_tensor("dbg_stg", [128, 2048], F32, kind="ExternalOutput").ap()
        dbg_pt = nc.dram_tensor("dbg_pt", [128, 2048], BF16, kind="ExternalOutput").ap()

    wq_s = [dscr("wq0_s", [D, 1536]), dscr("wq1_s", [D, 2816])]
    wo_s = [dscr("wo0_s", [D, D]), dscr("wo1_s", [D, D])]
    ada_s = [dscr("ada0_s", [D, 6 * D]), dscr("ada1_s", [D, 6 * D])]
    fi_s = [dscr("fi0_s", [D, 2 * FFN_H]), dscr("fi1_s", [D, 2 * FFN_H])]
    fo_s = [dscr("fo0_s", [FFN_H, D]), dscr("fo1_s", [FFN_H, D])]
    NQK = [1024, 1536]
    VW = [256, 768]
    qkT_s = [dscr("qkT0_s", [NQK[0], TOK]), dscr("qkT1_s", [NQK[1], TOK])]
    v_s = [dscr("v0_s", [TOK, VW[0]]), dscr("v1_s", [TOK, VW[1]])]

    def sb(name, shape, dt):
        return es.enter_context(nc.sbuf_tensor(name, list(shape), dt))

    X = sb("X", [128, NT, D], F32)
    HB = sb("HB", [128, 30720], BF16)
    RA = sb("RA", [128, 16384], BF16)
    R3 = sb("R3", [128, 4672], F32)
    CST = sb("CST", [128, NCST], F32)
    VEC = sb("VEC", [128, NV], F32)
    MOD = sb("MOD", [128, 2, 48, 2], F32)
    SILC = sb("SILC", [128, 8, 2], BF16)
    IDB = sb("IDB", [128, 128], BF16)
    ONB = sb("ONB", [128, 128], BF16)
    ONF = sb("ONF", [128, 128], F32)
    STG = sb("STG", [128, 2048], F32)
    TT = sb("TT", [128, 2, 512], BF16)
    VST = sb("VST", [128, 2, 512], BF16)
    PT = sb("PT", [128, 4, 512], BF16)
    SG = sb("SG", [128, 2, 512], F32)
    GBC = sb("GBC", [128, 2, 64], F32)
    BD = sb("BD", [128, 2, 128], BF16)
    DG = sb("DG", [128, 2, 128], F32)
    SM = sb("SM", [128, 256], F32)
    LAM = sb("LAM", [128, 8], F32)

    PSB = [es.enter_context(nc.psum_tensor(f"psb{i}", [128, 1024], F32)) for i in range(4)]

    def PS(i):
        return PSB[i // 2][:, (i % 2) * 512:(i % 2) * 512 + 512]

    ident = CST[:, C_ID:C_ID + 128]
    ropeC = CST[:, C_RC:C_RC + 1024].rearrange("p (t d) -> p t d", d=64)
    ropeS = CST[:, C_RS:C_RS + 1024].rearrange("p (t d) -> p t d", d=64)
    pedge = CST[:, C_PE:C_PE + 32].rearrange("p (c s i) -> p c s i", c=2, s=2)

    hT_all = HB[:, 0:8 * TOK].rearrange("p (k t) -> p k t", k=8)
    catT = hT_all
    WSLAB = [HB[:, 18432 + s * 4096: 18432 + (s + 1) * 4096].rearrange("p (k c) -> p k c", k=8) for s in range(2)]
    WOUT = HB[:, 18432:18432 + 8192].rearrange("p (k c) -> p k c", k=8)
    WO = HB[:, 0:22528].rearrange("p (j c) -> p j c", j=22)
    FRING = [(HB[:, 22528 + s * 4096: 22528 + s * 4096 + 2048].rearrange("p (k c) -> p k c", k=8),
              HB[:, 22528 + s * 4096 + 2048: 22528 + (s + 1) * 4096].rearrange("p (k c) -> p k c", k=8))
             for s in range(2)]
    hidT = RA[:, 0:11264].rearrange("p (j t) -> p j t", j=22)
    h2T = RA[:, 11264:15360].rearrange("p (k t) -> p k t", k=8)
    BCT = [R3[:, i * 1024:(i + 1) * 1024] for i in range(4)]
    uT = R3[:, 0:2 * UW].rearrange("p (c w) -> p c w", c=2)
    aT = R3[:, 0:4096].rearrange("p (c t) -> p c t", c=2)
    RAF = RA[:, :].bitcast(F32)

    S = Sched(nc, es)
    rot = {}

    def nxt(name, n):
        v = rot.get(name, 0)
        rot[name] = (v + 1) % n
        return v

    SCALE = 0.125

    def cast2d(dst, src, key):
        r, c = src.shape
        if c > 2048:
            if c % 2048 == 0:
                d2 = dst.rearrange("a (b c) -> a b c", c=2048)
                s2 = src.rearrange("a (b c) -> a b c", c=2048)
            else:
                d2 = dst.rearrange("a b -> (a b)").rearrange("(n c) -> n c", c=2048)
                s2 = src.rearrange("a b -> (a b)").rearrange("(n c) -> n c", c=2048)
        else:
            d2, s2 = dst, src
        S.dma('pool', lambda e: e.dma_start(out=d2, in_=s2), [], [key], semkey="c_" + key, sbuf=False)

    CASTQ = []

    def qcast(dst, src, key, rows_per):
        r = src.shape[0]
        for r0 in range(0, r, rows_per):
            r1 = min(r, r0 + rows_per)
            CASTQ.append((key, lambda dst=dst, src=src, r0=r0, r1=r1, key=key: cast2d(dst[r0:r1, :], src[r0:r1, :], key)))

    def pump(n=1):
        for _ in range(n):
            if not CASTQ:
                return
            S.gate('pool', 'pe')
            CASTQ.pop(0)[1]()

    def need(key):
        while any(k == key for k, _ in CASTQ):
            CASTQ.pop(0)[1]()

    cast2d(ada_s[0][:, 0:2048], ada_d[0:D, 0:2048], "ada0a")
    cast2d(wq_s[0], wq_d[0], "wq0")
    cast2d(ada_s[0][:, 2048:6144], ada_d[0:D, 2048:6144], "ada0b")
    qcast(wo_s[0], wo_d[0], "wo0", 1024)
    qcast(fi_s[0], fi_d[0:D, :], "fi0", 128)
    qcast(fo_s[0], fo_d[0:FFN_H, :], "fo0", 704)
    qcast(ada_s[1], ada_d[D:2 * D, :], "ada1", 128)
    qcast(wq_s[1], wq_d[1], "wq1", 256)
    qcast(wo_s[1], wo_d[1], "wo1", 1024)
    qcast(fi_s[1], fi_d[D:2 * D, :], "fi1", 128)
    qcast(fo_s[1], fo_d[FFN_H:2 * FFN_H, :], "fo1", 704)

    S.dma('sp', lambda e: e.dma_start(out=CST[:, :], in_=cst_d), [], ['cst'], 'l_cst')
    S.dma('sp', lambda e: e.dma_start(out=VEC[:, :], in_=vecs_d), [], ['vec'], 'l_vec')
    S.dma('sp', lambda e: e.dma_start(out=GBC[:, :, :], in_=gains_d.partition_broadcast(128)), [], ['gbc'], 'l_gbc')
    S.dma('sp', lambda e: e.dma_start(out=X[:, 0:2, :], in_=ctx_d.rearrange("(t p) d -> p t d", p=128)),
          [], [('x', 0), ('x', 1)], 'l_ctx')
    for i in range(4):
        S.dma('sp', lambda e, i=i: e.dma_start(out=X[:, 2 + 4 * i:6 + 4 * i, :],
                                               in_=x_d[i * 512:(i + 1) * 512, :].rearrange("(t p) d -> p t d", p=128)),
              [], [('x', 2 + 4 * i + j) for j in range(4)], f'l_x{i}')

    S.op('dve', lambda e: e.memset(BD[:, :, :], 0.0), [], ['bd'])
    for c in range(2):
        for hh in range(2):
            g = 2 * c + hh
            S.dma('pool', lambda e, c=c, hh=hh, g=g: e.dma_start(out=BD[hh * 64:(hh + 1) * 64, c, hh * 64:(hh + 1) * 64],
                                                                 in_=wpool_d[g * 64:(g + 1) * 64, :]),
                  [], ['bd'], 'l_bd')
    S.op('dve', lambda e: e.memset(ONB[:, :], 1.0), [], ['onb'])
    S.op('dve', lambda e: e.memset(ONF[:, :], 1.0), [], ['onf'])
    S.op('dve', lambda e: e.tensor_copy(out=IDB[:, :], in_=ident), ['cst'], ['idb'])
    S.op('act', lambda e: e.activation(out=SILC[:, :, :].rearrange("p k w -> p w k"),
                                       in_=VEC[:, 0:16].rearrange("p (w k) -> p w k", w=2), func=AF.Silu),
         ['vec'], ['silc'])

    def compute_mod(l, part):
        if l == 0:
            keys_a = ['ada0a'] if part == 0 else ['ada0b']
        else:
            need('ada1')
            keys_a = ['ada1']
        srange = range(0, 4) if part == 0 else range(4, 12)
        oc0, oc1 = (0, 16) if part == 0 else (16, 48)
        for s_ in srange:
            slot = nxt('wslab', 2)
            S.dma('sp', lambda e, s_=s_, slot=slot: e.dma_start(
                out=WSLAB[slot][:, :, :], in_=ada_s[l][:, s_ * 512:(s_ + 1) * 512].rearrange("(k p) c -> p k c", p=128)),
                keys_a, [('wslab', slot)], f'wslab{slot}')
            for o4 in range(4):
                oc = s_ * 4 + o4
                for kc in range(8):
                    S.op('pe', lambda e, slot=slot, o4=o4, kc=kc, oc=oc: e.matmul(
                        PS(0)[:, oc * 2:oc * 2 + 2], lhsT=WSLAB[slot][:, kc, o4 * 128:(o4 + 1) * 128],
                        rhs=SILC[:, kc, :], start=(kc == 0), stop=(kc == 7)),
                        [('wslab', slot), 'silc'], [('ps', 0)])
        mps = PS(0)[:, 0:96].rearrange("p (o w) -> p o w", w=2)
        mk = ('mod', l, part)
        for w in range(2):
            S.op('dve', lambda e, w=w: e.tensor_tensor(out=MOD[:, l, oc0:oc1, w], in0=mps[:, oc0:oc1, w],
                                                       in1=VEC[:, 16 + 48 * l + oc0:16 + 48 * l + oc1], op=ALU.add),
                 [('ps', 0), 'vec'], [mk])
        s0 = 8 if part == 0 else 32
        S.op('dve', lambda e: e.tensor_scalar(out=MOD[:, l, s0:s0 + 8, :], in0=MOD[:, l, s0:s0 + 8, :],
                                              scalar1=1.0, scalar2=None, op0=ALU.add), [mk], [mk])

    def modv(l, s_, kc, w):
        return MOD[:, l, s_ * 8 + kc, w:w + 1]

    def build_gate_bc(dst, l, s_, w, key):
        for half in range(2):
            bank = nxt('bcbank', 2)
            for k4 in range(4):
                kc = half * 4 + k4
                dslot = nxt('dg', 2)
                S.op('dve', lambda e, kc=kc, dslot=dslot: e.tensor_scalar(
                    out=DG[:, dslot, :], in0=ident, scalar1=modv(l, s_, kc, w), scalar2=None, op0=ALU.mult),
                    ['cst', ('mod', l, 1)], [('dg', dslot)])
                S.op('pe', lambda e, k4=k4, dslot=dslot, bank=bank: e.matmul(
                    PS(bank)[:, k4 * 128:(k4 + 1) * 128], lhsT=ONF[:, :], rhs=DG[:, dslot, :], start=True, stop=True),
                    ['onf', ('dg', dslot)], [('ps', bank)])
            S.op('act', lambda e, half=half, bank=bank: e.activation(
                out=dst[:, half * 512:(half + 1) * 512], in_=PS(bank), func=AF.Copy),
                [('ps', bank)], [key])

    def load_ln_bc(dst, row, key):
        S.dma('sp', lambda e: e.dma_start(out=dst.unsqueeze(1), in_=lnbc_d[row:row + 1, :].partition_broadcast(128)),
              [], [key], 'l_' + str(key[1]))

    BLOCKS = [([0, 1], 1)] + [([2 + 4 * i + j for j in range(4)], 0) for i in range(4)]

    def phase_p0(l):
        for tiles, w in BLOCKS:
            n = 128 * len(tiles)
            for kc in range(8):
                bank = nxt('p0bank', 4)
                for i, t in enumerate(tiles):
                    S.op('pe', lambda e, i=i, t=t, kc=kc, bank=bank: e.transpose(
                        out=PS(bank)[:, i * 128:(i + 1) * 128], in_=X[:, t, kc * 128:(kc + 1) * 128], identity=ident),
                        [('x', t), 'cst'], [('ps', bank)])
                S.op('act', lambda e, kc=kc, bank=bank, n=n, t0=tiles[0], w=w: e.activation(
                    out=hT_all[:, kc, t0 * 128:t0 * 128 + n], in_=PS(bank)[:, 0:n], func=AF.Identity,
                    bias=modv(l, 0, kc, w), scale=modv(l, 1, kc, w)),
                    [('ps', bank), ('mod', l, 0)], [('hT', tiles[0], kc)])

    PTF = PT[:, :, :].rearrange("p a b -> p (a b)").bitcast(F32)
    D_ = PTF[:, 0:256]
    SGF = SG[:, :, :].rearrange("p a b -> p (a b)")
    STSETS = [(STG[:, 0:512], STG[:, 512:1024], STG[:, 1024:1536], 0),
              (STG[:, 1536:2048], SGF[:, 0:512], SGF[:, 512:1024], 160)]

    def hv(ap, n):
        return ap.rearrange("p (h d) -> p h d", d=64)

    def post_qk(l, t, bank, cs, n, row0, gain):
        si_ = nxt('stset', 2)
        A_, B_, C_, so_ = STSETS[si_]
        kA, kB, kC, kS = f'stgA{si_}', f'stgB{si_}', f'stgC{si_}', f'sm{si_}'
        src = PS(bank)[:, cs:cs + n]
        NH = n // 64
        pk = ('ps', bank)
        if l == 0:
            S.op('act', lambda e: e.activation(out=C_[:, 0:n], in_=src, func=AF.Square), [pk], [kC])
            yield
            S.op('dve', lambda e: e.tensor_reduce(out=SM[:, so_:so_ + NH], in_=hv(C_[:, 0:n], n), axis=AX.X, op=ALU.add),
                 [kC], [kS])
            S.op('dve', lambda e: e.tensor_scalar(out=SM[:, so_:so_ + NH], in0=SM[:, so_:so_ + NH], scalar1=1.0 / 64, scalar2=EPS,
                                                  op0=ALU.mult, op1=ALU.add), [kS], [kS])
            yield
            S.op('act', lambda e: e.activation(out=SM[:, so_ + 16:so_ + 16 + NH], in_=SM[:, so_:so_ + NH], func=AF.Ln), [kS], [kS])
            S.op('act', lambda e: e.activation(out=SM[:, so_ + 32:so_ + 32 + NH], in_=SM[:, so_ + 16:so_ + 16 + NH], func=AF.Exp, scale=-0.5),
                 [kS], [kS])
            yield
            S.op('dve', lambda e: e.tensor_tensor(out=hv(A_[:, 0:n], n), in0=hv(src, n),
                                                  in1=SM[:, so_ + 32:so_ + 32 + NH].unsqueeze(2).to_broadcast([128, NH, 64]), op=ALU.mult),
                 [pk, kS], [kA])
            S.op('dve', lambda e: e.tensor_tensor(out=hv(A_[:, 0:n], n), in0=hv(A_[:, 0:n], n),
                                                  in1=GBC[:, gain, :].unsqueeze(1).to_broadcast([128, NH, 64]), op=ALU.mult),
                 [kA, 'gbc'], [kA])
            xin, xk = A_[:, 0:n], kA
        else:
            xin, xk = src, pk
        if t < 2:
            if l == 0:
                res, rk = A_[:, 0:n], kA
            else:
                S.op('act', lambda e: e.activation(out=B_[:, 0:n], in_=src, func=AF.Copy), [pk], [kB])
                res, rk = B_[:, 0:n], kB
        else:
            tt = t - 2
            S.op('dve', lambda e: e.tensor_tensor(out=hv(B_[:, 0:n], n), in0=hv(xin, n),
                                                  in1=ropeC[:, tt, :].unsqueeze(1).to_broadcast([128, NH, 64]), op=ALU.mult),
                 [xk, 'cst'], [kB])

            def v4(ap):
                return ap.rearrange("p (h a two s) -> p h a two s", a=2, two=2, s=16)
            sv = ropeS[:, tt, :].rearrange("p (a two s) -> p a two s", a=2, two=2)
            for two in range(2):
                S.op('dve', lambda e, two=two: e.tensor_tensor(
                    out=v4(C_[:, 0:n])[:, :, :, two, :], in0=v4(xin)[:, :, :, 1 - two, :],
                    in1=sv[:, :, two, :].unsqueeze(1).to_broadcast([128, NH, 2, 16]), op=ALU.mult),
                    [xk, 'cst'], [kC])
            S.op('dve', lambda e: e.tensor_tensor(out=B_[:, 0:n], in0=B_[:, 0:n], in1=C_[:, 0:n], op=ALU.add),
                 [kB, kC], [kB])
            res, rk = B_[:, 0:n], kB
        yield
        tb = 3 + nxt('p1tb', 2)
        for i in range(n // 128):
            S.op('pe', lambda e, i=i, tb=tb: e.transpose(out=PS(tb)[:, i * 128:(i + 1) * 128],
                                                         in_=res[:, i * 128:(i + 1) * 128], identity=ident),
                 [rk, 'cst'], [('ps', tb)])
        yield
        ts = nxt('tt', 2)
        S.op('act', lambda e, tb=tb, ts=ts: e.activation(out=TT[:, ts, 0:n], in_=PS(tb)[:, 0:n], func=AF.Copy),
             [('ps', tb)], [('tt', ts)])
        S.dma('sp', lambda e, ts=ts: e.dma_start(
            out=qkT_s[l][row0:row0 + n, t * 128:(t + 1) * 128].rearrange("(i p) c -> p i c", p=128),
            in_=TT[:, ts, 0:n].rearrange("p (i c) -> p i c", c=128)),
            [('tt', ts)], [('qkT', l, t, row0)], f'ttw{ts}', sbuf=False)

    def post_v(l, t, bank, cs, n, vc0):
        vs = nxt('vst', 2)
        S.op('act', lambda e: e.activation(out=VST[:, vs, 0:n], in_=PS(bank)[:, cs:cs + n], func=AF.Copy),
             [('ps', bank)], [('vst', vs)])
        S.dma('sp', lambda e: e.dma_start(out=v_s[l][t * 128:(t + 1) * 128, vc0:vc0 + n], in_=VST[:, vs, 0:n]),
              [('vst', vs)], [('vs', l, t, vc0)], f'vstw{vs}', sbuf=False)

    def post_u_T(t, bank, cs):
        S.op('act', lambda e: e.activation(out=D_[:, 0:256], in_=PS(bank)[:, cs:cs + 256], func=AF.Copy),
             [('ps', bank)], ['stgD'])
        tb = 3 + nxt('p1tb', 2)
        for c in range(2):
            S.op('pe', lambda e, c=c, tb=tb: e.transpose(out=PS(tb)[:, c * 128:(c + 1) * 128],
                                                         in_=D_[:, c * 128:(c + 1) * 128], identity=ident),
                 ['stgD', 'cst'], [('ps', tb)])
        return tb

    def uoff(t):
        return 8 + t * 128 if t < 2 else 280 + (t - 2) * 128

    def post_u0(t, bank, cs):
        tb = post_u_T(t, bank, cs)
        S.op('act', lambda e: e.activation(out=uT[:, :, uoff(t):uoff(t) + 128],
                                           in_=PS(tb)[:, 0:256].rearrange("p (c t) -> p c t", c=2), func=AF.Copy),
             [('ps', tb)], ['uT'])

    def post_ua(t, bank, cs):
        tb = post_u_T(t, bank, cs)
        S.op('act', lambda e: e.activation(out=aT[:, :, (t - 2) * 128:(t - 1) * 128],
                                           in_=PS(tb)[:, 0:256].rearrange("p (c t) -> p c t", c=2), func=AF.Copy),
             [('ps', tb)], [('aT', t)])

    zbf = RA[:, 0:4160].rearrange("p (c w) -> p c w", c=2)

    def post_ug(t, bank, cs):
        tb = post_u_T(t, bank, cs)
        S.op('act', lambda e: e.activation(out=PTF[:, 256:512], in_=PS(tb)[:, 0:256], func=AF.Sigmoid),
             [('ps', tb)], ['sg0'])
        o = 15 + (t - 2) * 128
        S.op('dve', lambda e: e.tensor_tensor(out=zbf[:, :, o:o + 128], in0=aT[:, :, (t - 2) * 128:(t - 1) * 128],
                                              in1=PTF[:, 256:512].rearrange("p (c t) -> p c t", c=2), op=ALU.mult),
             [('aT', t), 'sg0'], ['zbf'])

    INFL = []

    def pipe_add(gen):
        for g_ in list(INFL):
            try:
                next(g_)
            except StopIteration:
                INFL.remove(g_)
        while len(INFL) >= 2:
            g_ = INFL[0]
            try:
                next(g_)
            except StopIteration:
                INFL.remove(g_)
        INFL.append(gen)
        try:
            next(gen)
        except StopIteration:
            INFL.remove(gen)

    def pipe_flush():
        while INFL:
            for g_ in list(INFL):
                try:
                    next(g_)
                except StopIteration:
                    INFL.remove(g_)

    def segs_for(l, si):
        if l == 0:
            return [[('qk', 0, 512, (0, 0))],
                    [('qk', 0, 256, (512, 0)), ('qk', 256, 256, (768, 1))],
                    [('v', 0, 256, 0), ('u0', 256, 256, None)]][si]
        return [[('qk', 0, 512, (0, None))],
                [('qk', 0, 256, (512, None)), ('qk', 256, 256, (768, None))],
                [('qk', 0, 512, (1024, None))],
                [('v', 0, 512, 0)],
                [('v', 0, 256, 512), ('ua', 256, 256, None)],
                [('ug', 0, 256, None)]][si]

    def phase_p1(l):
        ncols = 1536 if l == 0 else 2816
        slabs = [(c0, min(512, ncols - c0)) for c0 in range(0, ncols, 512)]
        if l == 1:
            S.op('dve', lambda e: e.memset(zbf[:, :, :], 0.0), [], ['zbf'])
        need(f'wq{l}')
        for si, (c0, ncs) in enumerate(slabs):
            slot = nxt('wslab', 2)
            S.dma('sp', lambda e, c0=c0, ncs=ncs, slot=slot: e.dma_start(
                out=WSLAB[slot][:, :, 0:ncs], in_=wq_s[l][:, c0:c0 + ncs].rearrange("(k p) c -> p k c", p=128)),
                [f'wq{l}'], [('wslab', slot)], f'wslab{slot}')
            segs = segs_for(l, si)
            for t in range(NT):
                if t < 2 and all(k in ('ua', 'ug') for k, _, _, _ in segs):
                    continue
                bank = nxt('p1bank', 3)
                if t % 6 == 5:
                    pump()
                t0 = 0 if t < 2 else 2 + 4 * ((t - 2) // 4)
                for kc in range(8):
                    S.op('pe', lambda e, kc=kc, t=t, bank=bank, slot=slot, ncs=ncs: e.matmul(
                        PS(bank)[:, 0:ncs], lhsT=hT_all[:, kc, t * 128:(t + 1) * 128], rhs=WSLAB[slot][:, kc, 0:ncs],
                        start=(kc == 0), stop=(kc == 7)),
                        [('hT', t0, kc), ('wslab', slot)], [('ps', bank)])
                for kind, cs, n, ex in segs:
                    if kind == 'qk':
                        pipe_add(post_qk(l, t, bank, cs, n, ex[0], ex[1]))
                    elif kind == 'v':
                        post_v(l, t, bank, cs, n, ex)
                    elif kind == 'u0':
                        post_u0(t, bank, cs)
                    elif kind == 'ua' and t >= 2:
                        post_ua(t, bank, cs)
                    elif kind == 'ug' and t >= 2:
                        post_ug(t, bank, cs)
            pipe_flush()

    def phase_pool():
        T1 = [RAF[:, 0:UW], RAF[:, UW:2 * UW]]
        pooled = RA[:, 9472:9472 + 2 * TOK].rearrange("p (c t) -> p c t", c=2)
        segsP = [(8, 0, 256), (280, 256, 2048)]

        def emit_group(c, hh, w, Aw):
            P0_, P1_ = hh * 64, hh * 64 + 64
            hw = w // 2
            for (ps_, ts_, L) in segsP:
                S.op('dve', lambda e, ps_=ps_, ts_=ts_, L=L: e.scalar_tensor_tensor(
                    out=pooled[P0_:P1_, c, ts_:ts_ + L], in0=Aw[P0_:P1_, ps_ - hw:ps_ - hw + L], scalar=1.0 / w,
                    in1=uT[P0_:P1_, c, ps_:ps_ + L], op0=ALU.mult, op1=ALU.subtract),
                    ['uT', 'poolA'], ['pooled'])
                for side in range(2):
                    po = ps_ if side == 0 else ps_ + L - 8
                    to = ts_ if side == 0 else ts_ + L - 8
                    S.op('dve', lambda e, po=po, side=side: e.tensor_tensor(
                        out=SM[P0_:P1_, 64:72], in0=Aw[P0_:P1_, po - hw:po - hw + 8], in1=pedge[P0_:P1_, c, side, :], op=ALU.mult),
                        ['poolA', 'cst'], ['sm_pe'])
                    S.op('dve', lambda e, po=po, to=to: e.tensor_tensor(
                        out=pooled[P0_:P1_, c, to:to + 8], in0=SM[P0_:P1_, 64:72], in1=uT[P0_:P1_, c, po:po + 8], op=ALU.subtract),
                        ['sm_pe', 'uT'], ['pooled'])

        for c in range(2):
            src = uT[:, c, :]
            cur = 0
            S.op('dve', lambda e, src=src: e.tensor_tensor(out=T1[0][:, 0:UW - 1], in0=src[:, 0:UW - 1], in1=src[:, 1:UW], op=ALU.add),
                 ['uT', 'pooled'], ['poolA'])
            if c == 0:
                emit_group(0, 0, 2, T1[0])
            S.op('dve', lambda e: e.tensor_tensor(out=T1[1][:, 0:UW - 3], in0=T1[0][:, 0:UW - 3], in1=T1[0][:, 2:UW - 1], op=ALU.add),
                 ['poolA'], ['poolA'])
            if c == 0:
                emit_group(0, 1, 4, T1[1])
            else:
                S.op('dve', lambda e: e.tensor_tensor(out=T1[0][:, 0:UW - 7], in0=T1[1][:, 0:UW - 7], in1=T1[1][:, 4:UW - 3], op=ALU.add),
                     ['poolA'], ['poolA'])
                emit_group(1, 0, 8, T1[0])
                S.op('dve', lambda e: e.tensor_tensor(out=T1[1][:, 0:UW - 15], in0=T1[0][:, 0:UW - 15], in1=T1[0][:, 8:UW - 7], op=ALU.add),
                     ['poolA'], ['poolA'])
                emit_group(1, 1, 16, T1[1])
        for c in range(2):
            for t0 in range(0, TOK, 512):
                n = min(512, TOK - t0)
                bank = nxt('p2bank', 2)
                S.op('pe', lambda e, c=c, t0=t0, n=n, bank=bank: e.matmul(
                    PS(bank)[:, 0:n], lhsT=BD[:, c, :], rhs=pooled[:, c, t0:t0 + n], start=True, stop=True),
                    ['bd', 'pooled'], [('ps', bank)])
                S.op('act', lambda e, c=c, t0=t0, n=n, bank=bank: e.activation(
                    out=catT[:, 6 + c, t0:t0 + n], in_=PS(bank)[:, 0:n], func=AF.Identity, scale=VEC[:, 112 + c:113 + c]),
                    [('ps', bank), 'vec'], [('cat', 6 + c)])

    def phase_conv():
        DIAG = RA[:, 4160:4160 + 62 * 128].rearrange("p (j m) -> p j m", m=128)
        sts = RAF[:, 6048:6048 + 2048]
        MEAN, VAR, TMP = sts[:, 0:512], sts[:, 512:1024], sts[:, 1024:1536]
        Y = STG[:, 0:1024].rearrange("p (c t) -> p c t", c=2)
        YSQ = STG[:, 1024:2048].rearrange("p (c t) -> p c t", c=2)
        for j in range(31):
            for c in range(2):
                S.op('dve', lambda e, j=j, c=c: e.tensor_scalar(
                    out=DIAG[:, 2 * j + c, :], in0=IDB[:, :], scalar1=VEC[:, 125 + 2 * j + c:126 + 2 * j + c],
                    scalar2=None, op0=ALU.mult), ['idb', 'vec'], ['diag'])
        for b in range(4):
            for c in range(2):
                bank = c
                for j in range(31):
                    S.op('pe', lambda e, j=j, c=c, b=b, bank=bank: e.matmul(
                        PS(bank), lhsT=DIAG[:, 2 * j + c, :], rhs=zbf[:, c, b * 512 + j:b * 512 + j + 512],
                        start=(j == 0), stop=(j == 30)), ['diag', 'zbf'], [('ps', bank)])
                S.op('act', lambda e, c=c, bank=bank: e.activation(
                    out=Y[:, c, :], in_=PS(bank), func=AF.Identity, bias=VEC[:, 114 + c:115 + c]),
                    [('ps', bank), 'vec'], [('cy', c)])
                S.op('dve', lambda e, c=c: e.tensor_tensor(out=YSQ[:, c, :], in0=Y[:, c, :], in1=Y[:, c, :], op=ALU.mult),
                     [('cy', c)], [('cysq', c)])
            for c in range(2):
                S.op('pe', lambda e, c=c: e.matmul(PS(2), lhsT=ONF[:, :], rhs=Y[:, c, :], start=(c == 0), stop=(c == 1)),
                     ['onf', ('cy', c)], [('ps', 2)])
            for c in range(2):
                S.op('pe', lambda e, c=c: e.matmul(PS(3), lhsT=ONF[:, :], rhs=YSQ[:, c, :], start=(c == 0), stop=(c == 1)),
                     ['onf', ('cysq', c)], [('ps', 3)])
            S.op('dve', lambda e: e.tensor_scalar(out=MEAN, in0=PS(2), scalar1=1.0 / 256, scalar2=None, op0=ALU.mult),
                 [('ps', 2)], ['cmean'])
            S.op('dve', lambda e: e.tensor_tensor(out=TMP, in0=MEAN, in1=MEAN, op=ALU.mult), ['cmean'], ['ctmp'])
            S.op('dve', lambda e: e.scalar_tensor_tensor(out=VAR, in0=PS(3), scalar=1.0 / 256, in1=TMP,
                                                         op0=ALU.mult, op1=ALU.subtract), [('ps', 3), 'ctmp'], ['cvar'])
            S.op('dve', lambda e: e.tensor_scalar(out=VAR, in0=VAR, scalar1=EPS, scalar2=None, op0=ALU.add), ['cvar'], ['cvar'])
            S.op('act', lambda e: e.activation(out=TMP, in_=VAR, func=AF.Ln), ['cvar'], ['ctmp'])
            S.op('act', lambda e: e.activation(out=VAR, in_=TMP, func=AF.Exp, scale=-0.5), ['ctmp'], ['cvar'])
            for c in range(2):
                S.op('dve', lambda e, c=c: e.tensor_tensor(out=Y[:, c, :], in0=Y[:, c, :], in1=MEAN, op=ALU.subtract),
                     [('cy', c), 'cmean'], [('cy', c)])
                S.op('dve', lambda e, c=c: e.tensor_tensor(out=Y[:, c, :], in0=Y[:, c, :], in1=VAR, op=ALU.mult),
                     [('cy', c), 'cvar'], [('cy', c)])
                S.op('act', lambda e, c=c, b=b: e.activation(
                    out=catT[:, 6 + c, 256 + b * 512:256 + (b + 1) * 512], in_=Y[:, c, :], func=AF.Silu,
                    bias=VEC[:, 118 + c:119 + c], scale=VEC[:, 116 + c:117 + c]),
                    [('cy', c), 'vec'], [('cat', 6 + c)])

    def att_views(l, slot):
        base = slot * 8192
        Q = RA[:, base:base + TOK]
        K = RA[:, base + TOK:base + 2 * TOK]
        V = RA[:, base + 4608:base + 4608 + 18 * 128].rearrange("p (c d) -> p c d", d=128)
        return Q, K, V, V

    def phase_att(l):
        if l == 0:
            pass
        else:
            S.op('dve', lambda e: e.tensor_tensor(out=LAM[:, 0:2].unsqueeze(2), in0=VEC[:, 121:125].rearrange("p (a b) -> p a b", b=2)[:, :, 0:1],
                                                  in1=VEC[:, 121:125].rearrange("p (a b) -> p a b", b=2)[:, :, 1:2], op=ALU.mult),
                 ['vec'], ['lam'])
            S.op('pe', lambda e: e.matmul(PS(0)[:, 0:2], lhsT=ONF[:, :], rhs=LAM[:, 0:2], start=True, stop=True),
                 ['onf', 'lam'], [('ps', 0)])
            S.op('act', lambda e: e.activation(out=LAM[:, 2:4], in_=PS(0)[:, 0:2], func=AF.Exp), [('ps', 0)], ['lam'])
            S.op('dve', lambda e: e.scalar_tensor_tensor(out=LAM[:, 4:5], in0=LAM[:, 3:4], scalar=-LAM_INIT1, in1=LAM[:, 2:3],
                                                         op0=ALU.add, op1=ALU.subtract), ['lam'], ['lam'])
            S.op('dve', lambda e: e.tensor_scalar(out=LAM[:, 5:6], in0=VEC[:, 120:121], scalar1=1.0 - LAM_INIT1, scalar2=None,
                                                  op0=ALU.mult), ['vec'], ['lam'])
        nunits = 6

        def load_unit(u):
            slot = u % 2
            Q, K, vA, vB = att_views(l, slot)
            rk = [('qkT', l, t, r) for t in range(NT) for r in ()]
            wk = [('att', slot)]
            deps = [k for k in S.lastw.keys() if isinstance(k, tuple) and k[0] in ('qkT', 'vs') and k[1] == l]
            sk = f'att{slot}'
            if l == 0:
                hA, hB = 2 * u, 2 * u + 1
                gA, gB = hA // 3, hB // 3
                for half, h, g in ((0, hA, gA), (1, hB, gB)):
                    S.dma('sp', lambda e, half=half, h=h: e.dma_start(out=Q[half * 64:(half + 1) * 64, :],
                                                                     in_=qkT_s[0][h * 64:(h + 1) * 64, :]), deps, wk, sk)
                    S.dma('sp', lambda e, half=half, g=g: e.dma_start(out=K[half * 64:(half + 1) * 64, :],
                                                                     in_=qkT_s[0][768 + g * 64:768 + (g + 1) * 64, :]), deps, wk, sk)
                S.dma('sp', lambda e: e.dma_start(out=vA[:, :, 0:64],
                                                  in_=v_s[0][:, gA * 64:(gA + 1) * 64].rearrange("(c p) d -> p c d", p=128)), deps, wk, sk)
                S.dma('sp', lambda e: e.dma_start(out=vA[:, :, 64:128],
                                                  in_=v_s[0][:, gB * 64:(gB + 1) * 64].rearrange("(c p) d -> p c d", p=128)), deps, wk, sk)
            else:
                h = u
                S.dma('sp', lambda e: e.dma_start(out=Q[:, :], in_=qkT_s[1][h * 128:(h + 1) * 128, :]), deps, wk, sk)
                S.dma('sp', lambda e: e.dma_start(out=K[:, :], in_=qkT_s[1][768 + h * 128:768 + (h + 1) * 128, :]), deps, wk, sk)
                S.dma('sp', lambda e: e.dma_start(out=vA[:, :, :],
                                                  in_=v_s[1][:, h * 128:(h + 1) * 128].rearrange("(c p) d -> p c d", p=128)), deps, wk, sk)

        qblocks = ([(0, 256, [0, 1])] if l == 0 else []) + [(256 + qb * 512, 512, list(range(18))) for qb in range(4)]
        load_unit(0)
        for u in range(nunits):
            if u + 1 < nunits:
                load_unit(u + 1)
            slot = u % 2
            Q, K, vA, vB = att_views(l, slot)
            ak = ('att', slot)
            for (q0_, n_, chunks_) in qblocks:
                pump()
                att_block(l, u, Q, K, vA, ak, q0_, n_, chunks_)
        if LATE:
            LATE.pop(0)()

    LATE = []

    def att_block(l, u, Q, K, vA, ak, q0, n, chunks):
            if True:
                def qk(c, ci):
                    for i in range(2):
                        bank = 2 * i + (ci % 2)
                        S.op('pe', lambda e, i=i, c=c, bank=bank: e.matmul(
                            PS(bank)[:, 0:n], lhsT=K[i * 64:(i + 1) * 64, c * 128:(c + 1) * 128],
                            rhs=Q[i * 64:(i + 1) * 64, q0:q0 + n], start=True, stop=True), [ak], [('ps', bank)])

                def ex(c, ci):
                    for i in range(2):
                        bank = 2 * i + (ci % 2)
                        ps_ = (2 * ci + i) % 4
                        S.op('act', lambda e, bank=bank, ps_=ps_: e.activation(
                            out=PT[:, ps_, 0:n], in_=PS(bank)[:, 0:n], func=AF.Exp, scale=SCALE),
                            [('ps', bank)], [('pt', ps_)])

                def pv(c, ci):
                    first, last = (ci == 0), (ci == len(chunks) - 1)
                    for i in range(2):
                        ps_ = (2 * ci + i) % 4
                        if False:
                            pass
                        else:
                            S.op('pe', lambda e, c=c, ps_=ps_, i=i: e.matmul(PS(4 + i)[:, 0:n], lhsT=vA[:, c, :], rhs=PT[:, ps_, 0:n],
                                                                             start=first, stop=last), [ak, ('pt', ps_)], [('ps', 4 + i)])
                            S.op('pe', lambda e, ps_=ps_, i=i: e.matmul(PS(6 + i)[:, 0:n], lhsT=ONB[:, :], rhs=PT[:, ps_, 0:n],
                                                                        start=first, stop=last), ['onb', ('pt', ps_)], [('ps', 6 + i)])

                nch = len(chunks)
                qk(chunks[0], 0)
                ex(chunks[0], 0)
                for ci in range(nch):
                    if ci + 1 < nch:
                        qk(chunks[ci + 1], ci + 1)
                        ex(chunks[ci + 1], ci + 1)
                    pv(chunks[ci], ci)
                    if ci == 9 and LATE:
                        LATE.pop(0)()
                if LATE:
                    LATE.pop(0)()
                OSs = [STG[:, 0:512], STG[:, 512:1024]]
                DSs = [STG[:, 1024:1536], STG[:, 1536:2048]]
                for i in range(2):
                    r0, r1 = (i * 64, i * 64 + 64) if l == 0 else (0, 128)
                    S.op('act', lambda e, i=i, r0=r0, r1=r1: e.activation(out=OSs[i][r0:r1, 0:n], in_=PS(4 + i)[r0:r1, 0:n], func=AF.Copy),
                         [('ps', 4 + i)], [('os', i)])
                    S.op('dve', lambda e, i=i, r0=r0, r1=r1: e.tensor_copy(out=DSs[i][r0:r1, 0:n], in_=PS(6 + i)[r0:r1, 0:n]),
                         [('ps', 6 + i)], [('ds', i)])
                for i in range(2):
                    r0, r1 = (i * 64, i * 64 + 64) if l == 0 else (0, 128)
                    S.op('dve', lambda e, i=i, r0=r0, r1=r1: e.reciprocal(out=DSs[i][r0:r1, 0:n], in_=DSs[i][r0:r1, 0:n]),
                         [('ds', i)], [('ds', i)])
                    if l == 0:
                        S.op('dve', lambda e, i=i, r0=r0, r1=r1: e.tensor_tensor(
                            out=catT[r0:r1, u, q0:q0 + n], in0=OSs[i][r0:r1, 0:n], in1=DSs[i][r0:r1, 0:n], op=ALU.mult),
                            [('os', i), ('ds', i)], [('cat', u)])
                    else:
                        S.op('dve', lambda e, i=i: e.tensor_tensor(out=OSs[i][:, 0:n], in0=OSs[i][:, 0:n], in1=DSs[i][:, 0:n], op=ALU.mult),
                             [('os', i), ('ds', i)], [('os', i)])
                if l == 1:
                    T0, T1_ = OSs[0], OSs[1]
                    R1 = DSs[1]
                    SQB = TT[:, 0, :]
                    S.op('dve', lambda e: e.scalar_tensor_tensor(out=T0[:, 0:n], in0=T1_[:, 0:n], scalar=LAM[:, 4:5], in1=T0[:, 0:n],
                                                                 op0=ALU.mult, op1=ALU.add), [('os', 0), ('os', 1), 'lam'], [('os', 0)])
                    S.op('dve', lambda e: e.tensor_tensor(out=SQB[:, 0:n], in0=T0[:, 0:n], in1=T0[:, 0:n], op=ALU.mult),
                         [('os', 0)], [('tt', 0)])
                    def late():
                        S.op('pe', lambda e: e.matmul(PS(0)[:, 0:n], lhsT=ONB[:, :], rhs=SQB[:, 0:n], start=True, stop=True),
                             ['onb', ('tt', 0)], [('ps', 0)])
                        S.op('act', lambda e: e.activation(out=R1[:, 0:n], in_=PS(0)[:, 0:n], func=AF.Ln, scale=1.0 / 128, bias=SM[:, 80:81]),
                             [('ps', 0), 'sm_eps'], [('ds', 1)])
                        S.op('act', lambda e: e.activation(out=R1[:, 0:n], in_=R1[:, 0:n], func=AF.Exp, scale=-0.5), [('ds', 1)], [('ds', 1)])
                        S.op('dve', lambda e: e.scalar_tensor_tensor(out=catT[:, u, q0:q0 + n], in0=T0[:, 0:n], scalar=LAM[:, 5:6], in1=R1[:, 0:n],
                                                                     op0=ALU.mult, op1=ALU.mult), [('os', 0), ('ds', 1), 'lam'], [('cat', u)])
                    LATE.append(late)

    def ln_update(t, Y, gbc, gkey, lng, lnb, lnkeys, ykeys, after=None):
        k = nxt('lnstg', 2)
        stg = STG[:, k * 1024:(k + 1) * 1024]
        sk = ('lnstg', k)
        so = 96 + k * 32
        stats = SM[:, so:so + 12]
        mv = SM[:, so + 12:so + 14]
        lnv = SM[:, so + 14:so + 15]
        rstd = SM[:, so + 15:so + 16]
        nb = SM[:, so + 16:so + 17]
        smk = ('lnsm', k)
        xk = ('x', t)
        S.op('dve', lambda e: e.tensor_tensor(out=stg, in0=Y, in1=gbc, op=ALU.mult), list(ykeys) + [gkey], [sk])
        yield
        S.op('dve', lambda e: e.scalar_tensor_tensor(out=X[:, t, :], in0=X[:, t, :], scalar=ALPHA, in1=stg,
                                                     op0=ALU.mult, op1=ALU.add), [xk, sk], [xk])
        for hh in range(2):
            S.op('dve', lambda e, hh=hh: e.bn_stats(out=stats[:, hh * 6:(hh + 1) * 6], in_=X[:, t, hh * 512:(hh + 1) * 512]),
                 [xk], [smk])
        S.op('dve', lambda e: e.bn_aggr(out=mv, in_=stats), [smk], [smk])
        yield
        S.op('act', lambda e: e.activation(out=lnv, in_=mv[:, 1:2], func=AF.Ln, bias=SM[:, 80:81]), [smk, 'sm_eps'], [smk])
        S.op('act', lambda e: e.activation(out=rstd, in_=lnv, func=AF.Exp, scale=-0.5), [smk], [smk])
        yield
        S.op('dve', lambda e: e.scalar_tensor_tensor(out=nb, in0=mv[:, 0:1], scalar=-1.0, in1=rstd, op0=ALU.mult, op1=ALU.mult),
             [smk], [smk])
        yield
        S.op('act', lambda e: e.activation(out=stg, in_=X[:, t, :], func=AF.Identity, bias=nb, scale=rstd), [xk, smk], [sk])
        yield
        S.op('dve', lambda e: e.tensor_tensor(out=stg, in0=stg, in1=lng, op=ALU.mult), [sk, lnkeys[0]], [sk])
        S.op('dve', lambda e: e.tensor_tensor(out=X[:, t, :], in0=stg, in1=lnb, op=ALU.add), [sk, lnkeys[1]], [xk])
        if after is not None:
            after()

    def phase_p4(l):
        need(f'wo{l}')
        S.dma('sp', lambda e: e.dma_start(out=WOUT[:, :, :], in_=wo_s[l].rearrange("(k p) c -> p k c", p=128)),
              [f'wo{l}'], ['wout'], 'wout')
        build_gate_bc(BCT[0], l, 2, 0, ('bct', 0))
        if l == 0:
            build_gate_bc(BCT[1], l, 2, 1, ('bct', 1))
        load_ln_bc(BCT[2], l * 4 + 0, ('bct', 2))
        load_ln_bc(BCT[3], l * 4 + 1, ('bct', 3))
        tiles = range(NT) if l == 0 else range(2, NT)
        for t in tiles:
            if t % 3 == 0:
                pump()
            p4_tile(l, t)
        pipe_flush()

    def p4_tile(l, t):
        if True:
            pb = nxt('p4pair', 2)
            w = 1 if t < 2 else 0
            for nh in range(2):
                bank = 2 * pb + nh
                for kc in range(8):
                    S.op('pe', lambda e, kc=kc, nh=nh, bank=bank: e.matmul(
                        PS(bank), lhsT=catT[:, kc, t * 128:(t + 1) * 128], rhs=WOUT[:, kc, nh * 512:(nh + 1) * 512],
                        start=(kc == 0), stop=(kc == 7)), [('cat', kc), 'wout'], [('ps', bank)])
            pipe_add(ln_update(t, PSB[pb][:, :], BCT[w], ('bct', w), BCT[2], BCT[3], [('bct', 2), ('bct', 3)],
                               [('ps', 2 * pb), ('ps', 2 * pb + 1)]))

    def phase_p5(l, last):
        need(f'fi{l}')
        need(f'fo{l}')
        S.dma('sp', lambda e: e.dma_start(out=WO[:, 0:11, :], in_=fo_s[l][0:1408, :].rearrange("(j p) c -> p j c", p=128)),
              [f'fo{l}'], ['wo'], 'wo')
        S.dma('sp', lambda e: e.dma_start(out=WO[:, 11:22, :], in_=fo_s[l][1408:2816, :].rearrange("(j p) c -> p j c", p=128)),
              [f'fo{l}'], ['wo'], 'wo')
        build_gate_bc(BCT[0], l, 5, 0, ('bct', 0))
        if l == 0:
            build_gate_bc(BCT[1], l, 5, 1, ('bct', 1))
        load_ln_bc(BCT[2], l * 4 + 2, ('bct', 2))
        load_ln_bc(BCT[3], l * 4 + 3, ('bct', 3))
        sbs = BLOCKS if l == 0 else BLOCKS[1:]
        for tiles, w in sbs:
            p5_sb(l, last, tiles, w)

    def p5_sb(l, last, tiles, w):
        if True:
            n = 128 * len(tiles)
            for kc in range(8):
                bank = 4 + nxt('p5tb', 4)
                for i, t in enumerate(tiles):
                    S.op('pe', lambda e, i=i, t=t, kc=kc, bank=bank: e.transpose(
                        out=PS(bank)[:, i * 128:(i + 1) * 128], in_=X[:, t, kc * 128:(kc + 1) * 128], identity=ident),
                        [('x', t), 'cst'], [('ps', bank)])
                S.op('act', lambda e, kc=kc, bank=bank: e.activation(
                    out=h2T[:, kc, 0:n], in_=PS(bank)[:, 0:n], func=AF.Identity, bias=modv(l, 3, kc, w), scale=modv(l, 4, kc, w)),
                    [('ps', bank), ('mod', l, 1)], [('h2T', kc)])
            for jp in range(11):
                if jp % 4 == 0:
                    pump()
                slot = nxt('fring', 2)
                sa, sg_ = FRING[slot]
                S.dma('sp', lambda e, jp=jp, sa=sa: e.dma_start(
                    out=sa[:, :, :], in_=fi_s[l][:, jp * 256:(jp + 1) * 256].rearrange("(k p) c -> p k c", p=128)),
                    [f'fi{l}'], [('fring', slot)], f'fring{slot}')
                S.dma('sp', lambda e, jp=jp, sg_=sg_: e.dma_start(
                    out=sg_[:, :, :], in_=fi_s[l][:, FFN_H + jp * 256:FFN_H + (jp + 1) * 256].rearrange("(k p) c -> p k c", p=128)),
                    [f'fi{l}'], [('fring', slot)], f'fring{slot}')
                for jj in range(2):
                    j = 2 * jp + jj
                    pr = nxt('p5ab', 2)
                    ba, bg = 2 * pr, 2 * pr + 1
                    for (bank, slab) in ((ba, sa), (bg, sg_)):
                        for kc in range(8):
                            S.op('pe', lambda e, kc=kc, bank=bank, slab=slab, jj=jj: e.matmul(
                                PS(bank)[:, 0:n], lhsT=slab[:, kc, jj * 128:(jj + 1) * 128], rhs=h2T[:, kc, 0:n],
                                start=(kc == 0), stop=(kc == 7)), [('fring', slot), ('h2T', kc)], [('ps', bank)])
                    sgs = nxt('sgs', 2)
                    S.op('act', lambda e, bg=bg, sgs=sgs: e.activation(out=SG[:, sgs, 0:n], in_=PS(bg)[:, 0:n], func=AF.Silu),
                         [('ps', bg)], [('sg', sgs)])
                    S.op('dve', lambda e, ba=ba, sgs=sgs, j=j: e.tensor_tensor(out=hidT[:, j, 0:n], in0=PS(ba)[:, 0:n],
                                                                              in1=SG[:, sgs, 0:n], op=ALU.mult),
                         [('ps', ba), ('sg', sgs)], [('hid', j)])
            for ti, t in enumerate(tiles):
                pb = 2 + nxt('p5pair', 2)
                for nh in range(2):
                    bank = 2 * pb + nh
                    for j in range(22):
                        S.op('pe', lambda e, j=j, nh=nh, bank=bank, ti=ti: e.matmul(
                            PS(bank), lhsT=hidT[:, j, ti * 128:(ti + 1) * 128], rhs=WO[:, j, nh * 512:(nh + 1) * 512],
                            start=(j == 0), stop=(j == 21)), [('hid', j), 'wo'], [('ps', bank)])
                def after(t=t):
                    if last:
                        S.dma('sp', lambda e, t=t: e.dma_start(out=out_d[(t - 2) * 128:(t - 1) * 128, :], in_=X[:, t, :]),
                              [('x', t)], [('out', t)], 'outw', sbuf=False)
                pipe_add(ln_update(t, PSB[pb][:, :], BCT[w], ('bct', w), BCT[2], BCT[3], [('bct', 2), ('bct', 3)],
                                   [('ps', 2 * pb), ('ps', 2 * pb + 1)], after=after))
            pipe_flush()

    S.op('dve', lambda e: e.memset(SM[:, 80:81], EPS), [], ['sm_eps'])

    def dump_all():
        S.barrier()
        for t in range(NT):
            S.dma('sp', lambda e, t=t: e.dma_start(out=dbg_d[t * 128:(t + 1) * 128, :], in_=X[:, t, :]),
                  [('x', t)], [('dbg', t)], 'dbgw', sbuf=False)
        S.dma('sp', lambda e: e.dma_start(out=dbg_hb, in_=HB[:, 0:18432]), [], ['dbg_hb'], 'dbgw', sbuf=False)
        S.dma('sp', lambda e: e.dma_start(out=dbg_mod, in_=MOD[:, :, :, :].rearrange("p l o w -> p (l o w)")), [], ['dbg_mod'], 'dbgw', sbuf=False)
        S.dma('sp', lambda e: e.dma_start(out=dbg_r3, in_=R3[:, :]), [], ['dbg_r3'], 'dbgw', sbuf=False)

    def run_all():
        for l in range(2):
            compute_mod(l, 0)
            S.barrier()
            if stop == f'mod{l}':
                return
            phase_p0(l)
            if stop == f'p0{l}':
                return
            if l == 0:
                S.op('dve', lambda e: e.memset(R3[:, :], 0.0), [], ['uT'])
            phase_p1(l)
            S.barrier()
            compute_mod(l, 1)
            if stop == f'p1{l}':
                return
            if l == 0:
                phase_pool()
            else:
                phase_conv()
            S.barrier()
            if stop == f'p2{l}':
                return
            phase_att(l)
            S.barrier()
            if stop == f'p3{l}':
                return
            phase_p4(l)
            S.barrier()
            if stop == f'p4{l}':
                return
            phase_p5(l, last=(l == 1))
            S.barrier()
            if stop == f'p5{l}':
                return
    run_all()
    if debug:
        dump_all()
    S.wait_all('sp', ['outw', 'dbgw'])

    with nc.Block() as block:
        S.emit(block)
    es.close()
    return nc


def _rope_tables():
    n = np.arange(2048)
    row = (n // 64).astype(np.float32)
    col = (n % 64).astype(np.float32)
    freqs = (np.float32(10000.0) ** (-np.arange(0, 32, 2, dtype=np.float32) / np.float32(32))).astype(np.float32)
    ang = np.concatenate([row[:, None] * freqs, col[:, None] * freqs], axis=-1).astype(np.float32)
    cos, sin = np.cos(ang).astype(np.float32), np.sin(ang).astype(np.float32)
    cr, cc, sr, sc = cos[:, :16], cos[:, 16:], sin[:, :16], sin[:, 16:]
    C = np.concatenate([cr, cr, cc, cc], axis=-1)
    Sg = np.concatenate([-sr, sr, -sc, sc], axis=-1)
    C = C.reshape(16, 128, 64).transpose(1, 0, 2)
    Sg = Sg.reshape(16, 128, 64).transpose(1, 0, 2)
    return np.ascontiguousarray(C), np.ascontiguousarray(Sg)


def _consts():
    cst = np.zeros((128, NCST), np.float32)
    cst[:, C_ID:C_ID + 128] = np.eye(128, dtype=np.float32)
    C, Sg = _rope_tables()
    cst[:, C_RC:C_RC + 1024] = C.reshape(128, 1024)
    cst[:, C_RS:C_RS + 1024] = Sg.reshape(128, 1024)
    pe = np.zeros((128, 2, 2, 8), np.float32)
    for p in range(128):
        for c in range(2):
            w = POOL_W[2 * c + p // 64]
            for i in range(8):
                pe[p, c, 0, i] = 1.0 / ((i + w // 2) - max(i - w // 2, 0))
                pe[p, c, 1, i] = 1.0 / min(w, 8 - i + w // 2)
    cst[:, C_PE:C_PE + 32] = pe.reshape(128, 32)
    return cst


def _colvec(v):
    v = np.asarray(v, np.float32).reshape(-1, 128)
    return v.T


_NC_CACHE = {}


def kernel(x, c, ctx, c_ctx, ab_w_in, ab_q_gain, ab_k_gain, ab_w_pool, ab_pool_scale, ab_w_out,
           cd_w_in, cd_lambda_q1, cd_lambda_k1, cd_lambda_q2, cd_lambda_k2, cd_subln_gain,
           cd_conv_w, cd_conv_b, cd_conv_ln_g, cd_conv_ln_b, cd_w_out,
           ada_w, ada_b, ln1_g, ln1_b, ln2_g, ln2_b, ffn_w_in, ffn_w_out, _debug=False, _stop=None):
    f = lambda a: np.ascontiguousarray(np.asarray(a, dtype=np.float32))
    x, c, ctx, c_ctx = f(x), f(c), f(ctx), f(c_ctx)
    if (_debug, _stop) not in _NC_CACHE:
        _NC_CACHE[(_debug, _stop)] = build_program(debug=_debug, stop=_stop)
    nc = _NC_CACHE[(_debug, _stop)]
    cst = _consts()
    lnbc = np.stack([f(ln1_g)[0], f(ln1_b)[0], f(ln2_g)[0], f(ln2_b)[0],
                     f(ln1_g)[1], f(ln1_b)[1], f(ln2_g)[1], f(ln2_b)[1]], axis=0)
    gains = np.stack([f(ab_q_gain)[0], f(ab_k_gain)[0]], axis=0)
    shared = {
        "consts": cst, "lnbc": np.ascontiguousarray(lnbc), "gains": np.ascontiguousarray(gains),
        "ab_w_in": f(ab_w_in)[0], "cd_w_in": f(cd_w_in)[0], "ab_w_out": f(ab_w_out)[0], "cd_w_out": f(cd_w_out)[0],
        "ab_w_pool": f(ab_w_pool)[0].reshape(256, 64), "ada_w": f(ada_w).reshape(2 * D, 6 * D),
        "ffn_w_in": f(ffn_w_in).reshape(2 * D, 2 * FFN_H), "ffn_w_out": f(ffn_w_out).reshape(2 * FFN_H, D),
    }
    in_maps = []
    for b in range(NCORES):
        vecs = np.zeros((128, NV), np.float32)
        vecs[:, 0:8] = _colvec(c[b])
        vecs[:, 8:16] = _colvec(c_ctx)
        vecs[:, 16:64] = _colvec(f(ada_b)[0])
        vecs[:, 64:112] = _colvec(f(ada_b)[1])
        vecs[:, 112:114] = _colvec(f(ab_pool_scale)[0])
        vecs[:, 114:116] = _colvec(f(cd_conv_b)[0])
        vecs[:, 116:118] = _colvec(f(cd_conv_ln_g)[0])
        vecs[:, 118:120] = _colvec(f(cd_conv_ln_b)[0])
        vecs[:, 120:121] = _colvec(f(cd_subln_gain)[0])
        vecs[0:64, 121] = f(cd_lambda_q1)[0]
        vecs[0:64, 122] = f(cd_lambda_k1)[0]
        vecs[0:64, 123] = f(cd_lambda_q2)[0]
        vecs[0:64, 124] = f(cd_lambda_k2)[0]
        cw = f(cd_conv_w)[0]
        for j in range(31):
            vecs[:, 125 + 2 * j:127 + 2 * j] = _colvec(cw[j])
        m = dict(shared)
        m["x"] = x[b]
        m["ctx"] = ctx[b]
        m["vecs"] = vecs
        in_maps.append(m)
    res = run_bass_kernel_spmd(nc, in_maps, core_ids=list(range(NCORES)))
    out = np.stack([np.asarray(r["out"], dtype=np.float32) for r in res.results], axis=0)
    if _debug:
        return out, res.results
    return out
```

```python
import math
from contextlib import ExitStack

import numpy as np
import concourse.bass as bass
import concourse.mybir as mybir
from concourse.bass_utils import run_bass_kernel_spmd

F32 = mybir.dt.float32
BF16 = mybir.dt.bfloat16
AF = mybir.ActivationFunctionType
ALU = mybir.AluOpType
AX = mybir.AxisListType

NCORES = 8
D = 1024
NT = 18
TOK = 2304
ALPHA = 4.0 ** 0.25
EPS = 1e-6
FFN_H = 2816
LAM_INIT1 = 0.8 - 0.6 * math.exp(-0.3)
POOL_W = (2, 4, 8, 16)
NV = 192
C_ID = 0
C_RC = 128
C_RS = C_RC + 16 * 64
C_PE = C_RS + 16 * 64
NCST = C_PE + 32
UW = 2336


class Sched:
    def __init__(self, nc, es):
        self.nc = nc
        self.es = es
        self.engs = ['pe', 'act', 'dve', 'pool', 'sp']
        self.prog = {e: [] for e in self.engs}
        self.cnt = {e: 0 for e in self.engs}
        self.seen = {e: {} for e in self.engs}
        self.lastw = {}
        self.readers = {}
        self.sems = {}
        self.dcum = {}
        self.sbuf_dma = set()
        for e in self.engs:
            self.sems[e] = es.enter_context(nc.semaphore("sem_" + e))

    def _deps(self, reads, writes):
        deps = {}
        raw = {}

        def add(dct, s, v):
            if dct.get(s, 0) < v:
                dct[s] = v
        for k in reads:
            t = self.lastw.get(k)
            if t is not None:
                add(deps, *t)
                add(raw, *t)
        for k in writes:
            t = self.lastw.get(k)
            if t is not None:
                add(deps, *t)
            for s, v in self.readers.get(k, {}).items():
                add(deps, s, v)
        self._raw = raw
        return deps

    def _commit(self, tok, reads, writes):
        s, v = tok
        for k in reads:
            r = self.readers.setdefault(k, {})
            if r.get(s, 0) < v:
                r[s] = v
        for k in writes:
            self.lastw[k] = tok
            self.readers[k] = {}

    def _waits(self, eng, deps):
        w = []
        for s, v in deps.items():
            if s == eng and eng in ('pe', 'sp'):
                continue
            if self.seen[eng].get(s, 0) >= v:
                continue
            self.seen[eng][s] = v
            w.append((s, v))
        return w

    def op(self, eng, fn, reads=(), writes=()):
        deps = self._deps(reads, writes)
        waits = self._waits(eng, deps)
        self.cnt[eng] += 1
        self.prog[eng].append((waits, fn, (eng, 1)))
        self._commit((eng, self.cnt[eng]), reads, writes)

    def dma(self, eng, fn, reads, writes, semkey, sbuf=True):
        if semkey not in self.sems:
            self.sems[semkey] = self.es.enter_context(self.nc.semaphore("d_" + semkey))
            self.dcum[semkey] = 0
        if sbuf:
            self.sbuf_dma.add(semkey)
        deps = self._deps(reads, writes)
        waits = self._waits(eng, deps)
        self.dcum[semkey] += 16
        self.prog[eng].append((waits, fn, (semkey, 16)))
        self._commit((semkey, self.dcum[semkey]), reads, writes)

    def barrier(self, engs=('pe', 'act', 'dve', 'sp', 'pool')):
        toks = {e: self.cnt[e] for e in ('pe', 'act', 'dve', 'pool') if self.cnt[e] > 0}
        for k in self.sbuf_dma:
            toks[k] = self.dcum[k]
        self._raw = dict(toks)
        for e in engs:
            w = self._waits(e, toks)
            if w:
                self.prog[e].append((w, None, None))

    def gate(self, eng, on):
        toks = {on: self.cnt[on]}
        self._raw = dict(toks)
        w = self._waits(eng, toks)
        if w:
            self.prog[eng].append((w, None, None))

    def wait_all(self, eng, semkeys):
        toks = {k: self.dcum[k] for k in semkeys if k in self.dcum}
        self._raw = dict(toks)
        w = self._waits(eng, toks)
        if w:
            self.prog[eng].append((w, None, None))

    def emit(self, block):
        emap = {'pe': block.tensor, 'act': block.scalar, 'dve': block.vector,
                'pool': block.gpsimd, 'sp': block.sync}
        for e in self.engs:
            prog = self.prog[e]

            def body(eng, prog=prog, ename=e):
                for waits, fn, inc in prog:
                    attach = None
                    if fn is not None and ename in ('act', 'dve') and waits:
                        attach = waits[-1]
                        waits = waits[:-1]
                    for s, v in waits:
                        eng.wait_ge(self.sems[s], v)
                    if fn is not None:
                        ins = fn(eng)
                        if attach is not None:
                            ins._wait_ge(self.sems[attach[0]], attach[1])
                        ins.then_inc(self.sems[inc[0]], inc[1])
            emap[e](body)


def build_program(debug=False, stop=None):
    nc = bass.Bass("TRN2", target_bir_lowering=False, dynamic_dma_scratch_size=4096)
    es = ExitStack()

    def din(name, shape, dt=F32):
        return nc.dram_tensor(name, list(shape), dt, kind="ExternalInput").ap()

    x_d = din("x", [2048, D])
    ctx_d = din("ctx", [256, D])
    vecs_d = din("vecs", [128, NV])
    cst_d = din("consts", [128, NCST])
    lnbc_d = din("lnbc", [8, D])
    gains_d = din("gains", [2, 64])
    wq_d = [din("ab_w_in", [D, 1536]), din("cd_w_in", [D, 2816])]
    wo_d = [din("ab_w_out", [D, D]), din("cd_w_out", [D, D])]
    wpool_d = din("ab_w_pool", [256, 64])
    ada_d = din("ada_w", [2 * D, 6 * D])
    fi_d = din("ffn_w_in", [2 * D, 2 * FFN_H])
    fo_d = din("ffn_w_out", [2 * FFN_H, D])
    out_d = nc.dram_tensor("out", [2048, D], F32, kind="ExternalOutput").ap()
    dbg_d = None
    if debug:
        dbg_d = nc.dram_tensor("dbg", [TOK, D], F32, kind="ExternalOutput").ap()

    def dscr(name, shape):
        if debug and name in ("qkT0_s", "v0_s", "qkT1_s", "v1_s"):
            return nc.dram_tensor(name, list(shape), BF16, kind="ExternalOutput").ap()
        return nc.dram_tensor(name, list(shape), BF16).ap()

    if debug:
        dbg_hb = nc.dram_tensor("dbg_hb", [128, 18432], BF16, kind="ExternalOutput").ap()
        dbg_mod = nc.dram_tensor("dbg_mod", [128, 192], F32, kind="ExternalOutput").ap()
        dbg_r3 = nc.dram_tensor("dbg_r3", [128, 4672], F32, kind="ExternalOutput").ap()
        dbg_att = nc.dram_tensor("dbg_att", [128, 1024], F32, kind="ExternalOutput").ap()
        dbg_stg = nc.dram_tensor("dbg_stg", [128, 2048], F32, kind="ExternalOutput").ap()
        dbg_pt = nc.dram_tensor("dbg_pt", [128, 2048], BF16, kind="ExternalOutput").ap()

    wq_s = [dscr("wq0_s", [D, 1536]), dscr("wq1_s", [D, 2816])]
    wo_s = [dscr("wo0_s", [D, D]), dscr("wo1_s", [D, D])]
    ada_s = [dscr("ada0_s", [D, 6 * D]), dscr("ada1_s", [D, 6 * D])]
    fi_s = [dscr("fi0_s", [D, 2 * FFN_H]), dscr("fi1_s", [D, 2 * FFN_H])]
    fo_s = [dscr("fo0_s", [FFN_H, D]), dscr("fo1_s", [FFN_H, D])]
    NQK = [1024, 1536]
    VW = [256, 768]
    qkT_s = [dscr("qkT0_s", [NQK[0], TOK]), dscr("qkT1_s", [NQK[1], TOK])]
    v_s = [dscr("v0_s", [TOK, VW[0]]), dscr("v1_s", [TOK, VW[1]])]

    def sb(name, shape, dt):
        return es.enter_context(nc.sbuf_tensor(name, list(shape), dt))

    X = sb("X", [128, NT, D], F32)
    HB = sb("HB", [128, 30720], BF16)
    RA = sb("RA", [128, 16384], BF16)
    R3 = sb("R3", [128, 4672], F32)
    CST = sb("CST", [128, NCST], F32)
    VEC = sb("VEC", [128, NV], F32)
    MOD = sb("MOD", [128, 2, 48, 2], F32)
    SILC = sb("SILC", [128, 8, 2], BF16)
    IDB = sb("IDB", [128, 128], BF16)
    ONB = sb("ONB", [128, 128], BF16)
    ONF = sb("ONF", [128, 128], F32)
    STG = sb("STG", [128, 2048], F32)
    TT = sb("TT", [128, 2, 512], BF16)
    VST = sb("VST", [128, 2, 512], BF16)
    PT = sb("PT", [128, 4, 512], BF16)
    SG = sb("SG", [128, 2, 512], F32)
    GBC = sb("GBC", [128, 2, 64], F32)
    BD = sb("BD", [128, 2, 128], BF16)
    DG = sb("DG", [128, 2, 128], F32)
    SM = sb("SM", [128, 256], F32)
    LAM = sb("LAM", [128, 8], F32)

    PSB = [es.enter_context(nc.psum_tensor(f"psb{i}", [128, 1024], F32)) for i in range(4)]

    def PS(i):
        return PSB[i // 2][:, (i % 2) * 512:(i % 2) * 512 + 512]

    ident = CST[:, C_ID:C_ID + 128]
    ropeC = CST[:, C_RC:C_RC + 1024].rearrange("p (t d) -> p t d", d=64)
    ropeS = CST[:, C_RS:C_RS + 1024].rearrange("p (t d) -> p t d", d=64)
    pedge = CST[:, C_PE:C_PE + 32].rearrange("p (c s i) -> p c s i", c=2, s=2)

    hT_all = HB[:, 0:8 * TOK].rearrange("p (k t) -> p k t", k=8)
    catT = hT_all
    WSLAB = [HB[:, 18432 + s * 4096: 18432 + (s + 1) * 4096].rearrange("p (k c) -> p k c", k=8) for s in range(2)]
    WOUT = HB[:, 18432:18432 + 8192].rearrange("p (k c) -> p k c", k=8)
    WO = HB[:, 0:22528].rearrange("p (j c) -> p j c", j=22)
    FRING = [(HB[:, 22528 + s * 4096: 22528 + s * 4096 + 2048].rearrange("p (k c) -> p k c", k=8),
              HB[:, 22528 + s * 4096 + 2048: 22528 + (s + 1) * 4096].rearrange("p (k c) -> p k c", k=8))
             for s in range(2)]
    hidT = RA[:, 0:11264].rearrange("p (j t) -> p j t", j=22)
    h2T = RA[:, 11264:15360].rearrange("p (k t) -> p k t", k=8)
    BCT = [R3[:, i * 1024:(i + 1) * 1024] for i in range(4)]
    uT = R3[:, 0:2 * UW].rearrange("p (c w) -> p c w", c=2)
    aT = R3[:, 0:4096].rearrange("p (c t) -> p c t", c=2)
    RAF = RA[:, :].bitcast(F32)

    S = Sched(nc, es)
    rot = {}

    def nxt(name, n):
        v = rot.get(name, 0)
        rot[name] = (v + 1) % n
        return v

    SCALE = 0.125

    def cast2d(dst, src, key):
        r, c = src.shape
        if c > 2048:
            if c % 2048 == 0:
                d2 = dst.rearrange("a (b c) -> a b c", c=2048)
                s2 = src.rearrange("a (b c) -> a b c", c=2048)
            else:
                d2 = dst.rearrange("a b -> (a b)").rearrange("(n c) -> n c", c=2048)
                s2 = src.rearrange("a b -> (a b)").rearrange("(n c) -> n c", c=2048)
        else:
            d2, s2 = dst, src
        S.dma('pool', lambda e: e.dma_start(out=d2, in_=s2), [], [key], semkey="c_" + key, sbuf=False)

    CASTQ = []

    def qcast(dst, src, key, rows_per):
        r = src.shape[0]
        for r0 in range(0, r, rows_per):
            r1 = min(r, r0 + rows_per)
            CASTQ.append((key, lambda dst=dst, src=src, r0=r0, r1=r1, key=key: cast2d(dst[r0:r1, :], src[r0:r1, :], key)))

    def pump(n=1):
        for _ in range(n):
            if not CASTQ:
                return
            S.gate('pool', 'pe')
            CASTQ.pop(0)[1]()

    def need(key):
        while any(k == key for k, _ in CASTQ):
            CASTQ.pop(0)[1]()

    cast2d(ada_s[0][:, 0:2048], ada_d[0:D, 0:2048], "ada0a")

    def early_casts():
        S.gate('pool', 'pe')
        cast2d(wq_s[0], wq_d[0], "wq0")
        cast2d(ada_s[0][:, 2048:6144], ada_d[0:D, 2048:6144], "ada0b")
    qcast(wo_s[0], wo_d[0], "wo0", 1024)
    qcast(fi_s[0], fi_d[0:D, :], "fi0", 128)
    qcast(fo_s[0], fo_d[0:FFN_H, :], "fo0", 704)
    qcast(ada_s[1], ada_d[D:2 * D, :], "ada1", 128)
    qcast(wq_s[1], wq_d[1], "wq1", 256)
    qcast(wo_s[1], wo_d[1], "wo1", 1024)
    qcast(fi_s[1], fi_d[D:2 * D, :], "fi1", 128)
    qcast(fo_s[1], fo_d[FFN_H:2 * FFN_H, :], "fo1", 704)

    S.dma('sp', lambda e: e.dma_start(out=CST[:, :], in_=cst_d), [], ['cst'], 'l_cst')
    S.dma('sp', lambda e: e.dma_start(out=VEC[:, :], in_=vecs_d), [], ['vec'], 'l_vec')
    S.dma('sp', lambda e: e.dma_start(out=GBC[:, :, :], in_=gains_d.partition_broadcast(128)), [], ['gbc'], 'l_gbc')
    def load_x():
        S.dma('sp', lambda e: e.dma_start(out=X[:, 0:2, :], in_=ctx_d.rearrange("(t p) d -> p t d", p=128)),
              [], [('x', 0), ('x', 1)], 'l_ctx')
        for i in range(4):
            S.dma('sp', lambda e, i=i: e.dma_start(out=X[:, 2 + 4 * i:6 + 4 * i, :],
                                                   in_=x_d[i * 512:(i + 1) * 512, :].rearrange("(t p) d -> p t d", p=128)),
                  [], [('x', 2 + 4 * i + j) for j in range(4)], f'l_x{i}')

    S.op('dve', lambda e: e.memset(BD[:, :, :], 0.0), [], ['bd'])
    for c in range(2):
        for hh in range(2):
            g = 2 * c + hh
            S.dma('pool', lambda e, c=c, hh=hh, g=g: e.dma_start(out=BD[hh * 64:(hh + 1) * 64, c, hh * 64:(hh + 1) * 64],
                                                                 in_=wpool_d[g * 64:(g + 1) * 64, :]),
                  [], ['bd'], 'l_bd')
    S.op('dve', lambda e: e.memset(ONB[:, :], 1.0), [], ['onb'])
    S.op('dve', lambda e: e.memset(ONF[:, :], 1.0), [], ['onf'])
    S.op('dve', lambda e: e.tensor_copy(out=IDB[:, :], in_=ident), ['cst'], ['idb'])
    S.op('act', lambda e: e.activation(out=SILC[:, :, :].rearrange("p k w -> p w k"),
                                       in_=VEC[:, 0:16].rearrange("p (w k) -> p w k", w=2), func=AF.Silu),
         ['vec'], ['silc'])

    def compute_mod(l, part):
        if l == 0:
            keys_a = ['ada0a'] if part == 0 else ['ada0b']
        else:
            need('ada1')
            keys_a = ['ada1']
        srange = range(0, 4) if part == 0 else range(4, 12)
        oc0, oc1 = (0, 16) if part == 0 else (16, 48)
        for s_ in srange:
            slot = nxt('wslab', 2)
            S.dma('sp', lambda e, s_=s_, slot=slot: e.dma_start(
                out=WSLAB[slot][:, :, :], in_=ada_s[l][:, s_ * 512:(s_ + 1) * 512].rearrange("(k p) c -> p k c", p=128)),
                keys_a, [('wslab', slot)], f'wslab{slot}')
            for o4 in range(4):
                oc = s_ * 4 + o4
                for kc in range(8):
                    S.op('pe', lambda e, slot=slot, o4=o4, kc=kc, oc=oc: e.matmul(
                        PS(0)[:, oc * 2:oc * 2 + 2], lhsT=WSLAB[slot][:, kc, o4 * 128:(o4 + 1) * 128],
                        rhs=SILC[:, kc, :], start=(kc == 0), stop=(kc == 7)),
                        [('wslab', slot), 'silc'], [('ps', 0)])
        mps = PS(0)[:, 0:96].rearrange("p (o w) -> p o w", w=2)
        mk = ('mod', l, part)
        for w in range(2):
            S.op('dve', lambda e, w=w: e.tensor_tensor(out=MOD[:, l, oc0:oc1, w], in0=mps[:, oc0:oc1, w],
                                                       in1=VEC[:, 16 + 48 * l + oc0:16 + 48 * l + oc1], op=ALU.add),
                 [('ps', 0), 'vec'], [mk])
        s0 = 8 if part == 0 else 32
        S.op('dve', lambda e: e.tensor_scalar(out=MOD[:, l, s0:s0 + 8, :], in0=MOD[:, l, s0:s0 + 8, :],
                                              scalar1=1.0, scalar2=None, op0=ALU.add), [mk], [mk])

    def modv(l, s_, kc, w):
        return MOD[:, l, s_ * 8 + kc, w:w + 1]

    def build_gate_bc(dst, l, s_, w, key):
        for half in range(2):
            bank = nxt('bcbank', 2)
            for k4 in range(4):
                kc = half * 4 + k4
                dslot = nxt('dg', 2)
                S.op('dve', lambda e, kc=kc, dslot=dslot: e.tensor_scalar(
                    out=DG[:, dslot, :], in0=ident, scalar1=modv(l, s_, kc, w), scalar2=None, op0=ALU.mult),
                    ['cst', ('mod', l, 1)], [('dg', dslot)])
                S.op('pe', lambda e, k4=k4, dslot=dslot, bank=bank: e.matmul(
                    PS(bank)[:, k4 * 128:(k4 + 1) * 128], lhsT=ONF[:, :], rhs=DG[:, dslot, :], start=True, stop=True),
                    ['onf', ('dg', dslot)], [('ps', bank)])
            S.op('act', lambda e, half=half, bank=bank: e.activation(
                out=dst[:, half * 512:(half + 1) * 512], in_=PS(bank), func=AF.Copy),
                [('ps', bank)], [key])

    def load_ln_bc(dst, row, key):
        S.dma('sp', lambda e: e.dma_start(out=dst.unsqueeze(1), in_=lnbc_d[row:row + 1, :].partition_broadcast(128)),
              [], [key], 'l_' + str(key[1]))

    BLOCKS = [([0, 1], 1)] + [([2 + 4 * i + j for j in range(4)], 0) for i in range(4)]

    def phase_p0(l):
        for tiles, w in BLOCKS:
            n = 128 * len(tiles)
            for kc in range(8):
                bank = nxt('p0bank', 4)
                for i, t in enumerate(tiles):
                    S.op('pe', lambda e, i=i, t=t, kc=kc, bank=bank: e.transpose(
                        out=PS(bank)[:, i * 128:(i + 1) * 128], in_=X[:, t, kc * 128:(kc + 1) * 128], identity=ident),
                        [('x', t), 'cst'], [('ps', bank)])
                S.op('act', lambda e, kc=kc, bank=bank, n=n, t0=tiles[0], w=w: e.activation(
                    out=hT_all[:, kc, t0 * 128:t0 * 128 + n], in_=PS(bank)[:, 0:n], func=AF.Identity,
                    bias=modv(l, 0, kc, w), scale=modv(l, 1, kc, w)),
                    [('ps', bank), ('mod', l, 0)], [('hT', tiles[0], kc)])

    PTF = PT[:, :, :].rearrange("p a b -> p (a b)").bitcast(F32)
    D_ = PTF[:, 0:256]
    SGF = SG[:, :, :].rearrange("p a b -> p (a b)")
    STSETS = [(STG[:, 0:512], STG[:, 512:1024], STG[:, 1024:1536], 0),
              (STG[:, 1536:2048], SGF[:, 0:512], SGF[:, 512:1024], 160)]

    def hv(ap, n):
        return ap.rearrange("p (h d) -> p h d", d=64)

    def post_qk(l, t, bank, cs, n, row0, gain):
        si_ = nxt('stset', 2)
        A_, B_, C_, so_ = STSETS[si_]
        kA, kB, kC, kS = f'stgA{si_}', f'stgB{si_}', f'stgC{si_}', f'sm{si_}'
        src = PS(bank)[:, cs:cs + n]
        NH = n // 64
        pk = ('ps', bank)
        if l == 0:
            S.op('act', lambda e: e.activation(out=C_[:, 0:n], in_=src, func=AF.Square), [pk], [kC])
            yield
            S.op('dve', lambda e: e.tensor_reduce(out=SM[:, so_:so_ + NH], in_=hv(C_[:, 0:n], n), axis=AX.X, op=ALU.add),
                 [kC], [kS])
            S.op('dve', lambda e: e.tensor_scalar(out=SM[:, so_:so_ + NH], in0=SM[:, so_:so_ + NH], scalar1=1.0 / 64, scalar2=EPS,
                                                  op0=ALU.mult, op1=ALU.add), [kS], [kS])
            yield
            S.op('act', lambda e: e.activation(out=SM[:, so_ + 16:so_ + 16 + NH], in_=SM[:, so_:so_ + NH], func=AF.Ln), [kS], [kS])
            S.op('act', lambda e: e.activation(out=SM[:, so_ + 32:so_ + 32 + NH], in_=SM[:, so_ + 16:so_ + 16 + NH], func=AF.Exp, scale=-0.5),
                 [kS], [kS])
            yield
            S.op('dve', lambda e: e.tensor_tensor(out=hv(A_[:, 0:n], n), in0=hv(src, n),
                                                  in1=SM[:, so_ + 32:so_ + 32 + NH].unsqueeze(2).to_broadcast([128, NH, 64]), op=ALU.mult),
                 [pk, kS], [kA])
            S.op('dve', lambda e: e.tensor_tensor(out=hv(A_[:, 0:n], n), in0=hv(A_[:, 0:n], n),
                                                  in1=GBC[:, gain, :].unsqueeze(1).to_broadcast([128, NH, 64]), op=ALU.mult),
                 [kA, 'gbc'], [kA])
            xin, xk = A_[:, 0:n], kA
        else:
            xin, xk = src, pk
        if t < 2:
            if l == 0:
                res, rk = A_[:, 0:n], kA
            else:
                S.op('act', lambda e: e.activation(out=B_[:, 0:n], in_=src, func=AF.Copy), [pk], [kB])
                res, rk = B_[:, 0:n], kB
        else:
            tt = t - 2
            S.op('dve', lambda e: e.tensor_tensor(out=hv(B_[:, 0:n], n), in0=hv(xin, n),
                                                  in1=ropeC[:, tt, :].unsqueeze(1).to_broadcast([128, NH, 64]), op=ALU.mult),
                 [xk, 'cst'], [kB])

            def v4(ap):
                return ap.rearrange("p (h a two s) -> p h a two s", a=2, two=2, s=16)
            sv = ropeS[:, tt, :].rearrange("p (a two s) -> p a two s", a=2, two=2)
            for two in range(2):
                S.op('dve', lambda e, two=two: e.tensor_tensor(
                    out=v4(C_[:, 0:n])[:, :, :, two, :], in0=v4(xin)[:, :, :, 1 - two, :],
                    in1=sv[:, :, two, :].unsqueeze(1).to_broadcast([128, NH, 2, 16]), op=ALU.mult),
                    [xk, 'cst'], [kC])
            S.op('dve', lambda e: e.tensor_tensor(out=B_[:, 0:n], in0=B_[:, 0:n], in1=C_[:, 0:n], op=ALU.add),
                 [kB, kC], [kB])
            res, rk = B_[:, 0:n], kB
        yield
        tb = 3 + nxt('p1tb', 2)
        for i in range(n // 128):
            S.op('pe', lambda e, i=i, tb=tb: e.transpose(out=PS(tb)[:, i * 128:(i + 1) * 128],
                                                         in_=res[:, i * 128:(i + 1) * 128], identity=ident),
                 [rk, 'cst'], [('ps', tb)])
        yield
        ts = nxt('tt', 2)
        S.op('act', lambda e, tb=tb, ts=ts: e.activation(out=TT[:, ts, 0:n], in_=PS(tb)[:, 0:n], func=AF.Copy),
             [('ps', tb)], [('tt', ts)])
        S.dma('sp', lambda e, ts=ts: e.dma_start(
            out=qkT_s[l][row0:row0 + n, t * 128:(t + 1) * 128].rearrange("(i p) c -> p i c", p=128),
            in_=TT[:, ts, 0:n].rearrange("p (i c) -> p i c", c=128)),
            [('tt', ts)], [('qkT', l, t, row0)], f'ttw{ts}', sbuf=False)

    def post_v(l, t, bank, cs, n, vc0):
        vs = nxt('vst', 2)
        S.op('act', lambda e: e.activation(out=VST[:, vs, 0:n], in_=PS(bank)[:, cs:cs + n], func=AF.Copy),
             [('ps', bank)], [('vst', vs)])
        S.dma('sp', lambda e: e.dma_start(out=v_s[l][t * 128:(t + 1) * 128, vc0:vc0 + n], in_=VST[:, vs, 0:n]),
              [('vst', vs)], [('vs', l, t, vc0)], f'vstw{vs}', sbuf=False)

    def post_u_T(t, bank, cs):
        S.op('act', lambda e: e.activation(out=D_[:, 0:256], in_=PS(bank)[:, cs:cs + 256], func=AF.Copy),
             [('ps', bank)], ['stgD'])
        tb = 3 + nxt('p1tb', 2)
        for c in range(2):
            S.op('pe', lambda e, c=c, tb=tb: e.transpose(out=PS(tb)[:, c * 128:(c + 1) * 128],
                                                         in_=D_[:, c * 128:(c + 1) * 128], identity=ident),
                 ['stgD', 'cst'], [('ps', tb)])
        return tb

    def uoff(t):
        return 8 + t * 128 if t < 2 else 280 + (t - 2) * 128

    def post_u0(t, bank, cs):
        tb = post_u_T(t, bank, cs)
        S.op('act', lambda e: e.activation(out=uT[:, :, uoff(t):uoff(t) + 128],
                                           in_=PS(tb)[:, 0:256].rearrange("p (c t) -> p c t", c=2), func=AF.Copy),
             [('ps', tb)], ['uT'])

    def post_ua(t, bank, cs):
        tb = post_u_T(t, bank, cs)
        S.op('act', lambda e: e.activation(out=aT[:, :, (t - 2) * 128:(t - 1) * 128],
                                           in_=PS(tb)[:, 0:256].rearrange("p (c t) -> p c t", c=2), func=AF.Copy),
             [('ps', tb)], [('aT', t)])

    zbf = RA[:, 0:4160].rearrange("p (c w) -> p c w", c=2)

    def post_ug(t, bank, cs):
        tb = post_u_T(t, bank, cs)
        S.op('act', lambda e: e.activation(out=PTF[:, 256:512], in_=PS(tb)[:, 0:256], func=AF.Sigmoid),
             [('ps', tb)], ['sg0'])
        o = 15 + (t - 2) * 128
        S.op('dve', lambda e: e.tensor_tensor(out=zbf[:, :, o:o + 128], in0=aT[:, :, (t - 2) * 128:(t - 1) * 128],
                                              in1=PTF[:, 256:512].rearrange("p (c t) -> p c t", c=2), op=ALU.mult),
             [('aT', t), 'sg0'], ['zbf'])

    INFL = []

    def pipe_add(gen):
        for g_ in list(INFL):
            try:
                next(g_)
            except StopIteration:
                INFL.remove(g_)
        while len(INFL) >= 2:
            g_ = INFL[0]
            try:
                next(g_)
            except StopIteration:
                INFL.remove(g_)
        INFL.append(gen)
        try:
            next(gen)
        except StopIteration:
            INFL.remove(gen)

    def pipe_flush():
        while INFL:
            for g_ in list(INFL):
                try:
                    next(g_)
                except StopIteration:
                    INFL.remove(g_)

    def segs_for(l, si):
        if l == 0:
            return [[('qk', 0, 512, (0, 0))],
                    [('qk', 0, 256, (512, 0)), ('qk', 256, 256, (768, 1))],
                    [('v', 0, 256, 0), ('u0', 256, 256, None)]][si]
        return [[('qk', 0, 512, (0, None))],
                [('qk', 0, 256, (512, None)), ('qk', 256, 256, (768, None))],
                [('qk', 0, 512, (1024, None))],
                [('v', 0, 512, 0)],
                [('v', 0, 256, 512), ('ua', 256, 256, None)],
                [('ug', 0, 256, None)]][si]

    def phase_p1(l):
        ncols = 1536 if l == 0 else 2816
        slabs = [(c0, min(512, ncols - c0)) for c0 in range(0, ncols, 512)]
        if l == 1:
            S.op('dve', lambda e: e.memset(zbf[:, :, :], 0.0), [], ['zbf'])
        need(f'wq{l}')
        for si, (c0, ncs) in enumerate(slabs):
            slot = nxt('wslab', 2)
            S.dma('sp', lambda e, c0=c0, ncs=ncs, slot=slot: e.dma_start(
                out=WSLAB[slot][:, :, 0:ncs], in_=wq_s[l][:, c0:c0 + ncs].rearrange("(k p) c -> p k c", p=128)),
                [f'wq{l}'], [('wslab', slot)], f'wslab{slot}')
            segs = segs_for(l, si)
            for t in range(NT):
                if t < 2 and all(k in ('ua', 'ug') for k, _, _, _ in segs):
                    continue
                bank = nxt('p1bank', 3)
                if t % 6 == 5:
                    pump()
                t0 = 0 if t < 2 else 2 + 4 * ((t - 2) // 4)
                for kc in range(8):
                    S.op('pe', lambda e, kc=kc, t=t, bank=bank, slot=slot, ncs=ncs: e.matmul(
                        PS(bank)[:, 0:ncs], lhsT=hT_all[:, kc, t * 128:(t + 1) * 128], rhs=WSLAB[slot][:, kc, 0:ncs],
                        start=(kc == 0), stop=(kc == 7)),
                        [('hT', t0, kc), ('wslab', slot)], [('ps', bank)])
                for kind, cs, n, ex in segs:
                    if kind == 'qk':
                        pipe_add(post_qk(l, t, bank, cs, n, ex[0], ex[1]))
                    elif kind == 'v':
                        post_v(l, t, bank, cs, n, ex)
                    elif kind == 'u0':
                        post_u0(t, bank, cs)
                    elif kind == 'ua' and t >= 2:
                        post_ua(t, bank, cs)
                    elif kind == 'ug' and t >= 2:
                        post_ug(t, bank, cs)
            pipe_flush()

    def phase_pool():
        T1 = [RAF[:, 0:UW], RAF[:, UW:2 * UW]]
        pooled = RA[:, 9472:9472 + 2 * TOK].rearrange("p (c t) -> p c t", c=2)
        segsP = [(8, 0, 256), (280, 256, 2048)]

        def emit_group(c, hh, w, Aw):
            P0_, P1_ = hh * 64, hh * 64 + 64
            hw = w // 2
            for (ps_, ts_, L) in segsP:
                S.op('dve', lambda e, ps_=ps_, ts_=ts_, L=L: e.scalar_tensor_tensor(
                    out=pooled[P0_:P1_, c, ts_:ts_ + L], in0=Aw[P0_:P1_, ps_ - hw:ps_ - hw + L], scalar=1.0 / w,
                    in1=uT[P0_:P1_, c, ps_:ps_ + L], op0=ALU.mult, op1=ALU.subtract),
                    ['uT', 'poolA'], ['pooled'])
                for side in range(2):
                    po = ps_ if side == 0 else ps_ + L - 8
                    to = ts_ if side == 0 else ts_ + L - 8
                    S.op('dve', lambda e, po=po, side=side: e.tensor_tensor(
                        out=SM[P0_:P1_, 64:72], in0=Aw[P0_:P1_, po - hw:po - hw + 8], in1=pedge[P0_:P1_, c, side, :], op=ALU.mult),
                        ['poolA', 'cst'], ['sm_pe'])
                    S.op('dve', lambda e, po=po, to=to: e.tensor_tensor(
                        out=pooled[P0_:P1_, c, to:to + 8], in0=SM[P0_:P1_, 64:72], in1=uT[P0_:P1_, c, po:po + 8], op=ALU.subtract),
                        ['sm_pe', 'uT'], ['pooled'])

        for c in range(2):
            src = uT[:, c, :]
            cur = 0
            S.op('dve', lambda e, src=src: e.tensor_tensor(out=T1[0][:, 0:UW - 1], in0=src[:, 0:UW - 1], in1=src[:, 1:UW], op=ALU.add),
                 ['uT', 'pooled'], ['poolA'])
            if c == 0:
                emit_group(0, 0, 2, T1[0])
            S.op('dve', lambda e: e.tensor_tensor(out=T1[1][:, 0:UW - 3], in0=T1[0][:, 0:UW - 3], in1=T1[0][:, 2:UW - 1], op=ALU.add),
                 ['poolA'], ['poolA'])
            if c == 0:
                emit_group(0, 1, 4, T1[1])
            else:
                S.op('dve', lambda e: e.tensor_tensor(out=T1[0][:, 0:UW - 7], in0=T1[1][:, 0:UW - 7], in1=T1[1][:, 4:UW - 3], op=ALU.add),
                     ['poolA'], ['poolA'])
                emit_group(1, 0, 8, T1[0])
                S.op('dve', lambda e: e.tensor_tensor(out=T1[1][:, 0:UW - 15], in0=T1[0][:, 0:UW - 15], in1=T1[0][:, 8:UW - 7], op=ALU.add),
                     ['poolA'], ['poolA'])
                emit_group(1, 1, 16, T1[1])
        for c in range(2):
            for t0 in range(0, TOK, 512):
                n = min(512, TOK - t0)
                bank = nxt('p2bank', 2)
                S.op('pe', lambda e, c=c, t0=t0, n=n, bank=bank: e.matmul(
                    PS(bank)[:, 0:n], lhsT=BD[:, c, :], rhs=pooled[:, c, t0:t0 + n], start=True, stop=True),
                    ['bd', 'pooled'], [('ps', bank)])
                S.op('act', lambda e, c=c, t0=t0, n=n, bank=bank: e.activation(
                    out=catT[:, 6 + c, t0:t0 + n], in_=PS(bank)[:, 0:n], func=AF.Identity, scale=VEC[:, 112 + c:113 + c]),
                    [('ps', bank), 'vec'], [('cat', 6 + c)])

    def phase_conv():
        DIAG = RA[:, 4160:4160 + 62 * 128].rearrange("p (j m) -> p j m", m=128)
        sts = RAF[:, 6048:6048 + 2048]
        MEAN, VAR, TMP = sts[:, 0:512], sts[:, 512:1024], sts[:, 1024:1536]
        Y = STG[:, 0:1024].rearrange("p (c t) -> p c t", c=2)
        YSQ = STG[:, 1024:2048].rearrange("p (c t) -> p c t", c=2)
        for j in range(31):
            for c in range(2):
                S.op('dve', lambda e, j=j, c=c: e.tensor_scalar(
                    out=DIAG[:, 2 * j + c, :], in0=IDB[:, :], scalar1=VEC[:, 125 + 2 * j + c:126 + 2 * j + c],
                    scalar2=None, op0=ALU.mult), ['idb', 'vec'], ['diag'])
        for b in range(4):
            for c in range(2):
                bank = c
                for j in range(31):
                    S.op('pe', lambda e, j=j, c=c, b=b, bank=bank: e.matmul(
                        PS(bank), lhsT=DIAG[:, 2 * j + c, :], rhs=zbf[:, c, b * 512 + j:b * 512 + j + 512],
                        start=(j == 0), stop=(j == 30)), ['diag', 'zbf'], [('ps', bank)])
                S.op('act', lambda e, c=c, bank=bank: e.activation(
                    out=Y[:, c, :], in_=PS(bank), func=AF.Identity, bias=VEC[:, 114 + c:115 + c]),
                    [('ps', bank), 'vec'], [('cy', c)])
                S.op('dve', lambda e, c=c: e.tensor_tensor(out=YSQ[:, c, :], in0=Y[:, c, :], in1=Y[:, c, :], op=ALU.mult),
                     [('cy', c)], [('cysq', c)])
            for c in range(2):
                S.op('pe', lambda e, c=c: e.matmul(PS(2), lhsT=ONF[:, :], rhs=Y[:, c, :], start=(c == 0), stop=(c == 1)),
                     ['onf', ('cy', c)], [('ps', 2)])
            for c in range(2):
                S.op('pe', lambda e, c=c: e.matmul(PS(3), lhsT=ONF[:, :], rhs=YSQ[:, c, :], start=(c == 0), stop=(c == 1)),
                     ['onf', ('cysq', c)], [('ps', 3)])
            S.op('dve', lambda e: e.tensor_scalar(out=MEAN, in0=PS(2), scalar1=1.0 / 256, scalar2=None, op0=ALU.mult),
                 [('ps', 2)], ['cmean'])
            S.op('dve', lambda e: e.tensor_tensor(out=TMP, in0=MEAN, in1=MEAN, op=ALU.mult), ['cmean'], ['ctmp'])
            S.op('dve', lambda e: e.scalar_tensor_tensor(out=VAR, in0=PS(3), scalar=1.0 / 256, in1=TMP,
                                                         op0=ALU.mult, op1=ALU.subtract), [('ps', 3), 'ctmp'], ['cvar'])
            S.op('dve', lambda e: e.tensor_scalar(out=VAR, in0=VAR, scalar1=EPS, scalar2=None, op0=ALU.add), ['cvar'], ['cvar'])
            S.op('act', lambda e: e.activation(out=TMP, in_=VAR, func=AF.Ln), ['cvar'], ['ctmp'])
            S.op('act', lambda e: e.activation(out=VAR, in_=TMP, func=AF.Exp, scale=-0.5), ['ctmp'], ['cvar'])
            for c in range(2):
                S.op('dve', lambda e, c=c: e.tensor_tensor(out=Y[:, c, :], in0=Y[:, c, :], in1=MEAN, op=ALU.subtract),
                     [('cy', c), 'cmean'], [('cy', c)])
                S.op('dve', lambda e, c=c: e.tensor_tensor(out=Y[:, c, :], in0=Y[:, c, :], in1=VAR, op=ALU.mult),
                     [('cy', c), 'cvar'], [('cy', c)])
                S.op('act', lambda e, c=c, b=b: e.activation(
                    out=catT[:, 6 + c, 256 + b * 512:256 + (b + 1) * 512], in_=Y[:, c, :], func=AF.Silu,
                    bias=VEC[:, 118 + c:119 + c], scale=VEC[:, 116 + c:117 + c]),
                    [('cy', c), 'vec'], [('cat', 6 + c)])

    def att_views(l, slot):
        base = slot * 8192
        Q = RA[:, base:base + TOK]
        K = RA[:, base + TOK:base + 2 * TOK]
        V = RA[:, base + 4608:base + 4608 + 18 * 128].rearrange("p (c d) -> p c d", d=128)
        return Q, K, V, V

    def phase_att(l):
        if l == 0:
            pass
        else:
            S.op('dve', lambda e: e.tensor_tensor(out=LAM[:, 0:2].unsqueeze(2), in0=VEC[:, 121:125].rearrange("p (a b) -> p a b", b=2)[:, :, 0:1],
                                                  in1=VEC[:, 121:125].rearrange("p (a b) -> p a b", b=2)[:, :, 1:2], op=ALU.mult),
                 ['vec'], ['lam'])
            S.op('pe', lambda e: e.matmul(PS(0)[:, 0:2], lhsT=ONF[:, :], rhs=LAM[:, 0:2], start=True, stop=True),
                 ['onf', 'lam'], [('ps', 0)])
            S.op('act', lambda e: e.activation(out=LAM[:, 2:4], in_=PS(0)[:, 0:2], func=AF.Exp), [('ps', 0)], ['lam'])
            S.op('dve', lambda e: e.scalar_tensor_tensor(out=LAM[:, 4:5], in0=LAM[:, 3:4], scalar=-LAM_INIT1, in1=LAM[:, 2:3],
                                                         op0=ALU.add, op1=ALU.subtract), ['lam'], ['lam'])
            S.op('dve', lambda e: e.tensor_scalar(out=LAM[:, 5:6], in0=VEC[:, 120:121], scalar1=1.0 - LAM_INIT1, scalar2=None,
                                                  op0=ALU.mult), ['vec'], ['lam'])
        nunits = 6

        def load_unit(u):
            slot = u % 2
            Q, K, vA, vB = att_views(l, slot)
            rk = [('qkT', l, t, r) for t in range(NT) for r in ()]
            wk = [('att', slot)]
            deps = [k for k in S.lastw.keys() if isinstance(k, tuple) and k[0] in ('qkT', 'vs') and k[1] == l]
            sk = f'att{slot}'
            if l == 0:
                hA, hB = 2 * u, 2 * u + 1
                gA, gB = hA // 3, hB // 3
                for half, h, g in ((0, hA, gA), (1, hB, gB)):
                    S.dma('sp', lambda e, half=half, h=h: e.dma_start(out=Q[half * 64:(half + 1) * 64, :],
                                                                     in_=qkT_s[0][h * 64:(h + 1) * 64, :]), deps, wk, sk)
                    S.dma('sp', lambda e, half=half, g=g: e.dma_start(out=K[half * 64:(half + 1) * 64, :],
                                                                     in_=qkT_s[0][768 + g * 64:768 + (g + 1) * 64, :]), deps, wk, sk)
                S.dma('sp', lambda e: e.dma_start(out=vA[:, :, 0:64],
                                                  in_=v_s[0][:, gA * 64:(gA + 1) * 64].rearrange("(c p) d -> p c d", p=128)), deps, wk, sk)
                S.dma('sp', lambda e: e.dma_start(out=vA[:, :, 64:128],
                                                  in_=v_s[0][:, gB * 64:(gB + 1) * 64].rearrange("(c p) d -> p c d", p=128)), deps, wk, sk)
            else:
                h = u
                S.dma('sp', lambda e: e.dma_start(out=Q[:, :], in_=qkT_s[1][h * 128:(h + 1) * 128, :]), deps, wk, sk)
                S.dma('sp', lambda e: e.dma_start(out=K[:, :], in_=qkT_s[1][768 + h * 128:768 + (h + 1) * 128, :]), deps, wk, sk)
                S.dma('sp', lambda e: e.dma_start(out=vA[:, :, :],
                                                  in_=v_s[1][:, h * 128:(h + 1) * 128].rearrange("(c p) d -> p c d", p=128)), deps, wk, sk)

        qblocks = ([(0, 256, [0, 1])] if l == 0 else []) + [(256 + qb * 512, 512, list(range(18))) for qb in range(4)]
        load_unit(0)
        for u in range(nunits):
            if u + 1 < nunits:
                load_unit(u + 1)
            slot = u % 2
            Q, K, vA, vB = att_views(l, slot)
            ak = ('att', slot)
            for (q0_, n_, chunks_) in qblocks:
                pump()
                att_block(l, u, Q, K, vA, ak, q0_, n_, chunks_)
        if LATE:
            LATE.pop(0)()

    LATE = []

    def att_block(l, u, Q, K, vA, ak, q0, n, chunks):
            if True:
                def qk(c, ci):
                    for i in range(2):
                        bank = 2 * i + (ci % 2)
                        S.op('pe', lambda e, i=i, c=c, bank=bank: e.matmul(
                            PS(bank)[:, 0:n], lhsT=K[i * 64:(i + 1) * 64, c * 128:(c + 1) * 128],
                            rhs=Q[i * 64:(i + 1) * 64, q0:q0 + n], start=True, stop=True), [ak], [('ps', bank)])

                def ex(c, ci):
                    for i in range(2):
                        bank = 2 * i + (ci % 2)
                        ps_ = (2 * ci + i) % 4
                        S.op('act', lambda e, bank=bank, ps_=ps_: e.activation(
                            out=PT[:, ps_, 0:n], in_=PS(bank)[:, 0:n], func=AF.Exp, scale=SCALE),
                            [('ps', bank)], [('pt', ps_)])

                def pv(c, ci):
                    first, last = (ci == 0), (ci == len(chunks) - 1)
                    for i in range(2):
                        ps_ = (2 * ci + i) % 4
                        if False:
                            pass
                        else:
                            S.op('pe', lambda e, c=c, ps_=ps_, i=i: e.matmul(PS(4 + i)[:, 0:n], lhsT=vA[:, c, :], rhs=PT[:, ps_, 0:n],
                                                                             start=first, stop=last), [ak, ('pt', ps_)], [('ps', 4 + i)])
                            S.op('pe', lambda e, ps_=ps_, i=i: e.matmul(PS(6 + i)[:, 0:n], lhsT=ONB[:, :], rhs=PT[:, ps_, 0:n],
                                                                        start=first, stop=last), ['onb', ('pt', ps_)], [('ps', 6 + i)])

                nch = len(chunks)
                qk(chunks[0], 0)
                ex(chunks[0], 0)
                for ci in range(nch):
                    if ci + 1 < nch:
                        qk(chunks[ci + 1], ci + 1)
                        ex(chunks[ci + 1], ci + 1)
                    pv(chunks[ci], ci)
                    if ci == 9 and LATE:
                        LATE.pop(0)()
                if LATE:
                    LATE.pop(0)()
                OSs = [STG[:, 0:512], STG[:, 512:1024]]
                DSs = [STG[:, 1024:1536], STG[:, 1536:2048]]
                for i in range(2):
                    r0, r1 = (i * 64, i * 64 + 64) if l == 0 else (0, 128)
                    S.op('act', lambda e, i=i, r0=r0, r1=r1: e.activation(out=OSs[i][r0:r1, 0:n], in_=PS(4 + i)[r0:r1, 0:n], func=AF.Copy),
                         [('ps', 4 + i)], [('os', i)])
                    S.op('dve', lambda e, i=i, r0=r0, r1=r1: e.tensor_copy(out=DSs[i][r0:r1, 0:n], in_=PS(6 + i)[r0:r1, 0:n]),
                         [('ps', 6 + i)], [('ds', i)])
                for i in range(2):
                    r0, r1 = (i * 64, i * 64 + 64) if l == 0 else (0, 128)
                    S.op('dve', lambda e, i=i, r0=r0, r1=r1: e.reciprocal(out=DSs[i][r0:r1, 0:n], in_=DSs[i][r0:r1, 0:n]),
                         [('ds', i)], [('ds', i)])
                    if l == 0:
                        S.op('dve', lambda e, i=i, r0=r0, r1=r1: e.tensor_tensor(
                            out=catT[r0:r1, u, q0:q0 + n], in0=OSs[i][r0:r1, 0:n], in1=DSs[i][r0:r1, 0:n], op=ALU.mult),
                            [('os', i), ('ds', i)], [('cat', u)])
                    else:
                        S.op('dve', lambda e, i=i: e.tensor_tensor(out=OSs[i][:, 0:n], in0=OSs[i][:, 0:n], in1=DSs[i][:, 0:n], op=ALU.mult),
                             [('os', i), ('ds', i)], [('os', i)])
                if l == 1:
                    T0, T1_ = OSs[0], OSs[1]
                    R1 = DSs[1]
                    SQB = TT[:, 0, :]
                    S.op('dve', lambda e: e.scalar_tensor_tensor(out=T0[:, 0:n], in0=T1_[:, 0:n], scalar=LAM[:, 4:5], in1=T0[:, 0:n],
                                                                 op0=ALU.mult, op1=ALU.add), [('os', 0), ('os', 1), 'lam'], [('os', 0)])
                    S.op('dve', lambda e: e.tensor_tensor(out=SQB[:, 0:n], in0=T0[:, 0:n], in1=T0[:, 0:n], op=ALU.mult),
                         [('os', 0)], [('tt', 0)])
                    def late():
                        S.op('pe', lambda e: e.matmul(PS(0)[:, 0:n], lhsT=ONB[:, :], rhs=SQB[:, 0:n], start=True, stop=True),
                             ['onb', ('tt', 0)], [('ps', 0)])
                        S.op('act', lambda e: e.activation(out=R1[:, 0:n], in_=PS(0)[:, 0:n], func=AF.Ln, scale=1.0 / 128, bias=SM[:, 80:81]),
                             [('ps', 0), 'sm_eps'], [('ds', 1)])
                        S.op('act', lambda e: e.activation(out=R1[:, 0:n], in_=R1[:, 0:n], func=AF.Exp, scale=-0.5), [('ds', 1)], [('ds', 1)])
                        S.op('dve', lambda e: e.scalar_tensor_tensor(out=catT[:, u, q0:q0 + n], in0=T0[:, 0:n], scalar=LAM[:, 5:6], in1=R1[:, 0:n],
                                                                     op0=ALU.mult, op1=ALU.mult), [('os', 0), ('ds', 1), 'lam'], [('cat', u)])
                    LATE.append(late)

    def ln_update(t, Y, gbc, gkey, lng, lnb, lnkeys, ykeys, after=None):
        k = nxt('lnstg', 2)
        stg = STG[:, k * 1024:(k + 1) * 1024]
        sk = ('lnstg', k)
        so = 96 + k * 32
        stats = SM[:, so:so + 12]
        mv = SM[:, so + 12:so + 14]
        lnv = SM[:, so + 14:so + 15]
        rstd = SM[:, so + 15:so + 16]
        nb = SM[:, so + 16:so + 17]
        smk = ('lnsm', k)
        xk = ('x', t)
        S.op('dve', lambda e: e.tensor_tensor(out=stg, in0=Y, in1=gbc, op=ALU.mult), list(ykeys) + [gkey], [sk])
        yield
        S.op('dve', lambda e: e.scalar_tensor_tensor(out=X[:, t, :], in0=X[:, t, :], scalar=ALPHA, in1=stg,
                                                     op0=ALU.mult, op1=ALU.add), [xk, sk], [xk])
        for hh in range(2):
            S.op('dve', lambda e, hh=hh: e.bn_stats(out=stats[:, hh * 6:(hh + 1) * 6], in_=X[:, t, hh * 512:(hh + 1) * 512]),
                 [xk], [smk])
        S.op('dve', lambda e: e.bn_aggr(out=mv, in_=stats), [smk], [smk])
        yield
        S.op('act', lambda e: e.activation(out=lnv, in_=mv[:, 1:2], func=AF.Ln, bias=SM[:, 80:81]), [smk, 'sm_eps'], [smk])
        S.op('act', lambda e: e.activation(out=rstd, in_=lnv, func=AF.Exp, scale=-0.5), [smk], [smk])
        yield
        S.op('dve', lambda e: e.scalar_tensor_tensor(out=nb, in0=mv[:, 0:1], scalar=-1.0, in1=rstd, op0=ALU.mult, op1=ALU.mult),
             [smk], [smk])
        yield
        S.op('act', lambda e: e.activation(out=stg, in_=X[:, t, :], func=AF.Identity, bias=nb, scale=rstd), [xk, smk], [sk])
        yield
        S.op('dve', lambda e: e.tensor_tensor(out=stg, in0=stg, in1=lng, op=ALU.mult), [sk, lnkeys[0]], [sk])
        S.op('dve', lambda e: e.tensor_tensor(out=X[:, t, :], in0=stg, in1=lnb, op=ALU.add), [sk, lnkeys[1]], [xk])
        if after is not None:
            after()

    def phase_p4(l):
        need(f'wo{l}')
        S.dma('sp', lambda e: e.dma_start(out=WOUT[:, :, :], in_=wo_s[l].rearrange("(k p) c -> p k c", p=128)),
              [f'wo{l}'], ['wout'], 'wout')
        build_gate_bc(BCT[0], l, 2, 0, ('bct', 0))
        if l == 0:
            build_gate_bc(BCT[1], l, 2, 1, ('bct', 1))
        load_ln_bc(BCT[2], l * 4 + 0, ('bct', 2))
        load_ln_bc(BCT[3], l * 4 + 1, ('bct', 3))
        tiles = range(NT) if l == 0 else range(2, NT)
        for t in tiles:
            if t % 3 == 0:
                pump()
            p4_tile(l, t)
        pipe_flush()

    def p4_tile(l, t):
        if True:
            pb = nxt('p4pair', 2)
            w = 1 if t < 2 else 0
            for nh in range(2):
                bank = 2 * pb + nh
                for kc in range(8):
                    S.op('pe', lambda e, kc=kc, nh=nh, bank=bank: e.matmul(
                        PS(bank), lhsT=catT[:, kc, t * 128:(t + 1) * 128], rhs=WOUT[:, kc, nh * 512:(nh + 1) * 512],
                        start=(kc == 0), stop=(kc == 7)), [('cat', kc), 'wout'], [('ps', bank)])
            pipe_add(ln_update(t, PSB[pb][:, :], BCT[w], ('bct', w), BCT[2], BCT[3], [('bct', 2), ('bct', 3)],
                               [('ps', 2 * pb), ('ps', 2 * pb + 1)]))

    def phase_p5(l, last):
        need(f'fi{l}')
        need(f'fo{l}')
        S.dma('sp', lambda e: e.dma_start(out=WO[:, 0:11, :], in_=fo_s[l][0:1408, :].rearrange("(j p) c -> p j c", p=128)),
              [f'fo{l}'], ['wo'], 'wo')
        S.dma('sp', lambda e: e.dma_start(out=WO[:, 11:22, :], in_=fo_s[l][1408:2816, :].rearrange("(j p) c -> p j c", p=128)),
              [f'fo{l}'], ['wo'], 'wo')
        build_gate_bc(BCT[0], l, 5, 0, ('bct', 0))
        if l == 0:
            build_gate_bc(BCT[1], l, 5, 1, ('bct', 1))
        load_ln_bc(BCT[2], l * 4 + 2, ('bct', 2))
        load_ln_bc(BCT[3], l * 4 + 3, ('bct', 3))
        sbs = BLOCKS if l == 0 else BLOCKS[1:]
        for tiles, w in sbs:
            p5_sb(l, last, tiles, w)

    def p5_sb(l, last, tiles, w):
        if True:
            n = 128 * len(tiles)
            for kc in range(8):
                bank = 4 + nxt('p5tb', 4)
                for i, t in enumerate(tiles):
                    S.op('pe', lambda e, i=i, t=t, kc=kc, bank=bank: e.transpose(
                        out=PS(bank)[:, i * 128:(i + 1) * 128], in_=X[:, t, kc * 128:(kc + 1) * 128], identity=ident),
                        [('x', t), 'cst'], [('ps', bank)])
                S.op('act', lambda e, kc=kc, bank=bank: e.activation(
                    out=h2T[:, kc, 0:n], in_=PS(bank)[:, 0:n], func=AF.Identity, bias=modv(l, 3, kc, w), scale=modv(l, 4, kc, w)),
                    [('ps', bank), ('mod', l, 1)], [('h2T', kc)])
            for jp in range(11):
                if jp % 4 == 0:
                    pump()
                slot = nxt('fring', 2)
                sa, sg_ = FRING[slot]
                S.dma('sp', lambda e, jp=jp, sa=sa: e.dma_start(
                    out=sa[:, :, :], in_=fi_s[l][:, jp * 256:(jp + 1) * 256].rearrange("(k p) c -> p k c", p=128)),
                    [f'fi{l}'], [('fring', slot)], f'fring{slot}')
                S.dma('sp', lambda e, jp=jp, sg_=sg_: e.dma_start(
                    out=sg_[:, :, :], in_=fi_s[l][:, FFN_H + jp * 256:FFN_H + (jp + 1) * 256].rearrange("(k p) c -> p k c", p=128)),
                    [f'fi{l}'], [('fring', slot)], f'fring{slot}')
                for jj in range(2):
                    j = 2 * jp + jj
                    pr = nxt('p5ab', 2)
                    ba, bg = 2 * pr, 2 * pr + 1
                    for (bank, slab) in ((ba, sa), (bg, sg_)):
                        for kc in range(8):
                            S.op('pe', lambda e, kc=kc, bank=bank, slab=slab, jj=jj: e.matmul(
                                PS(bank)[:, 0:n], lhsT=slab[:, kc, jj * 128:(jj + 1) * 128], rhs=h2T[:, kc, 0:n],
                                start=(kc == 0), stop=(kc == 7)), [('fring', slot), ('h2T', kc)], [('ps', bank)])
                    sgs = nxt('sgs', 2)
                    S.op('act', lambda e, bg=bg, sgs=sgs: e.activation(out=SG[:, sgs, 0:n], in_=PS(bg)[:, 0:n], func=AF.Silu),
                         [('ps', bg)], [('sg', sgs)])
                    S.op('dve', lambda e, ba=ba, sgs=sgs, j=j: e.tensor_tensor(out=hidT[:, j, 0:n], in0=PS(ba)[:, 0:n],
                                                                              in1=SG[:, sgs, 0:n], op=ALU.mult),
                         [('ps', ba), ('sg', sgs)], [('hid', j)])
            for ti, t in enumerate(tiles):
                pb = 2 + nxt('p5pair', 2)
                for nh in range(2):
                    bank = 2 * pb + nh
                    for j in range(22):
                        S.op('pe', lambda e, j=j, nh=nh, bank=bank, ti=ti: e.matmul(
                            PS(bank), lhsT=hidT[:, j, ti * 128:(ti + 1) * 128], rhs=WO[:, j, nh * 512:(nh + 1) * 512],
                            start=(j == 0), stop=(j == 21)), [('hid', j), 'wo'], [('ps', bank)])
                def after(t=t):
                    if last:
                        S.dma('sp', lambda e, t=t: e.dma_start(out=out_d[(t - 2) * 128:(t - 1) * 128, :], in_=X[:, t, :]),
                              [('x', t)], [('out', t)], 'outw', sbuf=False)
                pipe_add(ln_update(t, PSB[pb][:, :], BCT[w], ('bct', w), BCT[2], BCT[3], [('bct', 2), ('bct', 3)],
                                   [('ps', 2 * pb), ('ps', 2 * pb + 1)], after=after))
            pipe_flush()

    S.op('dve', lambda e: e.memset(SM[:, 80:81], EPS), [], ['sm_eps'])

    def dump_all():
        S.barrier()
        for t in range(NT):
            S.dma('sp', lambda e, t=t: e.dma_start(out=dbg_d[t * 128:(t + 1) * 128, :], in_=X[:, t, :]),
                  [('x', t)], [('dbg', t)], 'dbgw', sbuf=False)
        S.dma('sp', lambda e: e.dma_start(out=dbg_hb, in_=HB[:, 0:18432]), [], ['dbg_hb'], 'dbgw', sbuf=False)
        S.dma('sp', lambda e: e.dma_start(out=dbg_mod, in_=MOD[:, :, :, :].rearrange("p l o w -> p (l o w)")), [], ['dbg_mod'], 'dbgw', sbuf=False)
        S.dma('sp', lambda e: e.dma_start(out=dbg_r3, in_=R3[:, :]), [], ['dbg_r3'], 'dbgw', sbuf=False)

    def run_all():
        for l in range(2):
            compute_mod(l, 0)
            if l == 0:
                early_casts()
                load_x()
            S.barrier()
            if stop == f'mod{l}':
                return
            phase_p0(l)
            if stop == f'p0{l}':
                return
            if l == 0:
                S.op('dve', lambda e: e.memset(R3[:, :], 0.0), [], ['uT'])
            phase_p1(l)
            S.barrier()
            compute_mod(l, 1)
            if stop == f'p1{l}':
                return
            if l == 0:
                phase_pool()
            else:
                phase_conv()
            S.barrier()
            if stop == f'p2{l}':
                return
            phase_att(l)
            S.barrier()
            if stop == f'p3{l}':
                return
            phase_p4(l)
            S.barrier()
            if stop == f'p4{l}':
                return
            phase_p5(l, last=(l == 1))
            S.barrier()
            if stop == f'p5{l}':
                return
    run_all()
    if debug:
        dump_all()
    S.wait_all('sp', ['outw', 'dbgw'])

    with nc.Block() as block:
        S.emit(block)
    es.close()
    return nc


def _rope_tables():
    n = np.arange(2048)
    row = (n // 64).astype(np.float32)
    col = (n % 64).astype(np.float32)
    freqs = (np.float32(10000.0) ** (-np.arange(0, 32, 2, dtype=np.float32) / np.float32(32))).astype(np.float32)
    ang = np.concatenate([row[:, None] * freqs, col[:, None] * freqs], axis=-1).astype(np.float32)
    cos, sin = np.cos(ang).astype(np.float32), np.sin(ang).astype(np.float32)
    cr, cc, sr, sc = cos[:, :16], cos[:, 16:], sin[:, :16], sin[:, 16:]
    C = np.concatenate([cr, cr, cc, cc], axis=-1)
    Sg = np.concatenate([-sr, sr, -sc, sc], axis=-1)
    C = C.reshape(16, 128, 64).transpose(1, 0, 2)
    Sg = Sg.reshape(16, 128, 64).transpose(1, 0, 2)
    return np.ascontiguousarray(C), np.ascontiguousarray(Sg)


def _consts():
    cst = np.zeros((128, NCST), np.float32)
    cst[:, C_ID:C_ID + 128] = np.eye(128, dtype=np.float32)
    C, Sg = _rope_tables()
    cst[:, C_RC:C_RC + 1024] = C.reshape(128, 1024)
    cst[:, C_RS:C_RS + 1024] = Sg.reshape(128, 1024)
    pe = np.zeros((128, 2, 2, 8), np.float32)
    for p in range(128):
        for c in range(2):
            w = POOL_W[2 * c + p // 64]
            for i in range(8):
                pe[p, c, 0, i] = 1.0 / ((i + w // 2) - max(i - w // 2, 0))
                pe[p, c, 1, i] = 1.0 / min(w, 8 - i + w // 2)
    cst[:, C_PE:C_PE + 32] = pe.reshape(128, 32)
    return cst


def _colvec(v):
    v = np.asarray(v, np.float32).reshape(-1, 128)
    return v.T


_NC_CACHE = {}


def kernel(x, c, ctx, c_ctx, ab_w_in, ab_q_gain, ab_k_gain, ab_w_pool, ab_pool_scale, ab_w_out,
           cd_w_in, cd_lambda_q1, cd_lambda_k1, cd_lambda_q2, cd_lambda_k2, cd_subln_gain,
           cd_conv_w, cd_conv_b, cd_conv_ln_g, cd_conv_ln_b, cd_w_out,
           ada_w, ada_b, ln1_g, ln1_b, ln2_g, ln2_b, ffn_w_in, ffn_w_out, _debug=False, _stop=None):
    f = lambda a: np.ascontiguousarray(np.asarray(a, dtype=np.float32))
    x, c, ctx, c_ctx = f(x), f(c), f(ctx), f(c_ctx)
    if (_debug, _stop) not in _NC_CACHE:
        _NC_CACHE[(_debug, _stop)] = build_program(debug=_debug, stop=_stop)
    nc = _NC_CACHE[(_debug, _stop)]
    cst = _consts()
    lnbc = np.stack([f(ln1_g)[0], f(ln1_b)[0], f(ln2_g)[0], f(ln2_b)[0],
                     f(ln1_g)[1], f(ln1_b)[1], f(ln2_g)[1], f(ln2_b)[1]], axis=0)
    gains = np.stack([f(ab_q_gain)[0], f(ab_k_gain)[0]], axis=0)
    shared = {
        "consts": cst, "lnbc": np.ascontiguousarray(lnbc), "gains": np.ascontiguousarray(gains),
        "ab_w_in": f(ab_w_in)[0], "cd_w_in": f(cd_w_in)[0], "ab_w_out": f(ab_w_out)[0], "cd_w_out": f(cd_w_out)[0],
        "ab_w_pool": f(ab_w_pool)[0].reshape(256, 64), "ada_w": f(ada_w).reshape(2 * D, 6 * D),
        "ffn_w_in": f(ffn_w_in).reshape(2 * D, 2 * FFN_H), "ffn_w_out": f(ffn_w_out).reshape(2 * FFN_H, D),
    }
    in_maps = []
    for b in range(NCORES):
        vecs = np.zeros((128, NV), np.float32)
        vecs[:, 0:8] = _colvec(c[b])
        vecs[:, 8:16] = _colvec(c_ctx)
        vecs[:, 16:64] = _colvec(f(ada_b)[0])
        vecs[:, 64:112] = _colvec(f(ada_b)[1])
        vecs[:, 112:114] = _colvec(f(ab_pool_scale)[0])
        vecs[:, 114:116] = _colvec(f(cd_conv_b)[0])
        vecs[:, 116:118] = _colvec(f(cd_conv_ln_g)[0])
        vecs[:, 118:120] = _colvec(f(cd_conv_ln_b)[0])
        vecs[:, 120:121] = _colvec(f(cd_subln_gain)[0])
        vecs[0:64, 121] = f(cd_lambda_q1)[0]
        vecs[0:64, 122] = f(cd_lambda_k1)[0]
        vecs[0:64, 123] = f(cd_lambda_q2)[0]
        vecs[0:64, 124] = f(cd_lambda_k2)[0]
        cw = f(cd_conv_w)[0]
        for j in range(31):
            vecs[:, 125 + 2 * j:127 + 2 * j] = _colvec(cw[j])
        m = dict(shared)
        m["x"] = x[b]
        m["ctx"] = ctx[b]
        m["vecs"] = vecs
        in_maps.append(m)
    res = run_bass_kernel_spmd(nc, in_maps, core_ids=list(range(NCORES)))
    out = np.stack([np.asarray(r["out"], dtype=np.float32) for r in res.results], axis=0)
    if _debug:
        return out, res.results
    return out
```
